# Optimizing a Trainium2 kernel written in Bass

```python
import math
import jax, jax.numpy as jnp
from jax import lax
import numpy as np

D_MODEL = 1024
BATCH = 8
SEQ = 8192
DEPTH = 2
DEC_BATCH = 8
DEC_SEQ = 2048
PAST_LEN = 128

HEAD_DIM = 64
A_HEADS = 4
A_VDIM = 2 * HEAD_DIM
B_HEADS = 4
C_HEADS = 4
MEM_TOKENS = 256
GRID_W = 64
NA_ROWS = 8
NA_COLS = 16
T5_BUCKETS = 32
T5_MAX_DIST = 128
Q_BLOCK = 128
D_FF = 2816
CONV_WIDTH = 3
EPS = 1e-6

A_QK = A_HEADS * HEAD_DIM
A_V = A_HEADS * A_VDIM
B_W = B_HEADS * HEAD_DIM
C_W = C_HEADS * HEAD_DIM
MIX_WIDTH = A_V + B_W + C_W
IN_SPLITS = (A_QK, A_QK, A_QK, A_QK, A_V, B_W, B_W, B_W, C_W)
IN_WIDTH = 4 * A_QK + A_V + 3 * B_W + C_W

kernel_name = 'hymba_diff_natten_mem_encoder'


def rms_norm(x, g):
    xf = x.astype(jnp.float32)
    y = xf * lax.rsqrt(jnp.mean(xf * xf, axis=-1, keepdims=True) + EPS)
    return (y * g.astype(jnp.float32)).astype(x.dtype)


def t5_bucket(rp):
    half = T5_BUCKETS // 2
    max_exact = half // 2
    ret = jnp.where(rp > 0, half, 0)
    n = jnp.abs(rp)
    nf = jnp.maximum(n, 1).astype(jnp.float32)
    large = max_exact + (jnp.log(nf / max_exact) / math.log(T5_MAX_DIST / max_exact)
                         * (half - max_exact)).astype(jnp.int32)
    large = jnp.minimum(large, half - 1)
    return ret + jnp.where(n < max_exact, n, large)


def diff_attention(q1, q2, k1, k2, v, rel_bias, lam, lam_init, subln_g):
    B, S, H, _ = q1.shape
    nblk = S // Q_BLOCK
    scale = HEAD_DIM ** -0.5
    keys = jnp.arange(S)

    def to_blocks(t):
        return t.reshape(B, nblk, Q_BLOCK, H, t.shape[-1]).swapaxes(0, 1)

    def block(args):
        q1b, q2b, start = args
        rp = keys[None, :] - (start + jnp.arange(Q_BLOCK))[:, None]
        bias = jnp.transpose(rel_bias[t5_bucket(rp)], (2, 0, 1)).astype(jnp.float32)
        s1 = jnp.einsum('bqhd,bkhd->bhqk', q1b, k1).astype(jnp.float32) * scale + bias
        s2 = jnp.einsum('bqhd,bkhd->bhqk', q2b, k2).astype(jnp.float32) * scale + bias
        a = jax.nn.softmax(s1, axis=-1) - lam * jax.nn.softmax(s2, axis=-1)
        return jnp.einsum('bhqk,bkhe->bqhe', a.astype(v.dtype), v)

    o = lax.map(block, (to_blocks(q1), to_blocks(q2), jnp.arange(nblk) * Q_BLOCK))
    o = o.swapaxes(0, 1).reshape(B, S, H, A_VDIM)
    return rms_norm(o, subln_g) * (1.0 - lam_init)


def neighbourhood_attention(q, k, v, bias_tab):
    B, S, H, d = q.shape
    R = S // GRID_W
    KH = min(NA_ROWS, R)
    scale = d ** -0.5
    qg = q.reshape(B, R, GRID_W, H, d)
    kg = k.reshape(B, R, GRID_W, H, d)
    vg = v.reshape(B, R, GRID_W, H, d)
    cols = jnp.arange(GRID_W)
    cs = jnp.clip(cols - NA_COLS // 2, 0, GRID_W - NA_COLS)
    colidx = cs[:, None] + jnp.arange(NA_COLS)[None, :]
    dcol = colidx - cols[:, None] + (NA_COLS - 1)

    def row(r):
        rs = jnp.clip(r - KH // 2, 0, R - KH)
        kr = lax.dynamic_slice_in_dim(kg, rs, KH, axis=1)
        vr = lax.dynamic_slice_in_dim(vg, rs, KH, axis=1)
        kw = kr[:, :, colidx]
        vw = vr[:, :, colidx]
        qr = lax.dynamic_index_in_dim(qg, r, axis=1, keepdims=False)
        s = jnp.einsum('bchd,bicjhd->bhcij', qr, kw).astype(jnp.float32) * scale
        drow = rs + jnp.arange(KH) - r + (NA_ROWS - 1)
        bias = bias_tab[:, drow[:, None, None], dcol[None, :, :]]
        s = s + jnp.transpose(bias, (0, 2, 1, 3))[None].astype(jnp.float32)
        p = jax.nn.softmax(s.reshape(B, H, GRID_W, KH * NA_COLS), axis=-1)
        p = p.reshape(B, H, GRID_W, KH, NA_COLS).astype(v.dtype)
        return jnp.einsum('bhcij,bicjhe->bche', p, vw)

    o = lax.map(row, jnp.arange(R))
    return o.swapaxes(0, 1).reshape(B, S, H, d)


def memory_attention(q, mem, mem_g, w_mem_kv, kn):
    B, M, _ = mem.shape
    kv = rms_norm(mem, mem_g) @ w_mem_kv
    k, v = jnp.split(kv, 2, axis=-1)
    k = rms_norm(k.reshape(B, M, C_HEADS, HEAD_DIM), kn)
    v = v.reshape(B, M, C_HEADS, HEAD_DIM)
    s = jnp.einsum('bqhd,bmhd->bhqm', q, k).astype(jnp.float32) * (HEAD_DIM ** -0.5)
    p = jax.nn.softmax(s, axis=-1).astype(v.dtype)
    return jnp.einsum('bhqm,bmhd->bqhd', p, v)


def dwconv_centred(h, w, b):
    S = h.shape[1]
    hp = jnp.pad(h, ((0, 0), (CONV_WIDTH // 2, CONV_WIDTH // 2), (0, 0)))
    out = b
    for i in range(CONV_WIDTH):
        out = out + w[i] * hp[:, i:i + S]
    return out


def setup_inputs(seed: int = 0) -> dict:
    key = jax.random.key(seed)
    ks = jax.random.split(key, 32)
    nrm = jax.random.normal
    f32 = jnp.float32

    def gain(k, shape):
        return 1.0 + 0.02 * nrm(k, shape, f32)

    return {
        'x_prompt': nrm(ks[0], (BATCH, SEQ, D_MODEL), f32),
        'x_sample': nrm(ks[1], (DEC_BATCH, DEC_SEQ, D_MODEL), f32),
        'mem_prompt': nrm(ks[2], (BATCH, MEM_TOKENS, D_MODEL), f32),
        'mem_sample': nrm(ks[3], (DEC_BATCH, MEM_TOKENS, D_MODEL), f32),
        'norm1_g': gain(ks[4], (DEPTH, D_MODEL)),
        'w_in': nrm(ks[5], (DEPTH, D_MODEL, IN_WIDTH), f32) * D_MODEL ** -0.5,
        'qn_a': gain(ks[6], (DEPTH, HEAD_DIM)),
        'kn_a': gain(ks[7], (DEPTH, HEAD_DIM)),
        'lam_q1': 0.1 * nrm(ks[8], (DEPTH, HEAD_DIM), f32),
        'lam_k1': 0.1 * nrm(ks[9], (DEPTH, HEAD_DIM), f32),
        'lam_q2': 0.1 * nrm(ks[10], (DEPTH, HEAD_DIM), f32),
        'lam_k2': 0.1 * nrm(ks[11], (DEPTH, HEAD_DIM), f32),
        'subln_g': gain(ks[12], (DEPTH, A_VDIM)),
        'rel_bias': 0.5 * nrm(ks[13], (T5_BUCKETS, A_HEADS), f32),
        'qn_b': gain(ks[14], (DEPTH, HEAD_DIM)),
        'kn_b': gain(ks[15], (DEPTH, HEAD_DIM)),
        'na_bias': 0.5 * nrm(ks[16], (DEPTH, B_HEADS, 2 * NA_ROWS - 1, 2 * NA_COLS - 1), f32),
        'mem_g': gain(ks[17], (DEPTH, D_MODEL)),
        'w_mem_kv': nrm(ks[18], (DEPTH, D_MODEL, 2 * C_W), f32) * D_MODEL ** -0.5,
        'qn_c': gain(ks[19], (DEPTH, HEAD_DIM)),
        'kn_c': gain(ks[20], (DEPTH, HEAD_DIM)),
        'w_out': nrm(ks[21], (DEPTH, MIX_WIDTH, D_MODEL), f32) * MIX_WIDTH ** -0.5,
        'norm2_g': gain(ks[22], (DEPTH, D_MODEL)),
        'w_up': nrm(ks[23], (DEPTH, D_MODEL, 2 * D_FF), f32) * D_MODEL ** -0.5,
        'conv_w': nrm(ks[24], (DEPTH, CONV_WIDTH, 2 * D_FF), f32) * CONV_WIDTH ** -0.5,
        'conv_b': 0.01 * nrm(ks[25], (DEPTH, 2 * D_FF), f32),
        'w_down': nrm(ks[26], (DEPTH, D_FF, D_MODEL), f32) * D_FF ** -0.5,
    }


def reference(x_prompt, x_sample, mem_prompt, mem_sample, norm1_g, w_in, qn_a, kn_a,
              lam_q1, lam_k1, lam_q2, lam_k2, subln_g, rel_bias, qn_b, kn_b, na_bias,
              mem_g, w_mem_kv, qn_c, kn_c, w_out, norm2_g, w_up, conv_w, conv_b, w_down):
    split_at = [int(i) for i in np.cumsum(IN_SPLITS)[:-1]]

    def layer(x, mem, l):
        B, S, _ = x.shape
        h = rms_norm(x, norm1_g[l])
        z = h @ w_in[l]
        q1, q2, k1, k2, va, qb, kb, vb, qc = jnp.split(z, split_at, axis=-1)
        heads = lambda t, n: t.reshape(B, S, n, t.shape[-1] // n)
        lam_init = 0.8 - 0.6 * math.exp(-0.3 * l)
        lam = (jnp.exp(jnp.sum(lam_q1[l].astype(jnp.float32) * lam_k1[l].astype(jnp.float32)))
               - jnp.exp(jnp.sum(lam_q2[l].astype(jnp.float32) * lam_k2[l].astype(jnp.float32)))
               + lam_init)
        o_a = diff_attention(rms_norm(heads(q1, A_HEADS), qn_a[l]), rms_norm(heads(q2, A_HEADS), qn_a[l]),
                             rms_norm(heads(k1, A_HEADS), kn_a[l]), rms_norm(heads(k2, A_HEADS), kn_a[l]),
                             heads(va, A_HEADS), rel_bias, lam, lam_init, subln_g[l])
        o_b = neighbourhood_attention(rms_norm(heads(qb, B_HEADS), qn_b[l]),
                                      rms_norm(heads(kb, B_HEADS), kn_b[l]),
                                      heads(vb, B_HEADS), na_bias[l])
        o_c = memory_attention(rms_norm(heads(qc, C_HEADS), qn_c[l]), mem, mem_g[l], w_mem_kv[l], kn_c[l])
        mix = jnp.concatenate([o_a.reshape(B, S, A_V), o_b.reshape(B, S, B_W),
                               o_c.reshape(B, S, C_W)], axis=-1)
        x = x + mix @ w_out[l]
        u = dwconv_centred(rms_norm(x, norm2_g[l]) @ w_up[l], conv_w[l], conv_b[l])
        val, gate = jnp.split(u, 2, axis=-1)
        return x + (jax.nn.silu(gate) * val) @ w_down[l]

    y_prompt = x_prompt
    y_sample = x_sample
    for l in range(DEPTH):
        y_prompt = layer(y_prompt, mem_prompt, l)
    for l in range(DEPTH):
        y_sample = layer(y_sample, mem_sample, l)
    return (y_prompt, y_sample)
```

```python
import math
from contextlib import ExitStack

import numpy as np
import concourse.bass as bass
import concourse.mybir as mybir
from concourse.bass_utils import run_bass_kernel_spmd

F32 = mybir.dt.float32
BF16 = mybir.dt.bfloat16
AF = mybir.ActivationFunctionType
ALU = mybir.AluOpType
AX = mybir.AxisListType

D_MODEL = 1024
DEPTH = 2
NCH = 8
HEAD_DIM = 64
IN_WIDTH = 2560
NQK = 14
VW = 768
D_FF = 2816
NFF = 22
MEM = 256
EPS = 1e-6
NEG = -30000.0
GRID_W = 64
FT = 510

ENGS = ("pe", "act", "dve", "pool", "sp")


class TB:
    __slots__ = ("name", "w", "r", "sem")

    def __init__(self, name):
        self.name = name
        self.w = {}
        self.r = {}
        self.sem = None


class FW:
    def __init__(self, nc, es, n_dma_sems=88):
        self.nc = nc
        self.ops = {e: [] for e in ENGS}
        self.seq = {e: 0 for e in ENGS}
        self.seen = {e: {} for e in ENGS}
        self.need = {e: set() for e in ENGS}
        self.esem = {e: es.enter_context(nc.semaphore("s_" + e)) for e in ENGS}
        self.dsems = [es.enter_context(nc.semaphore("d%d" % i)) for i in range(n_dma_sems)]
        self.dfree = list(range(n_dma_sems))
        self.dcount = [0] * n_dma_sems

    def tb(self, name, dma=False):
        t = TB(name)
        if dma:
            t.sem = self.dfree.pop()
        return t

    def release(self, tbs):
        for t in tbs:
            if t.sem is not None:
                self.dfree.append(t.sem)
                t.sem = None

    def op(self, eng, fn, reads=(), writes=(), accw=(), dsem=None):
        waits = {}

        def merge(d):
            for k, v in d.items():
                if waits.get(k, 0) < v:
                    waits[k] = v

        for t in reads:
            merge(t.w)
        for t in writes:
            merge(t.w)
            merge(t.r)
        for t in accw:
            merge(t.r)
        if dsem is not None:
            key = ("d", dsem)
            self.dcount[dsem] += 16
            val = self.dcount[dsem]
        else:
            key = ("e", eng)
            self.seq[eng] += 1
            val = self.seq[eng]
        mywaits = []
        seen = self.seen[eng]
        for k, v in waits.items():
            if eng == "pe" and k == ("e", "pe"):
                continue
            if seen.get(k, 0) >= v:
                continue
            seen[k] = v
            mywaits.append((k, v))
            if k[0] == "e":
                self.need[k[1]].add(v)
        self.ops[eng].append((fn, mywaits, key, val))
        for t in reads:
            if t.r.get(key, 0) < val:
                t.r[key] = val
        for t in writes:
            t.w = {key: val}
            t.r = {}
        for t in accw:
            if t.w.get(key, 0) < val:
                t.w[key] = val

    def dma(self, q, out, in_, semtb, reads=(), writes=(), accw=(), **kw):
        self.op(q, lambda e: e.dma_start(out=out, in_=in_, **kw), reads=reads, writes=writes,
                accw=accw, dsem=semtb.sem)

    def dma_rows(self, q, sb_fn, dr_fn, p0, n, semtb, to_sbuf, first_is_write=False, **kw):
        pieces = []
        n16 = (n // 16) * 16
        if n16 > 0:
            pieces.append((0, n16))
        if n - n16 > 0:
            pieces.append((n16, n - n16))
        for idx, (o, c) in enumerate(pieces):
            sb_ap, dr_ap = sb_fn(p0 + o, c), dr_fn(o, c)
            k = dict(kw)
            if to_sbuf:
                if first_is_write and idx == 0:
                    k["writes"] = list(k.get("writes", [])) + [semtb]
                else:
                    k["accw"] = list(k.get("accw", [])) + [semtb]
                self.dma(q, sb_ap, dr_ap, semtb, **k)
            else:
                self.dma(q, dr_ap, sb_ap, semtb, **k)

    def barrier(self):
        allw = {}
        for e in ENGS:
            if self.seq[e] > 0:
                allw[("e", e)] = self.seq[e]
        for i, c in enumerate(self.dcount):
            if c > 0:
                allw[("d", i)] = c
        t = TB("barrier")
        t.w = allw
        for e in ENGS:
            self.op(e, lambda eng: eng.nop(), reads=[t])

    def emit(self):
        sigidx = {}
        for e in ENGS:
            m = {}
            cnt = 0
            for i in sorted(self.need[e]):
                cnt += 1
                m[i] = cnt
            sigidx[e] = m
        esem, dsems, need = self.esem, self.dsems, self.need
        dcount = self.dcount

        def body(ename, eng, final=False):
            for (fn, waits, key, val) in self.ops[ename]:
                for (k, v) in waits:
                    if k[0] == "e":
                        eng.wait_ge(esem[k[1]], sigidx[k[1]][v])
                    else:
                        eng.wait_ge(dsems[k[1]], v)
                ins = fn(eng)
                if key[0] == "d":
                    ins.then_inc(dsems[key[1]], 16)
                elif val in need[ename]:
                    ins.then_inc(esem[ename], 1)
            if final:
                for i, c in enumerate(dcount):
                    if c > 0:
                        eng.wait_ge(dsems[i], c)
                for e in ENGS:
                    if e != ename and sigidx[e]:
                        eng.wait_ge(esem[e], len(sigidx[e]))

        with self.nc.Block() as block:
            block.tensor(lambda eng: body("pe", eng))
            block.scalar(lambda eng: body("act", eng))
            block.vector(lambda eng: body("dve", eng))
            block.gpsimd(lambda eng: body("pool", eng))
            block.sync(lambda eng: body("sp", eng, final=True))


def _t5_bucket(rp):
    half, max_exact = 16, 8
    ret = np.where(rp > 0, half, 0)
    n = np.abs(rp)
    nf = np.maximum(n, 1).astype(np.float32)
    large = max_exact + (np.log(nf / np.float32(max_exact)) / np.float32(math.log(128 / max_exact))
                         * (half - max_exact)).astype(np.int32)
    large = np.minimum(large, half - 1)
    return ret + np.where(n < max_exact, n, large)


def _host_consts():
    c = {}
    i = np.arange(1280)
    bk = _t5_bucket(i - 640)
    G = np.zeros((32, 1280), np.float32)
    G[bk, i] = 1.0
    c["c_g5"] = G
    S = np.zeros((32, 64, 128), np.float32)
    for cq in range(64):
        cs = min(max(cq - 8, 0), 48)
        for p in range(128):
            ck = p % 64
            if cs <= ck < cs + 16:
                S[ck - cq + 15, cq, p] = 1.0
            else:
                S[31, cq, p] = NEG
    c["c_sna"] = S
    RM = np.zeros((128, 2, 8, 8), np.float32)
    for p in range(128):
        rl_k = p // 64
        for j in range(8):
            for rl in range(8):
                r = rl
                rs = max(r - 4, 0)
                rk = -4 + 2 * j + rl_k
                ok = (rs <= rk < rs + 8)
                RM[p, 0, j, rl] = 0.0 if ok else NEG
                rs = min(rl - 4, 0)
                rk = -4 + 2 * j + rl_k
                ok = (rs <= rk < rs + 8) and rk < 8
                RM[p, 1, j, rl] = 0.0 if ok else NEG
    c["c_rm"] = RM.reshape(128, 128)
    c["c_ident"] = np.eye(128, dtype=np.float32)
    bo = np.zeros((128, 128), np.float32)
    bo[:64, :64] = 1.0
    bo[64:, 64:] = 1.0
    c["c_bones"] = bo
    return c


def _win_perm():
    cols = []
    for h in range(4):
        cols += list(range(h * 64, h * 64 + 64)) + list(range(256 + h * 64, 256 + h * 64 + 64))
    for h in range(4):
        cols += list(range(512 + h * 64, 512 + h * 64 + 64)) + list(range(768 + h * 64, 768 + h * 64 + 64))
    cols += list(range(1536, 1792))
    cols += list(range(1792, 2048))
    cols += list(range(2304, 2560))
    cols += list(range(1024, 1536))
    cols += list(range(2048, 2304))
    return np.array(cols, np.int64)


class Prog:
    def __init__(self, groups, depth, debug=(), stop_after=None):
        self.groups = groups
        self.depth = depth
        self.debug = set(debug)
        self.stop_after = stop_after
        self.nc = bass.Bass("TRN2", target_bir_lowering=False)
        self.es = ExitStack()
        self.fw = FW(self.nc, self.es)
        self.dram = {}
        self.dtb = {}

    def din(self, name, shape, dt=F32):
        self.dram[name] = self.nc.dram_tensor(name, list(shape), dt, kind="ExternalInput").ap()
        self.dtb[name] = TB(name)
        return self.dram[name]

    def dout(self, name, shape, dt=F32):
        self.dram[name] = self.nc.dram_tensor(name, list(shape), dt, kind="ExternalOutput").ap()
        self.dtb[name] = TB(name)
        return self.dram[name]

    def dscr(self, name, shape, dt):
        kind = "ExternalOutput" if name in self.debug else "Internal"
        self.dram[name] = self.nc.dram_tensor(name, list(shape), dt, kind=kind).ap()
        self.dtb[name] = TB(name)
        return self.dram[name]

    def sb(self, es, name, shape, dt, dma=False):
        self._uid = getattr(self, "_uid", 0) + 1
        name = "sb%d_%s" % (self._uid, name)
        t = es.enter_context(self.nc.sbuf_tensor(name, list(shape), dt))
        tb = self.fw.tb(name, dma=dma)
        self._phase_tbs.append(tb)
        return t, tb

    def begin_phase(self):
        self._phase_tbs = []
        return ExitStack()

    def end_phase(self, es):
        self.fw.barrier()
        self.fw.release(self._phase_tbs)
        es.close()

    def build(self):
        nc, fw = self.nc, self.fw
        for (g, S) in self.groups:
            self.din("x_" + g, [S, D_MODEL])
            self.din("mem_" + g, [MEM, D_MODEL])
            self.dout("y_" + g, [S, D_MODEL])
        L = self.depth
        self.din("norm1_g", [DEPTH, D_MODEL]); self.din("w_in", [DEPTH, D_MODEL, IN_WIDTH])
        for n in ("qn_a", "kn_a", "lam_q1", "lam_k1", "lam_q2", "lam_k2", "qn_b", "kn_b", "qn_c", "kn_c"):
            self.din(n, [DEPTH, 64])
        self.din("subln_g", [DEPTH, 128]); self.din("rel_bias", [32, 4])
        self.din("na_bias", [DEPTH, 4, 15, 31]); self.din("mem_g", [DEPTH, D_MODEL])
        self.din("w_mem_kv", [DEPTH, D_MODEL, 512]); self.din("w_out", [DEPTH, D_MODEL, D_MODEL])
        self.din("norm2_g", [DEPTH, D_MODEL]); self.din("w_up", [DEPTH, D_MODEL, 2 * D_FF])
        self.din("conv_w", [DEPTH, 3, 2 * D_FF]); self.din("conv_b", [DEPTH, 2 * D_FF])
        self.din("w_down", [DEPTH, D_FF, D_MODEL])
        self.din("c_g5", [32, 1280]); self.din("c_sna", [32, 64, 128]); self.din("c_rm", [128, 128])
        self.din("c_ident", [128, 128]); self.din("c_bones", [128, 128])
        for (g, S) in self.groups:
            self.dscr("qkA_" + g, [8, 128, S], BF16)
            self.dscr("vA_" + g, [S, 512], BF16)
            self.dscr("qkB_" + g, [4, 128, S], BF16)
            self.dscr("vB_" + g, [S, 256], BF16)
            self.dscr("qC_" + g, [2, 128, S], BF16)
            self.dscr("kmT_" + g, [2, 128, MEM], BF16)
            self.dscr("vm_" + g, [MEM, 256], BF16)
            self.dscr("mixT_" + g, [8, 128, S], BF16)
            self.dscr("xmid_" + g, [S, D_MODEL], F32)
            self.dscr("x1_" + g, [S, D_MODEL], F32)

        ges = self.es
        self._phase_tbs = []
        self.ps = []
        self.pstb = []
        self.ps2 = []
        for i in range(4):
            self.ps2.append(ges.enter_context(nc.psum_tensor("psd%d" % i, [128, 1024], F32)))
        for i in range(8):
            self.ps.append(self.ps2[i // 2][:, (i % 2) * 512:(i % 2 + 1) * 512])
            self.pstb.append(fw.tb("ps%d" % i))
        self.ident_f, self.ident_f_tb = self.sb(ges, "ident_f", [128, 128], F32, dma=True)
        self.ident_b, self.ident_b_tb = self.sb(ges, "ident_b", [128, 128], BF16)
        self.ones_b, self.ones_b_tb = self.sb(ges, "ones_b", [128, 128], BF16)
        self.bones_f, self.bones_f_tb = self.sb(ges, "bones_f", [128, 128], F32, dma=True)
        self.bones_b, self.bones_b_tb = self.sb(ges, "bones_b", [128, 128], BF16)
        fw.dma("sp", self.ident_f[:], self.dram["c_ident"][:, :], self.ident_f_tb,
               reads=[self.dtb["c_ident"]], writes=[self.ident_f_tb])
        fw.dma("sp", self.bones_f[:], self.dram["c_bones"][:, :], self.bones_f_tb,
               reads=[self.dtb["c_bones"]], writes=[self.bones_f_tb])
        fw.op("dve", lambda e: e.tensor_copy(self.ident_b[:], self.ident_f[:]),
              reads=[self.ident_f_tb], writes=[self.ident_b_tb])
        fw.op("dve", lambda e: e.tensor_copy(self.bones_b[:], self.bones_f[:]),
              reads=[self.bones_f_tb], writes=[self.bones_b_tb])
        fw.op("dve", lambda e: e.memset(self.ones_b[:], 1.0), writes=[self.ones_b_tb])
        self.eps_t, self.eps_tb = self.sb(ges, "eps", [128, 1], F32)
        fw.op("dve", lambda e: e.memset(self.eps_t[:], EPS), writes=[self.eps_tb])
        self.mhalf, self.mhalf_tb = self.sb(ges, "mhalf", [128, 1], F32)
        fw.op("dve", lambda e: e.memset(self.mhalf[:], -0.5), writes=[self.mhalf_tb])

        self.setup_t5()
        for l in range(L):
            last = (l == L - 1)
            self.phase_proj(l)
            if self.stop_after == ("proj", l):
                break
            self.phase_attn_a(l)
            if self.stop_after == ("attn_a", l):
                break
            self.phase_attn_bc(l)
            if self.stop_after == ("attn_bc", l):
                break
            self.phase_outproj(l)
            if self.stop_after == ("outproj", l):
                break
            self.phase_ffn(l, 0, last)
            self.phase_ffn(l, 1, last)
        fw.emit()
        self.es.close()
        return nc

    def load_gain_cols(self, es, name, src_ap, src_tb, nchunk):
        fw = self.fw
        t, tb = self.sb(es, name, [128, nchunk], F32, dma=True)
        fw.dma("sp", t[:], src_ap.rearrange("(k p) -> p k", p=128), tb, reads=[src_tb], writes=[tb],
               allow_slow_non_contiguous=True)
        return t, tb

    def prep_weight(self, es, name, w_ap, w_tb, nk, ncols, gain, gain_tb, stage, stage_tbs, col0=0,
                    engines=("pool", "dve")):
        fw = self.fw
        wt, wtb = self.sb(es, name, [128, nk, ncols], BF16)
        CW = stage[0].shape[1]
        i = 0
        for k in range(nk):
            for c0 in range(0, ncols, CW):
                cw = min(CW, ncols - c0)
                st, stb = stage[i % len(stage)], stage_tbs[i % len(stage)]
                fw.dma("sp", st[:, 0:cw], w_ap[k * 128:(k + 1) * 128, col0 + c0:col0 + c0 + cw], stb,
                       reads=[w_tb], writes=[stb])
                eng = engines[i % len(engines)]
                if gain is None:
                    fw.op(eng, (lambda e, st=st, c0=c0, cw=cw, k=k: e.tensor_copy(wt[:, k, c0:c0 + cw], st[:, 0:cw])),
                          reads=[stb], accw=[wtb])
                else:
                    fw.op(eng, (lambda e, st=st, c0=c0, cw=cw, k=k: e.tensor_scalar(
                        wt[:, k, c0:c0 + cw], st[:, 0:cw], gain[:, k:k + 1], 1.0, ALU.mult, ALU.mult)),
                        reads=[stb, gain_tb], accw=[wtb])
                i += 1
        return wt, wtb

    def token_rstd(self, x_ap, xtb, junk, junktb, ss, sstb, tmp, tmptb, rstd, rstdtb):
        fw = self.fw
        fw.op("dve", lambda e: e.scalar_tensor_tensor(junk[:], x_ap, 1.0, x_ap, ALU.mult, ALU.mult,
                                                      accum_out=ss[:, 0:1]),
              reads=[xtb], writes=[junktb, sstb])
        fw.op("pool", lambda e: e.tensor_scalar(tmp[:, 0:1], ss[:, 0:1], 1.0 / D_MODEL, EPS, ALU.mult, ALU.add),
              reads=[sstb], writes=[tmptb])
        fw.op("pool", lambda e: e.tensor_tensor(rstd[:, 0:1], tmp[:, 0:1], self.mhalf[:, 0:1], ALU.pow),
              reads=[tmptb, self.mhalf_tb], writes=[rstdtb])

    def rstd_from_ss(self, ss_ap, out_ap, tmp_ap, inv_n, tbs_r, tbs_tmp, tbs_out):
        fw = self.fw
        fw.op("act", lambda e: e.activation(tmp_ap, ss_ap, AF.Ln, bias=self.eps_t[:, 0:1], scale=inv_n),
              reads=tbs_r + [self.eps_tb], writes=tbs_tmp)
        fw.op("act", lambda e: e.activation(out_ap, tmp_ap, AF.Exp, scale=-0.5),
              reads=tbs_tmp, writes=tbs_out)

    def phase_proj(self, l):
        nc, fw = self.nc, self.fw
        es = self.begin_phase()
        D = self.dram
        src_pref = "x_" if l == 0 else "x1_"
        stage = []
        stage_tbs = []
        for i in range(3):
            t, tb = self.sb(es, "wst%d" % i, [128, 2560], F32, dma=True)
            stage.append(t); stage_tbs.append(tb)
        g1, g1tb = self.load_gain_cols(es, "g1", D["norm1_g"][l, :], self.dtb["norm1_g"], 8)
        gm, gmtb = self.load_gain_cols(es, "gm", D["mem_g"][l, :], self.dtb["mem_g"], 8)
        win, wintb = self.prep_weight(es, "win", D["w_in"][l], self.dtb["w_in"], 8, IN_WIDTH, g1, g1tb,
                                      stage, stage_tbs)
        wmem, wmemtb = self.prep_weight(es, "wmem", D["w_mem_kv"][l], self.dtb["w_mem_kv"], 8, 512, gm, gmtb,
                                        stage, stage_tbs)
        gq, gqtb = self.sb(es, "gq", [128, 16], F32, dma=True)
        plan = [("qn_a", range(0, 4)), ("kn_a", range(4, 8)), ("qn_b", range(8, 10)), ("kn_b", range(10, 12)),
                ("qn_c", range(12, 14)), ("kn_c", range(14, 16))]
        for (nm, cols) in plan:
            for half in range(2):
                src = D[nm][l, :].rearrange("(p o) -> p o", o=1)
                fw.dma("sp", gq[half * 64:(half + 1) * 64, cols[0]:cols[0] + 1], src, gqtb,
                       reads=[self.dtb[nm]], accw=[gqtb], allow_slow_non_contiguous=True)
        gq2, gq2tb = self.sb(es, "gq2", [128, 16], F32)

        def mk_gq2(e):
            last = None
            for (nm, cols) in plan:
                sc = 0.125 if nm.startswith("qn") else 1.0
                for c in cols:
                    last = e.tensor_scalar(gq2[:, c:c + 1], gq[:, cols[0]:cols[0] + 1], sc, None, ALU.mult)
            return last
        fw.op("dve", mk_gq2, reads=[gqtb], writes=[gq2tb])

        xt = []; xttb = []
        for i in range(2):
            t, tb = self.sb(es, "xt%d" % i, [128, 4, 1024], F32, dma=True)
            xt.append(t); xttb.append(tb)
        junk, junktb = self.sb(es, "junk", [128, 1024], BF16)
        xn = []; xntb = []
        for i in range(2):
            t, tb = self.sb(es, "xn%d" % i, [128, 1024], BF16)
            xn.append(t); xntb.append(tb)
        xnT = []; xnTtb = []
        for i in range(2):
            t, tb = self.sb(es, "xnT%d" % i, [128, 8, 512], BF16)
            xnT.append(t); xnTtb.append(tb)
        st = {}
        for nm, shp, dt, n in (("ss", [128, 1], F32, 4), ("lnv", [128, 1], F32, 4), ("rstd", [128, 1], F32, 4),
                               ("sq", [128, 512], BF16, 3), ("lnq", [128, 512], F32, 3), ("rsq", [128, 512], F32, 3),
                               ("zo", [128, 512], BF16, 4), ("vo", [128, 768], BF16, 3)):
            st[nm] = [self.sb(es, "%s%d" % (nm, i), shp, dt, dma=(nm in ("zo", "vo"))) for i in range(n)]
        cnt = {k: 0 for k in st}

        def nxt(nm):
            i = cnt[nm] % len(st[nm])
            cnt[nm] += 1
            return st[nm][i]

        ps, pstb = self.ps, self.pstb
        psrot = {"tr": [0, 1], "z": [2, 3, 4], "hs": [5, 6], "v": [7]}
        pcnt = {k: 0 for k in psrot}

        def pnext(role):
            i = psrot[role][pcnt[role] % len(psrot[role])]
            pcnt[role] += 1
            return i

        tiles = []
        for (g, S) in self.groups:
            tiles.append(dict(kind="mem", g=g, S=MEM, src=D["mem_" + g], srctb=self.dtb["mem_" + g], t0=0, TT=256,
                              W=wmem, Wtb=wmemtb, chunks=[(0, 14, ("kmT_" + g, 0)), (1, 15, ("kmT_" + g, 1))],
                              vparts=[(256, 256, "vm_" + g)]))
        for (g, S) in self.groups:
            chunks = [(c, c, ("qkA_" + g, c) if c < 8 else (("qkB_" + g, c - 8) if c < 12 else ("qC_" + g, c - 12)))
                      for c in range(NQK)]
            for t0 in range(0, S, 512):
                tiles.append(dict(kind="x", g=g, S=S, src=D[src_pref + g], srctb=self.dtb[src_pref + g], t0=t0, TT=512,
                                  W=win, Wtb=wintb, chunks=chunks,
                                  vparts=[(1792, 512, "vA_" + g), (2304, 256, "vB_" + g)]))
        NT = len(tiles)
        xn4 = [self.sb(es, "xnq%d" % i, [128, 1024], BF16) for i in range(2)]
        xn_stash = {}
        xcnt = {"n": 0}

        def load_x(ti):
            t = tiles[ti]
            nsub = t["TT"] // 128
            fw.dma("sp", xt[ti % 2][:, 0:nsub, :],
                   t["src"][t["t0"]:t["t0"] + t["TT"], :].rearrange("(j p) d -> p j d", p=128),
                   xttb[ti % 2], reads=[t["srctb"]], writes=[xttb[ti % 2]])

        def norm_a(ti, j):
            xb, xbtb = xt[ti % 2], xttb[ti % 2]
            (ss, sstb), (lnv, lnvtb), (rstd, rstdtb) = nxt("ss"), nxt("lnv"), nxt("rstd")
            self.token_rstd(xb[:, j, :], xbtb, junk, junktb, ss, sstb, lnv, lnvtb, rstd, rstdtb)
            (xnb, xnbtb) = xn4[xcnt["n"] % 2]
            xcnt["n"] += 1
            fw.op("pool", lambda e: e.tensor_scalar(xnb[:], xb[:, j, :], rstd[:, 0:1], 1.0, ALU.mult, ALU.mult),
                  reads=[xbtb, rstdtb], writes=[xnbtb])
            xn_stash[(ti, j)] = (xnb, xnbtb)

        def norm_b(ti, j):
            xT, xTtb = xnT[ti % 2], xnTtb[ti % 2]
            (xnb, xnbtb) = xn_stash.pop((ti, j))
            pi = pnext("tr")
            ptr = ps[pi][:].bitcast(BF16)

            def do_tr(e):
                last = None
                for k in range(8):
                    last = e.transpose(ptr[:, k * 128:(k + 1) * 128], xnb[:, k * 128:(k + 1) * 128], self.ident_b[:])
                return last
            fw.op("pe", do_tr, reads=[xnbtb, self.ident_b_tb], writes=[pstb[pi]])
            fw.op("dve", lambda e: e.tensor_copy(xT[:, :, j * 128:(j + 1) * 128], ptr.rearrange("p (k t) -> p k t", k=8)),
                  reads=[pstb[pi]], accw=[xTtb])

        def do_tile(ti):
            t = tiles[ti]
            TT, W, Wtb, chunks, vparts, t0 = t["TT"], t["W"], t["Wtb"], t["chunks"], t["vparts"], t["t0"]
            nsub = TT // 128
            xT, xTtb = xnT[ti % 2], xnTtb[ti % 2]
            sched = {}
            if ti + 1 < NT:
                nsn = tiles[ti + 1]["TT"] // 128
                nchk = len(chunks)
                if nchk >= 12:
                    for j in range(nsn):
                        sched.setdefault(1 + 3 * j, []).append(("a", j))
                        sched.setdefault(3 + 3 * j, []).append(("b", j))
                else:
                    for j in range(nsn):
                        sched.setdefault(nchk, []).append(("a", j))
                        sched.setdefault(nchk, []).append(("b", j))
            stash = {}

            def st1(ci):
                (wc, gc, (dname, didx)) = chunks[ci]
                zi = pnext("z")

                def do_z(e):
                    last = None
                    for k in range(8):
                        last = e.matmul(ps[zi][:, 0:TT], W[:, k, wc * 128:(wc + 1) * 128], xT[:, k, 0:TT],
                                        start=(k == 0), stop=(k == 7))
                    return last
                fw.op("pe", do_z, reads=[xTtb, Wtb], writes=[pstb[zi]])
                (sq, sqtb) = nxt("sq")
                fw.op("act", lambda e: e.activation(sq[:, 0:TT], ps[zi][:, 0:TT], AF.Square),
                      reads=[pstb[zi]], writes=[sqtb])
                stash[ci] = (zi, sq, sqtb)

            def st2(ci):
                (wc, gc, (dname, didx)) = chunks[ci]
                (zi, sq, sqtb) = stash.pop(ci)
                (lnq, lnqtb), (rsq, rsqtb), (zo, zotb) = nxt("lnq"), nxt("rsq"), nxt("zo")
                hi = pnext("hs")
                fw.op("pe", lambda e: e.matmul(ps[hi][:, 0:TT], self.bones_b[:], sq[:, 0:TT], start=True, stop=True),
                      reads=[sqtb, self.bones_b_tb], writes=[pstb[hi]])
                self.rstd_from_ss(ps[hi][:, 0:TT], rsq[:, 0:TT], lnq[:, 0:TT], 1.0 / 64, [pstb[hi]], [lnqtb], [rsqtb])
                fw.op("dve", lambda e: e.scalar_tensor_tensor(
                    zo[:, 0:TT], ps[zi][:, 0:TT], gq2[:, gc:gc + 1], rsq[:, 0:TT], ALU.mult, ALU.mult),
                    reads=[pstb[zi], rsqtb, gq2tb], writes=[zotb])
                fw.dma("pool", D[dname][didx, :, t0:t0 + TT], zo[:, 0:TT], zotb, reads=[zotb],
                       accw=[self.dtb[dname]])

            nchk = len(chunks)
            for ci in range(nchk + 1):
                if ci < nchk:
                    st1(ci)
                if ci >= 1:
                    st2(ci - 1)
                if ci == 1 and ti + 2 < NT:
                    load_x(ti + 2)
                for (what, j) in sched.get(ci, []):
                    if what == "a":
                        norm_a(ti + 1, j)
                    else:
                        norm_b(ti + 1, j)
            for j in range(nsub):
                (vo, votb) = nxt("vo")
                off = 0
                for (c0, cw, dname) in vparts:
                    vi = pnext("v")

                    def do_v(e, c0=c0, cw=cw, vi=vi, j=j):
                        last = None
                        for k in range(8):
                            last = e.matmul(ps[vi][:, 0:cw], xT[:, k, j * 128:(j + 1) * 128], W[:, k, c0:c0 + cw],
                                            start=(k == 0), stop=(k == 7))
                        return last
                    fw.op("pe", do_v, reads=[xTtb, Wtb], writes=[pstb[vi]])
                    fw.op("act", lambda e, vi=vi, off=off, cw=cw, vo=vo: e.copy(vo[:, off:off + cw], ps[vi][:, 0:cw]),
                          reads=[pstb[vi]], accw=[votb])
                    fw.dma("pool", D[dname][t0 + j * 128:t0 + (j + 1) * 128, :], vo[:, off:off + cw], votb,
                           reads=[votb], accw=[self.dtb[dname]])
                    off += cw

        load_x(0)
        if NT > 1:
            load_x(1)
        for j in range(tiles[0]["TT"] // 128):
            norm_a(0, j)
            norm_b(0, j)
        for ti in range(NT):
            do_tile(ti)
        self.end_phase(es)

    def setup_t5(self):
        nc, fw = self.nc, self.fw
        D = self.dram
        ges = self.es
        self.strip, self.strip_tb = self.sb(ges, "strip", [128, 4, 1152], F32)
        self.cb, self.cb_tb = self.sb(ges, "cb", [128, 8], F32, dma=True)
        for h in range(4):
            for side, b in ((0, 15), (1, 31)):
                fw.dma("sp", self.cb[:, 2 * h + side:2 * h + side + 1],
                       D["rel_bias"][b:b + 1, h:h + 1].partition_broadcast(128), self.cb_tb,
                       reads=[self.dtb["rel_bias"]], accw=[self.cb_tb], allow_slow_non_contiguous=True)
        es = self.begin_phase()
        g5, g5tb = self.sb(es, "g5", [32, 1280], F32, dma=True)
        g5b, g5btb = self.sb(es, "g5b", [32, 1280], BF16)
        rb, rbtb = self.sb(es, "rb", [32, 4], F32, dma=True)
        rb3, rb3tb = self.sb(es, "rb3", [32, 12], BF16)
        rd, rdtb = self.sb(es, "rd", [32, 8], F32)
        fw.dma("sp", g5[:], D["c_g5"][:, :], g5tb, reads=[self.dtb["c_g5"]], writes=[g5tb])
        fw.dma("sp", rb[:], D["rel_bias"][:, :], rbtb, reads=[self.dtb["rel_bias"]], writes=[rbtb])
        fw.op("dve", lambda e: e.tensor_copy(g5b[:], g5[:]), reads=[g5tb], writes=[g5btb])
        fw.op("dve", lambda e: e.tensor_copy(rb3[:, 0:4], rb[:]), reads=[rbtb], accw=[rb3tb])
        fw.op("dve", lambda e: e.tensor_tensor(rd[:, 0:4], rb[:], rb3[:, 0:4], ALU.subtract),
              reads=[rbtb, rb3tb], accw=[rdtb])
        fw.op("dve", lambda e: e.tensor_copy(rb3[:, 4:8], rd[:, 0:4]), reads=[rdtb], accw=[rb3tb])
        fw.op("dve", lambda e: e.tensor_tensor(rd[:, 4:8], rd[:, 0:4], rb3[:, 4:8], ALU.subtract),
              reads=[rdtb, rb3tb], accw=[rdtb])
        fw.op("dve", lambda e: e.tensor_copy(rb3[:, 8:12], rd[:, 4:8]), reads=[rdtb], accw=[rb3tb])
        ps, pstb = self.ps, self.pstb
        sa = [self.sb(es, "t5a%d" % i, [128, 32, 4], F32) for i in range(2)]
        sbb = [self.sb(es, "t5b%d" % i, [128, 32, 4], F32) for i in range(2)]
        for r in range(36):
            pi = r % 2
            (a_, atb), (b_, btb) = sa[r % 2], sbb[r % 2]

            def do_mm(e, r=r, pi=pi):
                last = None
                for ml in range(32):
                    m = r * 32 + ml
                    last = e.matmul(ps[pi][:, ml * 12:(ml + 1) * 12], g5b[:, 1152 - m:1280 - m], rb3[:, :],
                                    start=True, stop=True)
                return last
            fw.op("pe", do_mm, reads=[g5btb, rb3tb], writes=[pstb[pi]])
            pv = ps[pi][:, 0:384].rearrange("p (m t h) -> p m t h", t=3, h=4)
            fw.op("act", lambda e, a_=a_, pv=pv: e.copy(a_[:], pv[:, :, 0, :]), reads=[pstb[pi]], writes=[atb])
            fw.op("dve", lambda e, a_=a_, b_=b_, pv=pv: e.tensor_tensor(b_[:], pv[:, :, 1, :], a_[:], ALU.add),
                  reads=[pstb[pi], atb], writes=[btb])
            fw.op("dve", lambda e, b_=b_, pv=pv, r=r: e.tensor_tensor(
                self.strip[:, :, r * 32:(r + 1) * 32].rearrange("p h m -> p m h"), pv[:, :, 2, :], b_[:], ALU.add),
                reads=[pstb[pi], btb], accw=[self.strip_tb])
        self.end_phase(es)

    def phase_attn_a(self, l):
        nc, fw = self.nc, self.fw
        es = self.begin_phase()
        D = self.dram
        ps, pstb = self.ps, self.pstb
        lam_init = 0.8 - 0.6 * math.exp(-0.3 * l)
        lv = {}
        for nm in ("lam_q1", "lam_k1", "lam_q2", "lam_k2"):
            t, tb = self.sb(es, nm, [128, 64], F32, dma=True)
            fw.dma("sp", t[:], D[nm][l:l + 1, :].partition_broadcast(128), tb, reads=[self.dtb[nm]], writes=[tb],
                   allow_slow_non_contiguous=True)
            lv[nm] = (t, tb)
        ltmp, ltmptb = self.sb(es, "ltmp", [128, 64], F32)
        lsc, lsctb = self.sb(es, "lsc", [128, 8], F32)
        neg_lam, neg_lam_tb = self.sb(es, "neg_lam", [128, 1], F32)
        for i, (a, b) in enumerate((("lam_q1", "lam_k1"), ("lam_q2", "lam_k2"))):
            fw.op("dve", lambda e, a=a, b=b: e.tensor_tensor(ltmp[:], lv[a][0][:], lv[b][0][:], ALU.mult),
                  reads=[lv[a][1], lv[b][1]], writes=[ltmptb])
            fw.op("dve", lambda e, i=i: e.tensor_reduce(lsc[:, i:i + 1], ltmp[:], AX.X, ALU.add),
                  reads=[ltmptb], accw=[lsctb])
            fw.op("act", lambda e, i=i: e.activation(lsc[:, 2 + i:3 + i], lsc[:, i:i + 1], AF.Exp),
                  reads=[lsctb], accw=[lsctb])
        fw.op("dve", lambda e: e.tensor_tensor(lsc[:, 4:5], lsc[:, 3:4], lsc[:, 2:3], ALU.subtract),
              reads=[lsctb], accw=[lsctb])
        fw.op("dve", lambda e: e.tensor_scalar(neg_lam[:], lsc[:, 4:5], -lam_init, None, ALU.add),
              reads=[lsctb], writes=[neg_lam_tb])

        sel, seltb = self.sb(es, "sel", [128, 2, 128], F32)

        def mk_sel(e):
            e.memset(sel[:], 0.0)
            e.memset(sel[0:1, 0, :], 1.0)
            return e.memset(sel[64:65, 1, :], 1.0)
        fw.op("pool", mk_sel, writes=[seltb])
        qkv = {}
        Smax = max(S for (_, S) in self.groups)
        for nm, shp in (("Q", [128, Smax]), ("K", [128, Smax]), ("V", [128, Smax // 128, 128])):
            qkv[nm] = [self.sb(es, "%s%d" % (nm, i), shp, BF16, dma=True) for i in range(2)]
        P = [self.sb(es, "P%d" % i, [128, 1024], BF16) for i in range(6)]
        T = [self.sb(es, "T%d" % i, [128, 1024], F32) for i in range(2)]
        ep = {nm: [self.sb(es, "%s%d" % (nm, i), [128, 512], F32) for i in range(2)] for nm in ("r1", "o1", "r2", "t2")}
        ost = [self.sb(es, "ost%d" % i, [128, 512], BF16, dma=True) for i in range(2)]
        cnt = {"P": 0, "T": 0, "ep": 0, "ost": 0}

        heads = [(g, S, h) for (g, S) in self.groups for h in range(4)]

        def load_head(idx):
            g, S, h = heads[idx]
            sl = idx % 2
            (q, qtb), (k, ktb), (v, vtb) = qkv["Q"][sl], qkv["K"][sl], qkv["V"][sl]
            nm = "qkA_" + g
            for c0 in range(0, S, 2048):
                fw.dma("sp", q[:, c0:c0 + 2048], D[nm][h, :, c0:c0 + 2048], qtb, reads=[self.dtb[nm]], writes=[qtb] if c0 == 0 else (), accw=() if c0 == 0 else [qtb])
                fw.dma("sp", k[:, c0:c0 + 2048], D[nm][4 + h, :, c0:c0 + 2048], ktb, reads=[self.dtb[nm]], writes=[ktb] if c0 == 0 else (), accw=() if c0 == 0 else [ktb])
            vn = "vA_" + g
            for c0 in range(0, S // 128, 8):
                fw.dma("sp", v[:, c0:c0 + 8, :],
                       D[vn][c0 * 128:(c0 + 8) * 128, h * 128:(h + 1) * 128].rearrange("(c p) e -> p c e", p=128),
                       vtb, reads=[self.dtb[vn]], writes=[vtb] if c0 == 0 else (), accw=() if c0 == 0 else [vtb])

        def do_head(idx, g, S, h):
            sl = idx % 2
            (q, qtb), (k, ktb), (v, vtb) = qkv["Q"][sl], qkv["K"][sl], qkv["V"][sl]
            nq, nk = S // 512, S // 128
            units = [(qc, kc) for qc in range(nq) for kc in range(nk)]
            pend = {}

            def stage_a(u):
                qc, kc = units[u]
                q0, k0 = qc * 512, kc * 128
                slot = u % 2
                ba, bb = 2 * slot, 2 * slot + 1
                pd = self.ps2[slot]

                def do_qk(e):
                    e.matmul(ps[ba][:], k[0:64, k0:k0 + 128], q[0:64, q0:q0 + 512], start=True, stop=True)
                    return e.matmul(ps[bb][:], k[64:128, k0:k0 + 128], q[64:128, q0:q0 + 512], start=True, stop=True)
                fw.op("pe", do_qk, reads=[qtb, ktb], writes=[pstb[ba], pstb[bb]])
                delta = k0 - q0
                (pp, pptb) = P[cnt["P"] % 6]
                cnt["P"] += 1
                if -256 < delta < 640:
                    (tt, tttb) = T[cnt["T"] % 2]
                    cnt["T"] += 1
                    st0 = 512 - delta

                    def do_add(e):
                        e.tensor_tensor(tt[:, 0:512], ps[ba][:], self.strip[:, h, st0:st0 + 512], ALU.add)
                        return e.tensor_tensor(tt[:, 512:1024], ps[bb][:], self.strip[:, h, st0:st0 + 512], ALU.add)
                    fw.op("dve", do_add, reads=[pstb[ba], pstb[bb], self.strip_tb], writes=[tttb])
                    fw.op("act", lambda e: e.activation(pp[:], tt[:], AF.Exp), reads=[tttb], writes=[pptb])
                else:
                    ci = 2 * h + (0 if delta < 0 else 1)
                    fw.op("act", lambda e: e.activation(pp[:], pd[:], AF.Exp, bias=self.cb[:, ci:ci + 1]),
                          reads=[pstb[ba], pstb[bb], self.cb_tb], writes=[pptb])
                pend[u] = (pp, pptb)

            def stage_b(u):
                qc, kc1 = units[u]
                kc0 = kc1 - 1
                (p1, p1tb) = pend.pop(u)
                (p0, p0tb) = pend.pop(u - 1)
                first, lastk = (kc0 == 0), (kc1 == nk - 1)
                kc = kc1

                def do_av(e):
                    e.matmul(ps[4][:], v[:, kc0, :], p0[:, 0:512], start=first, stop=False)
                    e.matmul(ps[5][:], v[:, kc0, :], p0[:, 512:1024], start=first, stop=False)
                    e.matmul(ps[4][:], v[:, kc1, :], p1[:, 0:512], start=False, stop=lastk)
                    e.matmul(ps[5][:], v[:, kc1, :], p1[:, 512:1024], start=False, stop=lastk)
                    e.matmul(ps[6][0:64, :], self.ones_b[:, 0:64], p0[:, 0:512], start=first, stop=False)
                    e.matmul(ps[6][64:128, :], self.ones_b[:, 0:64], p0[:, 512:1024], start=first, stop=False)
                    e.matmul(ps[6][0:64, :], self.ones_b[:, 0:64], p1[:, 0:512], start=False, stop=lastk)
                    return e.matmul(ps[6][64:128, :], self.ones_b[:, 0:64], p1[:, 512:1024], start=False, stop=lastk)
                fw.op("pe", do_av, reads=[vtb, p0tb, p1tb, self.ones_b_tb],
                      writes=[pstb[4], pstb[5], pstb[6]] if first else (),
                      accw=() if first else [pstb[4], pstb[5], pstb[6]])
                if lastk:
                    i = cnt["ep"] % 2
                    cnt["ep"] += 1
                    (cO1, cO1tb), (cL, cLtb), (cO2, cO2tb) = ep["r1"][i], ep["o1"][i], ep["r2"][i]
                    (os_, ostb) = ost[cnt["ost"] % 2]
                    cnt["ost"] += 1
                    fw.op("dve", lambda e: e.tensor_copy(cO1[:], ps[4][:]), reads=[pstb[4]], writes=[cO1tb])
                    fw.op("act", lambda e: e.copy(cL[:], ps[6][:]), reads=[pstb[6]], writes=[cLtb])
                    fw.op("dve", lambda e: e.tensor_copy(cO2[:], ps[5][:]), reads=[pstb[5]], writes=[cO2tb])
                    fw.op("dve", lambda e: e.reciprocal(cL[:], cL[:]), reads=[cLtb], writes=[cLtb])

                    def part2():
                        fw.op("pe", lambda e: e.matmul(ps[7][:], sel[:, 0, :], cL[:], start=True, stop=True),
                              reads=[seltb, cLtb], writes=[pstb[7]])
                        fw.op("dve", lambda e: e.tensor_tensor(cO1[:], cO1[:], ps[7][:], ALU.mult),
                              reads=[cO1tb, pstb[7]], writes=[cO1tb])
                        fw.op("pe", lambda e: e.matmul(ps[7][:], sel[:, 1, :], cL[:], start=True, stop=True),
                              reads=[seltb, cLtb], writes=[pstb[7]])
                        fw.op("dve", lambda e: e.tensor_tensor(cO2[:], cO2[:], ps[7][:], ALU.mult),
                              reads=[cO2tb, pstb[7]], writes=[cO2tb])
                        fw.op("dve", lambda e: e.scalar_tensor_tensor(os_[:], cO2[:], neg_lam[:, 0:1], cO1[:],
                                                                      ALU.mult, ALU.add),
                              reads=[cO2tb, cO1tb, neg_lam_tb], writes=[ostb])
                        fw.dma("pool", D["mixT_" + g][h, :, qc * 512:(qc + 1) * 512], os_[:], ostb, reads=[ostb],
                               accw=[self.dtb["mixT_" + g]])
                    deferred.append([u + 8, part2])

            deferred = []
            N = len(units)
            assert N % 2 == 0 and nk % 2 == 0
            for i in range(0, N + 2, 2):
                if i < N:
                    stage_a(i)
                    stage_a(i + 1)
                if i >= 2:
                    stage_b(i - 1)
                    while deferred and deferred[0][0] <= i - 1:
                        deferred.pop(0)[1]()
            while deferred:
                deferred.pop(0)[1]()

        load_head(0)
        for idx, (g, S, h) in enumerate(heads):
            if idx + 1 < len(heads):
                load_head(idx + 1)
            do_head(idx, g, S, h)
        self.end_phase(es)

    def phase_attn_bc(self, l):
        nc, fw = self.nc, self.fw
        es = self.begin_phase()
        D = self.dram
        ps, pstb, ps2 = self.ps, self.pstb, self.ps2
        sna, snatb = self.sb(es, "sna", [32, 64, 128], F32, dma=True)
        fw.dma("sp", sna[:], D["c_sna"][:, :, :], snatb, reads=[self.dtb["c_sna"]], writes=[snatb])
        rm, rmtb = self.sb(es, "rm", [128, 2, 8, 8], F32, dma=True)
        fw.dma("sp", rm[:], D["c_rm"][:, :].rearrange("p (e j r) -> p e j r", e=2, j=8), rmtb,
               reads=[self.dtb["c_rm"]], writes=[rmtb])
        text, texttb = self.sb(es, "text", [32, 4, 15], F32, dma=True)
        fw.op("dve", lambda e: e.memset(text[:], 1.0), writes=[texttb])
        for ei in range(15):
            fw.dma("sp", text[0:31, :, ei], D["na_bias"][l, :, 14 - ei, :].rearrange("h d -> d h"), texttb,
                   reads=[self.dtb["na_bias"], texttb], accw=[texttb], allow_slow_non_contiguous=True)
        zz, zztb = self.sb(es, "zz", [128, 60, 64], F32)
        for r in range(8):
            pi = r % 2

            def do_mm(e, r=r, pi=pi):
                last = None
                for cl in range(8):
                    c = r * 8 + cl
                    last = e.matmul(ps[pi][:, cl * 60:(cl + 1) * 60], sna[:, c, :],
                                    text[:].rearrange("d h e -> d (h e)"), start=True, stop=True)
                return last
            fw.op("pe", do_mm, reads=[snatb, texttb], writes=[pstb[pi]])
            fw.op("dve", lambda e, r=r, pi=pi: e.tensor_copy(
                zz[:, :, r * 8:(r + 1) * 8], ps[pi][:, 0:480].rearrange("p (c x) -> p x c", c=8)),
                reads=[pstb[pi]], accw=[zztb])
        wall, walltb = self.sb(es, "wall", [128, 4, 22, 64], F32)
        wint, winttb = self.sb(es, "wint", [128, 4, 22, 64], F32)
        fw.op("pool", lambda e: e.memset(wall[:], NEG), writes=[walltb])
        fw.op("pool", lambda e: e.memset(wint[:], NEG), writes=[winttb])
        zv = zz[:].rearrange("p (h e) c -> p h e c", h=4)
        fw.op("dve", lambda e: e.tensor_copy(wall[0:64, :, 3:18, :], zv[0:64, :, :, :]), reads=[zztb], writes=[walltb])
        fw.op("dve", lambda e: e.tensor_copy(wall[64:128, :, 4:19, :], zv[64:128, :, :, :]), reads=[zztb], accw=[walltb])
        fw.op("dve", lambda e: e.tensor_copy(wint[0:64, :, 7:15, :], zv[0:64, :, 4:12, :]), reads=[zztb], writes=[winttb])
        fw.op("dve", lambda e: e.tensor_copy(wint[64:128, :, 8:16, :], zv[64:128, :, 4:12, :]), reads=[zztb], accw=[winttb])

        qb = [self.sb(es, "qb%d" % i, [128, 512], BF16, dma=True) for i in range(3)]
        kb = [self.sb(es, "kb%d" % i, [128, 1024], BF16, dma=True) for i in range(3)]
        vb = [self.sb(es, "vb%d" % i, [128, 8, 128], BF16, dma=True) for i in range(3)]
        kmg = {g: self.sb(es, "km_" + g, [128, 2, MEM], BF16, dma=True) for (g, _) in self.groups}
        vmg = {g: self.sb(es, "vm_" + g, [128, 2, 256], BF16, dma=True) for (g, _) in self.groups}
        P = [self.sb(es, "P%d" % i, [128, 1024], BF16) for i in range(4)]
        T = [self.sb(es, "T%d" % i, [128, 1024], F32) for i in range(3)]
        rr = [self.sb(es, "rr%d" % i, [128, 512], F32) for i in range(2)]
        ost = [self.sb(es, "ost%d" % i, [128, 512], BF16, dma=True) for i in range(2)]
        cnt = {"P": 0, "T": 0, "job": 0, "u": 0}

        pend = {}

        def stage_a(u):
            q, qtb = u["q"], u["qtb"]
            slot = cnt["u"] % 2
            cnt["u"] += 1
            ba, bb = 2 * slot, 2 * slot + 1

            def do_qk(e):
                e.matmul(ps[ba][:], u["kA"], q[0:64, :], start=True, stop=True)
                return e.matmul(ps[bb][:], u["kB"], q[64:128, :], start=True, stop=True)
            fw.op("pe", do_qk, reads=[qtb] + u["ktbs"], writes=[pstb[ba], pstb[bb]])
            (pp, pptb) = P[cnt["P"] % 4]
            cnt["P"] += 1
            if u["bias"] is not None:
                (tt, tttb) = T[cnt["T"] % 3]
                cnt["T"] += 1
                wsel, wtb, j, edge, hp = u["bias"]
                i0 = 14 - 2 * j

                def do_add(e):
                    e.tensor_tensor(tt[:, 0:512], ps[ba][:],
                                    wsel[:, 2 * hp, i0:i0 + 8, :].rearrange("p a c -> p (a c)"), ALU.add)
                    return e.tensor_tensor(tt[:, 512:1024], ps[bb][:],
                                           wsel[:, 2 * hp + 1, i0:i0 + 8, :].rearrange("p a c -> p (a c)"), ALU.add)
                fw.op("dve", do_add, reads=[pstb[ba], pstb[bb], wtb], writes=[tttb])
                if edge is not None:
                    def do_rm(e):
                        last = None
                        for hh in range(2):
                            tv = tt[:, hh * 512:(hh + 1) * 512].rearrange("p (a c) -> p a c", c=64)
                            last = e.tensor_tensor(tv, tv, rm[:, edge, j, :].unsqueeze(2).to_broadcast([128, 8, 64]),
                                                   ALU.add)
                        return last
                    fw.op("dve", do_rm, reads=[tttb, rmtb], writes=[tttb])
                fw.op("act", lambda e: e.activation(pp[:], tt[:], AF.Exp), reads=[tttb], writes=[pptb])
            else:
                fw.op("act", lambda e: e.activation(pp[:], ps2[slot][:], AF.Exp),
                      reads=[pstb[ba], pstb[bb]], writes=[pptb])
            pend[id(u)] = (pp, pptb)

        def stage_b(u):
            (pp, pptb) = pend.pop(id(u))
            jobi, first, lastu = u["job"], u["first"], u["last"]
            ob, lb = 4 + jobi % 2, 6 + jobi % 2

            def do_av(e):
                e.matmul(ps[ob][0:64, :], u["vA"], pp[:, 0:512], start=first, stop=lastu)
                e.matmul(ps[ob][64:128, :], u["vB"], pp[:, 512:1024], start=first, stop=lastu)
                e.matmul(ps[lb][0:64, :], self.ones_b[:, 0:64], pp[:, 0:512], start=first, stop=lastu)
                return e.matmul(ps[lb][64:128, :], self.ones_b[:, 0:64], pp[:, 512:1024], start=first, stop=lastu)
            fw.op("pe", do_av, reads=u["vtbs"] + [pptb, self.ones_b_tb],
                  writes=[pstb[ob], pstb[lb]] if first else (), accw=() if first else [pstb[ob], pstb[lb]])
            if lastu:
                (r_, rtb) = rr[jobi % 2]
                (os_, ostb) = ost[jobi % 2]
                g, q0, dest_idx = u["g"], u["q0"], u["dest"]
                fw.op("act", lambda e: e.activation(r_[:], ps[lb][:], AF.Ln), reads=[pstb[lb]], writes=[rtb])
                fw.op("act", lambda e: e.activation(r_[:], r_[:], AF.Exp, scale=-1.0), reads=[rtb], writes=[rtb])
                fw.op("dve", lambda e: e.tensor_tensor(os_[:], ps[ob][:], r_[:], ALU.mult),
                      reads=[pstb[ob], rtb], writes=[ostb])
                fw.dma("pool", D["mixT_" + g][dest_idx, :, q0:q0 + 512], os_[:], ostb, reads=[ostb],
                       accw=[self.dtb["mixT_" + g]])

        jobs = []
        for (g, S) in self.groups:
            R = S // GRID_W
            for qt in range(S // 512):
                for hp in range(2):
                    jobs.append(("B", g, S, qt, hp))
            for qt in range(S // 512):
                for hp in range(2):
                    jobs.append(("C", g, S, qt, hp))

        def load_job(ji):
            kind, g, S, qt, hp = jobs[ji]
            sl = ji % 3
            q0 = qt * 512
            (q, qtb) = qb[sl]
            (km, kmtb), (vm, vmtb) = kmg[g], vmg[g]
            if kind == "B":
                fw.dma("sp", q[:], D["qkB_" + g][hp, :, q0:q0 + 512], qtb, reads=[self.dtb["qkB_" + g]], writes=[qtb])
                ks = q0 - 256
                lo, hi = max(ks, 0), min(ks + 1024, S)
                (k, ktb), (v, vtb) = kb[sl], vb[sl]
                fw.dma("sp", k[:, lo - ks:hi - ks], D["qkB_" + g][2 + hp, :, lo:hi], ktb,
                       reads=[self.dtb["qkB_" + g]], writes=[ktb])
                fw.dma("sp", v[:, (lo - ks) // 128:(hi - ks) // 128, :],
                       D["vB_" + g][lo:hi, hp * 128:(hp + 1) * 128].rearrange("(j p) e -> p j e", p=128), vtb,
                       reads=[self.dtb["vB_" + g]], writes=[vtb])
            else:
                fw.dma("sp", q[:], D["qC_" + g][hp, :, q0:q0 + 512], qtb, reads=[self.dtb["qC_" + g]], writes=[qtb])
                if qt == 0 and hp == 0:
                    fw.dma("sp", km[:], D["kmT_" + g][:, :, :].rearrange("c p m -> p c m"), kmtb,
                           reads=[self.dtb["kmT_" + g]], writes=[kmtb])
                    fw.dma("sp", vm[:], D["vm_" + g][:, :].rearrange("(c p) e -> p c e", p=128), vmtb,
                           reads=[self.dtb["vm_" + g]], writes=[vmtb])

        def job_units(ji):
            kind, g, S, qt, hp = jobs[ji]
            sl = ji % 3
            q0 = qt * 512
            (q, qtb) = qb[sl]
            (km, kmtb), (vm, vmtb) = kmg[g], vmg[g]
            units = []
            if kind == "B":
                (k, ktb), (v, vtb) = kb[sl], vb[sl]
                ks = q0 - 256
                nt = S // 512
                edge = 0 if qt == 0 else (1 if qt == nt - 1 else None)
                wsel, wtb = (wint, winttb) if edge is None else (wall, walltb)
                for j in range(8):
                    if ks + j * 128 < 0 or ks + j * 128 >= S:
                        continue
                    units.append(dict(kA=k[0:64, j * 128:(j + 1) * 128], kB=k[64:128, j * 128:(j + 1) * 128], ktbs=[ktb],
                                      vA=v[:, j, 0:64], vB=v[:, j, 64:128], vtbs=[vtb],
                                      bias=(wsel, wtb, j, edge, hp), dest=4 + hp))
            else:
                for mc in range(2):
                    units.append(dict(kA=km[0:64, hp, mc * 128:(mc + 1) * 128], kB=km[64:128, hp, mc * 128:(mc + 1) * 128],
                                      ktbs=[kmtb], vA=vm[:, mc, (2 * hp) * 64:(2 * hp + 1) * 64],
                                      vB=vm[:, mc, (2 * hp + 1) * 64:(2 * hp + 2) * 64], vtbs=[vmtb], bias=None,
                                      dest=6 + hp))
            for i, u in enumerate(units):
                u.update(job=ji, g=g, q0=q0, q=q, qtb=qtb, first=(i == 0), last=(i == len(units) - 1))
            return units

        load_job(0)
        flat = []
        for ji in range(len(jobs)):
            flat.extend(job_units(ji))
        nu = len(flat)
        for i in range(nu + 2):
            if i < nu:
                u = flat[i]
                if u["first"] and u["job"] + 1 < len(jobs):
                    load_job(u["job"] + 1)
                stage_a(u)
            if i >= 2:
                stage_b(flat[i - 2])
        self.end_phase(es)

    def phase_outproj(self, l):
        nc, fw = self.nc, self.fw
        es = self.begin_phase()
        D = self.dram
        ps, pstb = self.ps, self.pstb
        lam_init = 0.8 - 0.6 * math.exp(-0.3 * l)
        src_pref = "x_" if l == 0 else "x1_"
        stage = [self.sb(es, "wst%d" % i, [128, 1024], F32, dma=True) for i in range(3)]
        wout, wouttb = self.prep_weight(es, "wout", D["w_out"][l], self.dtb["w_out"], 8, D_MODEL, None, None,
                                        [t for t, _ in stage], [tb for _, tb in stage])
        gs0, gs0tb = self.sb(es, "gs0", [128, 1], F32, dma=True)
        gsub, gsubtb = self.sb(es, "gsub", [128, 1], F32)
        fw.dma("sp", gs0[:], D["subln_g"][l, :].rearrange("(p o) -> p o", o=1), gs0tb, reads=[self.dtb["subln_g"]],
               writes=[gs0tb], allow_slow_non_contiguous=True)
        fw.op("dve", lambda e: e.tensor_scalar(gsub[:], gs0[:], 1.0 - lam_init, None, ALU.mult),
              reads=[gs0tb], writes=[gsubtb])
        mx = [self.sb(es, "mx%d" % i, [128, 8, 512], BF16, dma=True) for i in range(3)]
        xt = [self.sb(es, "xt%d" % i, [128, 4, 1024], F32, dma=True) for i in range(3)]
        sq = [self.sb(es, "sq%d" % i, [128, 512], BF16) for i in range(2)]
        lnq = [self.sb(es, "lnq%d" % i, [128, 512], F32) for i in range(2)]
        rsq = [self.sb(es, "rsq%d" % i, [128, 512], F32) for i in range(2)]
        cnt = {"n": 0, "ps": 0}
        tiles = [(g, S, t0) for (g, S) in self.groups for t0 in range(0, S, 512)]

        def load_tile(ti):
            g, S, t0 = tiles[ti]
            (m, mtb), (x, xtb) = mx[ti % 3], xt[ti % 3]
            fw.dma("sp", m[:], D["mixT_" + g][:, :, t0:t0 + 512].rearrange("c p t -> p c t"), mtb,
                   reads=[self.dtb["mixT_" + g]], writes=[mtb])
            fw.dma("sp", x[:], D[src_pref + g][t0:t0 + 512, :].rearrange("(j p) d -> p j d", p=128), xtb,
                   reads=[self.dtb[src_pref + g]], writes=[xtb])

        def norm_tile(ti):
            g, S, t0 = tiles[ti]
            (m, mtb) = mx[ti % 3]
            for c in range(4):
                i = cnt["n"] % 2
                cnt["n"] += 1
                (sq_, sqtb), (ln_, lntb), (rs_, rstb) = sq[i], lnq[i], rsq[i]
                pi = 4 + (cnt["n"] % 2)
                fw.op("act", lambda e, c=c, sq_=sq_: e.activation(sq_[:], m[:, c, :], AF.Square), reads=[mtb], writes=[sqtb])
                fw.op("pe", lambda e, sq_=sq_, pi=pi: e.matmul(ps[pi][:], self.ones_b[:], sq_[:], start=True, stop=True),
                      reads=[sqtb, self.ones_b_tb], writes=[pstb[pi]])
                self.rstd_from_ss(ps[pi][:], rs_[:], ln_[:], 1.0 / 128, [pstb[pi]], [lntb], [rstb])
                fw.op("dve", lambda e, c=c, rs_=rs_: e.scalar_tensor_tensor(m[:, c, :], m[:, c, :], gsub[:, 0:1], rs_[:],
                                                                              ALU.mult, ALU.mult),
                      reads=[mtb, rstb, gsubtb], writes=[mtb])

        def do_tile(ti):
            g, S, t0 = tiles[ti]
            (m, mtb), (x, xtb) = mx[ti % 3], xt[ti % 3]
            for j in range(4):
                for n in range(2):
                    pi = cnt["ps"] % 4
                    cnt["ps"] += 1

                    def do_mm(e, j=j, n=n, pi=pi):
                        last = None
                        for c in range(8):
                            last = e.matmul(ps[pi][:], m[:, c, j * 128:(j + 1) * 128], wout[:, c, n * 512:(n + 1) * 512],
                                            start=(c == 0), stop=(c == 7))
                        return last
                    fw.op("pe", do_mm, reads=[mtb, wouttb], writes=[pstb[pi]])
                    fw.op("dve", lambda e, j=j, n=n, pi=pi: e.tensor_tensor(
                        x[:, j, n * 512:(n + 1) * 512], ps[pi][:], x[:, j, n * 512:(n + 1) * 512], ALU.add),
                        reads=[pstb[pi], xtb], accw=[xtb])
            fw.dma("pool", D["xmid_" + g][t0:t0 + 512, :].rearrange("(j p) d -> p j d", p=128), x[:], xtb,
                   reads=[xtb], accw=[self.dtb["xmid_" + g]])

        load_tile(0)
        if len(tiles) > 1:
            load_tile(1)
        norm_tile(0)
        for ti in range(len(tiles)):
            if ti + 2 < len(tiles):
                load_tile(ti + 2)
            if ti + 1 < len(tiles):
                norm_tile(ti + 1)
            do_tile(ti)
        self.end_phase(es)

    def phase_ffn(self, l, half, last):
        nc, fw = self.nc, self.fw
        es = self.begin_phase()
        D = self.dram
        ps, pstb = self.ps, self.pstb
        HC = NFF // 2
        HW_ = HC * 128
        stage = [self.sb(es, "wst%d" % i, [128, HW_], F32, dma=True) for i in range(2)]
        stl, sttb = [t for t, _ in stage], [tb for _, tb in stage]
        g2, g2tb = self.load_gain_cols(es, "g2", D["norm2_g"][l, :], self.dtb["norm2_g"], 8)
        wupv, wupvtb = self.prep_weight(es, "wupv", D["w_up"][l], self.dtb["w_up"], 8, HW_, g2, g2tb, stl, sttb,
                                        col0=half * HW_)
        wupg, wupgtb = self.prep_weight(es, "wupg", D["w_up"][l], self.dtb["w_up"], 8, HW_, g2, g2tb, stl, sttb,
                                        col0=D_FF + half * HW_)
        wdn, wdntb = self.sb(es, "wdn", [128, HC, D_MODEL], BF16)
        for i in range(HC):
            st, stb = stage[i % 2]
            r0 = half * HW_ + i * 128
            fw.dma("sp", st[:, 0:1024], D["w_down"][l, r0:r0 + 128, :], stb, reads=[self.dtb["w_down"]], writes=[stb])
            eng = ("pool", "dve")[i % 2]
            fw.op(eng, lambda e, st=st, i=i: e.tensor_copy(wdn[:, i, :], st[:, 0:1024]), reads=[stb], accw=[wdntb])
        for part, c0 in ((0, half * HW_), (1, D_FF + half * HW_)):
            (st, stb) = stage[part]
            fw.dma("sp", st[0:3, :], D["conv_w"][l, :, c0:c0 + HW_], stb, reads=[self.dtb["conv_w"]], writes=[stb])
            fw.dma("sp", st[3:4, :], D["conv_b"][l:l + 1, c0:c0 + HW_], stb, reads=[self.dtb["conv_b"]], accw=[stb])
        cwt, cwttb = self.sb(es, "cwt", [128, 2 * HC, 4], F32)

        def do_cwt(e):
            last = None
            for f in range(2 * HC):
                st = stage[f // HC][0]
                fl = f % HC
                last = e.transpose(ps[0][:, f * 4:(f + 1) * 4], st[0:4, fl * 128:(fl + 1) * 128], self.ident_f[0:4, 0:4])
            return last
        fw.op("pe", do_cwt, reads=[sttb[0], sttb[1], self.ident_f_tb], writes=[pstb[0]])
        fw.op("dve", lambda e: e.tensor_copy(cwt[:], ps[0][:, 0:8 * HC].rearrange("p (f w) -> p f w", w=4)),
              reads=[pstb[0]], writes=[cwttb])

        xs = [self.sb(es, "xs%d" % i, [128, 4, 1024], F32, dma=True) for i in range(2)]
        og = [self.sb(es, "og%d" % i, [128, 1024], F32, dma=True) for i in range(4)]
        junk, junktb = self.sb(es, "junk", [128, 1024], BF16)
        xn = [self.sb(es, "xn%d" % i, [128, 1024], BF16) for i in range(2)]
        xnT = [self.sb(es, "xnT%d" % i, [128, 8, 512], BF16) for i in range(2)]
        hT = [self.sb(es, "hT%d" % i, [128, HC, 512], BF16) for i in range(2)]
        for (t, tb) in hT:
            fw.op("pool", lambda e, t=t: e.memset(t[:], 0.0), writes=[tb])
        av = [self.sb(es, "av%d" % i, [128, 512], F32) for i in range(3)]
        ag = [self.sb(es, "ag%d" % i, [128, 512], F32) for i in range(3)]
        sg = [self.sb(es, "sg%d" % i, [128, 512], F32) for i in range(2)]
        sm = {nm: [self.sb(es, "%s%d" % (nm, i), [128, 1], F32) for i in range(4)] for nm in ("ss", "lnv", "rstd")}
        cnt = {"c": 0, "og": 0, "dn": 0, "xn": 0, "tr": 0}
        dest_pref = "y_" if last else "x1_"
        tiles = [(g, S, t0) for (g, S) in self.groups for t0 in range(0, S, FT)]

        def rows(ti):
            g, S, t0 = tiles[ti]
            T = min(FT, S - t0)
            return g, S, t0, T, t0 - 1

        def load_tile(ti):
            g, S, t0, T, r0 = rows(ti)
            (x, xtb) = xs[ti % 2]
            lo, hi = max(r0, 0), min(r0 + 512, S)
            full = (lo == r0 and hi == r0 + 512)
            if full:
                fw.dma("sp", x[:], D["xmid_" + g][r0:r0 + 512, :].rearrange("(j p) d -> p j d", p=128), xtb,
                       reads=[self.dtb["xmid_" + g]], writes=[xtb])
                return
            fw.op("pool", lambda e: e.memset(x[:], 0.0), writes=[xtb])
            for j in range(4):
                a, b = max(r0 + 128 * j, lo), min(r0 + 128 * (j + 1), hi)
                if a >= b:
                    continue
                p0 = a - (r0 + 128 * j)
                fw.dma_rows("sp", lambda p, c, j=j: x[p:p + c, j, :],
                            lambda o, c, a=a: D["xmid_" + g][a + o:a + o + c, :], p0, b - a, xtb, True,
                            reads=[self.dtb["xmid_" + g], xtb])

        xn_stash = {}

        def norm_sub_a(ti, j):
            (x, xtb) = xs[ti % 2]
            (ss, sstb), (lnv, lnvtb), (rstd, rstdtb) = sm["ss"][j], sm["lnv"][j], sm["rstd"][j]
            self.token_rstd(x[:, j, :], xtb, junk, junktb, ss, sstb, lnv, lnvtb, rstd, rstdtb)
            (xnb, xnbtb) = xn[cnt["xn"] % 2]
            cnt["xn"] += 1
            fw.op("pool", lambda e: e.tensor_scalar(xnb[:], x[:, j, :], rstd[:, 0:1], 1.0, ALU.mult, ALU.mult),
                  reads=[xtb, rstdtb], writes=[xnbtb])
            xn_stash[(ti, j)] = (xnb, xnbtb)

        def norm_sub_b(ti, j):
            (xT, xTtb) = xnT[ti % 2]
            (xnb, xnbtb) = xn_stash.pop((ti, j))
            pi = cnt["tr"] % 2
            cnt["tr"] += 1
            ptr = ps[pi][:].bitcast(BF16)

            def do_tr(e):
                last = None
                for k in range(8):
                    last = e.transpose(ptr[:, k * 128:(k + 1) * 128], xnb[:, k * 128:(k + 1) * 128], self.ident_b[:])
                return last
            fw.op("pe", do_tr, reads=[xnbtb, self.ident_b_tb], writes=[pstb[pi]])
            fw.op("act", lambda e: e.copy(xT[:, :, j * 128:(j + 1) * 128], ptr.rearrange("p (k t) -> p k t", k=8)),
                  reads=[pstb[pi]], accw=[xTtb])

        chunk_res = {}

        def up_chunk(ti, i):
            (xT, xTtb) = xnT[ti % 2]
            res = {}
            for kind, W, Wtb, f, bufs in (("v", wupv, wupvtb, i, av), ("g", wupg, wupgtb, HC + i, ag)):
                pi = 2 + (cnt["c"] % 4)
                cnt["c"] += 1
                (a_, atb) = bufs[i % 3]

                def do_up(e, W=W, pi=pi):
                    last = None
                    for k in range(8):
                        last = e.matmul(ps[pi][:], W[:, k, i * 128:(i + 1) * 128], xT[:, k, :],
                                        start=(k == 0), stop=(k == 7))
                    return last
                fw.op("pe", do_up, reads=[xTtb, Wtb], writes=[pstb[pi]])
                fw.op("act", lambda e, a_=a_, pi=pi, f=f: e.activation(
                    a_[:, 1:511], ps[pi][:, 1:511], AF.Identity, bias=cwt[:, f, 3:4], scale=cwt[:, f, 1:2]),
                    reads=[pstb[pi], cwttb], writes=[atb])
                fw.op("dve", lambda e, a_=a_, pi=pi, f=f: e.scalar_tensor_tensor(
                    a_[:, 1:511], ps[pi][:, 0:510], cwt[:, f, 0:1], a_[:, 1:511], ALU.mult, ALU.add),
                    reads=[pstb[pi], cwttb, atb], writes=[atb])
                fw.op("dve", lambda e, a_=a_, pi=pi, f=f: e.scalar_tensor_tensor(
                    a_[:, 1:511], ps[pi][:, 2:512], cwt[:, f, 2:3], a_[:, 1:511], ALU.mult, ALU.add),
                    reads=[pstb[pi], cwttb, atb], writes=[atb])
                res[kind] = (a_, atb)
            chunk_res[(ti, i)] = res

        def finish_chunk(ti, i):
            (h_, htb) = hT[ti % 2]
            res = chunk_res.pop((ti, i))
            (s_, stb_) = sg[i % 2]
            (a_g, agtb), (a_v, avtb) = res["g"], res["v"]
            fw.op("act", lambda e: e.activation(s_[:, 1:511], a_g[:, 1:511], AF.Silu), reads=[agtb], writes=[stb_])
            fw.op("pool", lambda e: e.tensor_tensor(h_[:, i, 1:511], a_v[:, 1:511], s_[:, 1:511], ALU.mult),
                  reads=[avtb, stb_], accw=[htb])

        base_pref = "xmid_" if half == 0 else dest_pref

        def og_prefetch(ti):
            g, S, t0, T, r0 = rows(ti)
            for j in range(4):
                a, b = max(r0 + 128 * j, t0), min(r0 + 128 * (j + 1), t0 + T)
                if a >= b:
                    continue
                p0 = a - (r0 + 128 * j)
                (o_, otb) = og[j]
                fw.dma_rows("sp", lambda p, c, o_=o_: o_[p:p + c, :],
                            lambda o, c, a=a: D[base_pref + g][a + o:a + o + c, :], p0, b - a, otb, True,
                            first_is_write=True, reads=[self.dtb[base_pref + g]])

        def do_tile(ti):
            nxt = ti + 1 < len(tiles)
            for i in range(HC):
                if i == 0 and nxt:
                    load_tile(ti + 1)
                up_chunk(ti, i)
                if i >= 1:
                    finish_chunk(ti, i - 1)
                if i == 1 and ti >= 1:
                    down_tile(ti - 1)
                if i == 5 and ti >= 1:
                    store_tile(ti - 1)
                if i == 6:
                    og_prefetch(ti)
                if nxt:
                    if i in (2, 4, 6, 8):
                        norm_sub_a(ti + 1, (i - 2) // 2)
                    if i in (4, 6, 8, 10):
                        norm_sub_b(ti + 1, (i - 4) // 2)
            finish_chunk(ti, HC - 1)

        def down_tile(ti):
            g, S, t0, T, r0 = rows(ti)
            (h_, htb) = hT[ti % 2]
            for j in range(4):
                a, b = max(r0 + 128 * j, t0), min(r0 + 128 * (j + 1), t0 + T)
                if a >= b:
                    continue
                (o_, otb) = og[j]
                for n in range(2):
                    pi = 6 + (cnt["dn"] % 2)
                    cnt["dn"] += 1

                    def do_dn(e, j=j, n=n, pi=pi):
                        last = None
                        for i in range(HC):
                            last = e.matmul(ps[pi][:], h_[:, i, j * 128:(j + 1) * 128], wdn[:, i, n * 512:(n + 1) * 512],
                                            start=(i == 0), stop=(i == HC - 1))
                        return last
                    fw.op("pe", do_dn, reads=[htb, wdntb], writes=[pstb[pi]])
                    fw.op("dve", lambda e, n=n, pi=pi, o_=o_: e.tensor_tensor(
                        o_[:, n * 512:(n + 1) * 512], ps[pi][:], o_[:, n * 512:(n + 1) * 512], ALU.add),
                        reads=[pstb[pi], otb], accw=[otb])

        def store_tile(ti):
            g, S, t0, T, r0 = rows(ti)
            for j in range(4):
                a, b = max(r0 + 128 * j, t0), min(r0 + 128 * (j + 1), t0 + T)
                if a >= b:
                    continue
                p0 = a - (r0 + 128 * j)
                (o_, otb) = og[j]
                fw.dma_rows("pool", lambda p, c, o_=o_: o_[p:p + c, :],
                            lambda o, c, a=a: D[dest_pref + g][a + o:a + o + c, :], p0, b - a, otb, False,
                            reads=[otb], accw=[self.dtb[dest_pref + g]])

        load_tile(0)
        for j in range(4):
            norm_sub_a(0, j)
            norm_sub_b(0, j)
        for ti in range(len(tiles)):
            do_tile(ti)
        down_tile(len(tiles) - 1)
        store_tile(len(tiles) - 1)
        self.end_phase(es)


_W_NAMES = ("norm1_g", "qn_a", "kn_a", "lam_q1", "lam_k1", "lam_q2", "lam_k2", "subln_g", "rel_bias", "qn_b",
            "kn_b", "na_bias", "mem_g", "w_mem_kv", "qn_c", "kn_c", "w_out", "norm2_g", "w_up", "conv_w",
            "conv_b", "w_down")


def make_in_maps(inputs, n_cores, groups):
    consts = _host_consts()
    perm = _win_perm()
    w_in_p = np.ascontiguousarray(np.asarray(inputs["w_in"], np.float32)[:, :, perm])
    shared = {n: np.ascontiguousarray(np.asarray(inputs[n], np.float32)) for n in _W_NAMES}
    shared["w_in"] = w_in_p
    shared.update(consts)
    srcmap = {"p": ("x_prompt", "mem_prompt"), "s": ("x_sample", "mem_sample")}
    in_maps = []
    for c in range(n_cores):
        m = dict(shared)
        for (g, S) in groups:
            xn, mn = srcmap[g]
            m["x_" + g] = np.ascontiguousarray(np.asarray(inputs[xn][c], np.float32))
            m["mem_" + g] = np.ascontiguousarray(np.asarray(inputs[mn][c], np.float32))
        in_maps.append(m)
    return in_maps


def kernel(**inputs):
    groups = [("p", 8192), ("s", 2048)]
    prog = Prog(groups, DEPTH)
    nc = prog.build()
    in_maps = make_in_maps(inputs, 8, groups)
    res = run_bass_kernel_spmd(nc, in_maps, core_ids=list(range(8)))
    y_p = np.stack([np.asarray(r["y_p"], np.float32) for r in res.results], axis=0)
    y_s = np.stack([np.asarray(r["y_s"], np.float32) for r in res.results], axis=0)
    return (y_p, y_s)
```

```python
import math
from contextlib import ExitStack

import numpy as np
import concourse.bass as bass
import concourse.mybir as mybir
from concourse.bass_utils import run_bass_kernel_spmd

F32 = mybir.dt.float32
BF16 = mybir.dt.bfloat16
AF = mybir.ActivationFunctionType
ALU = mybir.AluOpType
AX = mybir.AxisListType

D_MODEL = 1024
DEPTH = 2
NCH = 8
HEAD_DIM = 64
IN_WIDTH = 2560
NQK = 14
VW = 768
D_FF = 2816
NFF = 22
MEM = 256
EPS = 1e-6
NEG = -30000.0
GRID_W = 64
FT = 510

ENGS = ("pe", "act", "dve", "pool", "sp")


class TB:
    __slots__ = ("name", "w", "r", "sem")

    def __init__(self, name):
        self.name = name
        self.w = {}
        self.r = {}
        self.sem = None


class FW:
    def __init__(self, nc, es, n_dma_sems=88):
        self.nc = nc
        self.ops = {e: [] for e in ENGS}
        self.seq = {e: 0 for e in ENGS}
        self.seen = {e: {} for e in ENGS}
        self.need = {e: set() for e in ENGS}
        self.esem = {e: es.enter_context(nc.semaphore("s_" + e)) for e in ENGS}
        self.dsems = [es.enter_context(nc.semaphore("d%d" % i)) for i in range(n_dma_sems)]
        self.dfree = list(range(n_dma_sems))
        self.dcount = [0] * n_dma_sems

    def tb(self, name, dma=False):
        t = TB(name)
        if dma:
            t.sem = self.dfree.pop()
        return t

    def release(self, tbs):
        for t in tbs:
            if t.sem is not None:
                self.dfree.append(t.sem)
                t.sem = None

    def op(self, eng, fn, reads=(), writes=(), accw=(), dsem=None):
        waits = {}

        def merge(d):
            for k, v in d.items():
                if waits.get(k, 0) < v:
                    waits[k] = v

        for t in reads:
            merge(t.w)
        for t in writes:
            merge(t.w)
            merge(t.r)
        for t in accw:
            merge(t.r)
        if dsem is not None:
            key = ("d", dsem)
            self.dcount[dsem] += 16
            val = self.dcount[dsem]
        else:
            key = ("e", eng)
            self.seq[eng] += 1
            val = self.seq[eng]
        mywaits = []
        seen = self.seen[eng]
        for k, v in waits.items():
            if eng == "pe" and k == ("e", "pe"):
                continue
            if seen.get(k, 0) >= v:
                continue
            seen[k] = v
            mywaits.append((k, v))
            if k[0] == "e":
                self.need[k[1]].add(v)
        self.ops[eng].append((fn, mywaits, key, val))
        for t in reads:
            if t.r.get(key, 0) < val:
                t.r[key] = val
        for t in writes:
            t.w = {key: val}
            t.r = {}
        for t in accw:
            if t.w.get(key, 0) < val:
                t.w[key] = val

    def dma(self, q, out, in_, semtb, reads=(), writes=(), accw=(), **kw):
        self.op(q, lambda e: e.dma_start(out=out, in_=in_, **kw), reads=reads, writes=writes,
                accw=accw, dsem=semtb.sem)

    def dma_rows(self, q, sb_fn, dr_fn, p0, n, semtb, to_sbuf, first_is_write=False, **kw):
        pieces = []
        n16 = (n // 16) * 16
        if n16 > 0:
            pieces.append((0, n16))
        if n - n16 > 0:
            pieces.append((n16, n - n16))
        for idx, (o, c) in enumerate(pieces):
            sb_ap, dr_ap = sb_fn(p0 + o, c), dr_fn(o, c)
            k = dict(kw)
            if to_sbuf:
                if first_is_write and idx == 0:
                    k["writes"] = list(k.get("writes", [])) + [semtb]
                else:
                    k["accw"] = list(k.get("accw", [])) + [semtb]
                self.dma(q, sb_ap, dr_ap, semtb, **k)
            else:
                self.dma(q, dr_ap, sb_ap, semtb, **k)

    def barrier(self):
        allw = {}
        for e in ENGS:
            if self.seq[e] > 0:
                allw[("e", e)] = self.seq[e]
        for i, c in enumerate(self.dcount):
            if c > 0:
                allw[("d", i)] = c
        t = TB("barrier")
        t.w = allw
        for e in ENGS:
            self.op(e, lambda eng: eng.nop(), reads=[t])

    def emit(self):
        sigidx = {}
        for e in ENGS:
            m = {}
            cnt = 0
            for i in sorted(self.need[e]):
                cnt += 1
                m[i] = cnt
            sigidx[e] = m
        esem, dsems, need = self.esem, self.dsems, self.need
        dcount = self.dcount

        def body(ename, eng, final=False):
            for (fn, waits, key, val) in self.ops[ename]:
                for (k, v) in waits:
                    if k[0] == "e":
                        eng.wait_ge(esem[k[1]], sigidx[k[1]][v])
                    else:
                        eng.wait_ge(dsems[k[1]], v)
                ins = fn(eng)
                if key[0] == "d":
                    ins.then_inc(dsems[key[1]], 16)
                elif val in need[ename]:
                    ins.then_inc(esem[ename], 1)
            if final:
                for i, c in enumerate(dcount):
                    if c > 0:
                        eng.wait_ge(dsems[i], c)
                for e in ENGS:
                    if e != ename and sigidx[e]:
                        eng.wait_ge(esem[e], len(sigidx[e]))

        with self.nc.Block() as block:
            block.tensor(lambda eng: body("pe", eng))
            block.scalar(lambda eng: body("act", eng))
            block.vector(lambda eng: body("dve", eng))
            block.gpsimd(lambda eng: body("pool", eng))
            block.sync(lambda eng: body("sp", eng, final=True))


def _t5_bucket(rp):
    half, max_exact = 16, 8
    ret = np.where(rp > 0, half, 0)
    n = np.abs(rp)
    nf = np.maximum(n, 1).astype(np.float32)
    large = max_exact + (np.log(nf / np.float32(max_exact)) / np.float32(math.log(128 / max_exact))
                         * (half - max_exact)).astype(np.int32)
    large = np.minimum(large, half - 1)
    return ret + np.where(n < max_exact, n, large)


def _host_consts():
    c = {}
    i = np.arange(1280)
    bk = _t5_bucket(i - 640)
    G = np.zeros((32, 1280), np.float32)
    G[bk, i] = 1.0
    c["c_g5"] = G
    S = np.zeros((32, 64, 128), np.float32)
    for cq in range(64):
        cs = min(max(cq - 8, 0), 48)
        for p in range(128):
            ck = p % 64
            if cs <= ck < cs + 16:
                S[ck - cq + 15, cq, p] = 1.0
            else:
                S[31, cq, p] = NEG
    c["c_sna"] = S
    RM = np.zeros((128, 2, 8, 8), np.float32)
    for p in range(128):
        rl_k = p // 64
        for j in range(8):
            for rl in range(8):
                r = rl
                rs = max(r - 4, 0)
                rk = -4 + 2 * j + rl_k
                ok = (rs <= rk < rs + 8)
                RM[p, 0, j, rl] = 0.0 if ok else NEG
                rs = min(rl - 4, 0)
                rk = -4 + 2 * j + rl_k
                ok = (rs <= rk < rs + 8) and rk < 8
                RM[p, 1, j, rl] = 0.0 if ok else NEG
    c["c_rm"] = RM.reshape(128, 128)
    c["c_ident"] = np.eye(128, dtype=np.float32)
    bo = np.zeros((128, 128), np.float32)
    bo[:64, :64] = 1.0
    bo[64:, 64:] = 1.0
    c["c_bones"] = bo
    return c


def _win_perm():
    cols = []
    for h in range(4):
        cols += list(range(h * 64, h * 64 + 64)) + list(range(256 + h * 64, 256 + h * 64 + 64))
    for h in range(4):
        cols += list(range(512 + h * 64, 512 + h * 64 + 64)) + list(range(768 + h * 64, 768 + h * 64 + 64))
    cols += list(range(1536, 1792))
    cols += list(range(1792, 2048))
    cols += list(range(2304, 2560))
    cols += list(range(1024, 1536))
    cols += list(range(2048, 2304))
    return np.array(cols, np.int64)


class Prog:
    def __init__(self, groups, depth, debug=(), stop_after=None):
        self.groups = groups
        self.depth = depth
        self.debug = set(debug)
        self.stop_after = stop_after
        self.nc = bass.Bass("TRN2", target_bir_lowering=False)
        self.es = ExitStack()
        self.fw = FW(self.nc, self.es)
        self.dram = {}
        self.dtb = {}

    def din(self, name, shape, dt=F32):
        self.dram[name] = self.nc.dram_tensor(name, list(shape), dt, kind="ExternalInput").ap()
        self.dtb[name] = TB(name)
        return self.dram[name]

    def dout(self, name, shape, dt=F32):
        self.dram[name] = self.nc.dram_tensor(name, list(shape), dt, kind="ExternalOutput").ap()
        self.dtb[name] = TB(name)
        return self.dram[name]

    def dscr(self, name, shape, dt):
        kind = "ExternalOutput" if name in self.debug else "Internal"
        self.dram[name] = self.nc.dram_tensor(name, list(shape), dt, kind=kind).ap()
        self.dtb[name] = TB(name)
        return self.dram[name]

    def sb(self, es, name, shape, dt, dma=False):
        self._uid = getattr(self, "_uid", 0) + 1
        name = "sb%d_%s" % (self._uid, name)
        t = es.enter_context(self.nc.sbuf_tensor(name, list(shape), dt))
        tb = self.fw.tb(name, dma=dma)
        self._phase_tbs.append(tb)
        return t, tb

    def begin_phase(self):
        self._phase_tbs = []
        return ExitStack()

    def end_phase(self, es):
        self.fw.barrier()
        self.fw.release(self._phase_tbs)
        es.close()

    def build(self):
        nc, fw = self.nc, self.fw
        for (g, S) in self.groups:
            self.din("x_" + g, [S, D_MODEL])
            self.din("mem_" + g, [MEM, D_MODEL])
            self.dout("y_" + g, [S, D_MODEL])
        L = self.depth
        self.din("norm1_g", [DEPTH, D_MODEL]); self.din("w_in", [DEPTH, D_MODEL, IN_WIDTH])
        for n in ("qn_a", "kn_a", "lam_q1", "lam_k1", "lam_q2", "lam_k2", "qn_b", "kn_b", "qn_c", "kn_c"):
            self.din(n, [DEPTH, 64])
        self.din("subln_g", [DEPTH, 128]); self.din("rel_bias", [32, 4])
        self.din("na_bias", [DEPTH, 4, 15, 31]); self.din("mem_g", [DEPTH, D_MODEL])
        self.din("w_mem_kv", [DEPTH, D_MODEL, 512]); self.din("w_out", [DEPTH, D_MODEL, D_MODEL])
        self.din("norm2_g", [DEPTH, D_MODEL]); self.din("w_up", [DEPTH, D_MODEL, 2 * D_FF])
        self.din("conv_w", [DEPTH, 3, 2 * D_FF]); self.din("conv_b", [DEPTH, 2 * D_FF])
        self.din("w_down", [DEPTH, D_FF, D_MODEL])
        self.din("c_g5", [32, 1280]); self.din("c_sna", [32, 64, 128]); self.din("c_rm", [128, 128])
        self.din("c_ident", [128, 128]); self.din("c_bones", [128, 128])
        for (g, S) in self.groups:
            self.dscr("qkA_" + g, [8, 128, S], BF16)
            self.dscr("vA_" + g, [S, 512], BF16)
            self.dscr("qkB_" + g, [4, 128, S], BF16)
            self.dscr("vB_" + g, [S, 256], BF16)
            self.dscr("qC_" + g, [2, 128, S], BF16)
            self.dscr("kmT_" + g, [2, 128, MEM], BF16)
            self.dscr("vm_" + g, [MEM, 256], BF16)
            self.dscr("mixT_" + g, [8, 128, S], BF16)
            self.dscr("xmid_" + g, [S, D_MODEL], F32)
            self.dscr("x1_" + g, [S, D_MODEL], F32)

        ges = self.es
        self._phase_tbs = []
        self.ps = []
        self.pstb = []
        self.ps2 = []
        for i in range(4):
            self.ps2.append(ges.enter_context(nc.psum_tensor("psd%d" % i, [128, 1024], F32)))
        for i in range(8):
            self.ps.append(self.ps2[i // 2][:, (i % 2) * 512:(i % 2 + 1) * 512])
            self.pstb.append(fw.tb("ps%d" % i))
        self.ident_f, self.ident_f_tb = self.sb(ges, "ident_f", [128, 128], F32, dma=True)
        self.ident_b, self.ident_b_tb = self.sb(ges, "ident_b", [128, 128], BF16)
        self.ones_b, self.ones_b_tb = self.sb(ges, "ones_b", [128, 128], BF16)
        self.bones_f, self.bones_f_tb = self.sb(ges, "bones_f", [128, 128], F32, dma=True)
        self.bones_b, self.bones_b_tb = self.sb(ges, "bones_b", [128, 128], BF16)
        fw.dma("sp", self.ident_f[:], self.dram["c_ident"][:, :], self.ident_f_tb,
               reads=[self.dtb["c_ident"]], writes=[self.ident_f_tb])
        fw.dma("sp", self.bones_f[:], self.dram["c_bones"][:, :], self.bones_f_tb,
               reads=[self.dtb["c_bones"]], writes=[self.bones_f_tb])
        fw.op("dve", lambda e: e.tensor_copy(self.ident_b[:], self.ident_f[:]),
              reads=[self.ident_f_tb], writes=[self.ident_b_tb])
        fw.op("dve", lambda e: e.tensor_copy(self.bones_b[:], self.bones_f[:]),
              reads=[self.bones_f_tb], writes=[self.bones_b_tb])
        fw.op("dve", lambda e: e.memset(self.ones_b[:], 1.0), writes=[self.ones_b_tb])
        self.eps_t, self.eps_tb = self.sb(ges, "eps", [128, 1], F32)
        fw.op("dve", lambda e: e.memset(self.eps_t[:], EPS), writes=[self.eps_tb])
        self.mhalf, self.mhalf_tb = self.sb(ges, "mhalf", [128, 1], F32)
        fw.op("dve", lambda e: e.memset(self.mhalf[:], -0.5), writes=[self.mhalf_tb])

        self.setup_t5()
        for l in range(L):
            last = (l == L - 1)
            self.phase_proj(l)
            if self.stop_after == ("proj", l):
                break
            self.phase_attn_a(l)
            if self.stop_after == ("attn_a", l):
                break
            self.phase_attn_bc(l)
            if self.stop_after == ("attn_bc", l):
                break
            self.phase_outproj(l)
            if self.stop_after == ("outproj", l):
                break
            self.phase_ffn(l, 0, last)
            self.phase_ffn(l, 1, last)
        fw.emit()
        self.es.close()
        return nc

    def load_gain_cols(self, es, name, src_ap, src_tb, nchunk):
        fw = self.fw
        t, tb = self.sb(es, name, [128, nchunk], F32, dma=True)
        fw.dma("sp", t[:], src_ap.rearrange("(k p) -> p k", p=128), tb, reads=[src_tb], writes=[tb],
               allow_slow_non_contiguous=True)
        return t, tb

    def prep_weight(self, es, name, w_ap, w_tb, nk, ncols, gain, gain_tb, stage, stage_tbs, col0=0,
                    engines=("pool", "dve")):
        fw = self.fw
        wt, wtb = self.sb(es, name, [128, nk, ncols], BF16)
        CW = stage[0].shape[1]
        i = 0
        for k in range(nk):
            for c0 in range(0, ncols, CW):
                cw = min(CW, ncols - c0)
                st, stb = stage[i % len(stage)], stage_tbs[i % len(stage)]
                fw.dma("sp", st[:, 0:cw], w_ap[k * 128:(k + 1) * 128, col0 + c0:col0 + c0 + cw], stb,
                       reads=[w_tb], writes=[stb])
                eng = engines[i % len(engines)]
                if gain is None:
                    fw.op(eng, (lambda e, st=st, c0=c0, cw=cw, k=k: e.tensor_copy(wt[:, k, c0:c0 + cw], st[:, 0:cw])),
                          reads=[stb], accw=[wtb])
                else:
                    fw.op(eng, (lambda e, st=st, c0=c0, cw=cw, k=k: e.tensor_scalar(
                        wt[:, k, c0:c0 + cw], st[:, 0:cw], gain[:, k:k + 1], 1.0, ALU.mult, ALU.mult)),
                        reads=[stb, gain_tb], accw=[wtb])
                i += 1
        return wt, wtb

    def token_rstd(self, x_ap, xtb, junk, junktb, ss, sstb, tmp, tmptb, rstd, rstdtb):
        fw = self.fw
        fw.op("dve", lambda e: e.scalar_tensor_tensor(junk[:], x_ap, 1.0, x_ap, ALU.mult, ALU.mult,
                                                      accum_out=ss[:, 0:1]),
              reads=[xtb], writes=[junktb, sstb])
        fw.op("pool", lambda e: e.tensor_scalar(tmp[:, 0:1], ss[:, 0:1], 1.0 / D_MODEL, EPS, ALU.mult, ALU.add),
              reads=[sstb], writes=[tmptb])
        fw.op("pool", lambda e: e.tensor_tensor(rstd[:, 0:1], tmp[:, 0:1], self.mhalf[:, 0:1], ALU.pow),
              reads=[tmptb, self.mhalf_tb], writes=[rstdtb])

    def rstd_from_ss(self, ss_ap, out_ap, tmp_ap, inv_n, tbs_r, tbs_tmp, tbs_out):
        fw = self.fw
        fw.op("act", lambda e: e.activation(tmp_ap, ss_ap, AF.Ln, bias=self.eps_t[:, 0:1], scale=inv_n),
              reads=tbs_r + [self.eps_tb], writes=tbs_tmp)
        fw.op("act", lambda e: e.activation(out_ap, tmp_ap, AF.Exp, scale=-0.5),
              reads=tbs_tmp, writes=tbs_out)

    def phase_proj(self, l):
        nc, fw = self.nc, self.fw
        es = self.begin_phase()
        D = self.dram
        src_pref = "x_" if l == 0 else "x1_"
        stage = []
        stage_tbs = []
        for i in range(3):
            t, tb = self.sb(es, "wst%d" % i, [128, 2560], F32, dma=True)
            stage.append(t); stage_tbs.append(tb)
        g1, g1tb = self.load_gain_cols(es, "g1", D["norm1_g"][l, :], self.dtb["norm1_g"], 8)
        gm, gmtb = self.load_gain_cols(es, "gm", D["mem_g"][l, :], self.dtb["mem_g"], 8)
        win, wintb = self.prep_weight(es, "win", D["w_in"][l], self.dtb["w_in"], 8, IN_WIDTH, g1, g1tb,
                                      stage, stage_tbs)
        wmem, wmemtb = self.prep_weight(es, "wmem", D["w_mem_kv"][l], self.dtb["w_mem_kv"], 8, 512, gm, gmtb,
                                        stage, stage_tbs)
        gq, gqtb = self.sb(es, "gq", [128, 16], F32, dma=True)
        plan = [("qn_a", range(0, 4)), ("kn_a", range(4, 8)), ("qn_b", range(8, 10)), ("kn_b", range(10, 12)),
                ("qn_c", range(12, 14)), ("kn_c", range(14, 16))]
        for (nm, cols) in plan:
            for half in range(2):
                src = D[nm][l, :].rearrange("(p o) -> p o", o=1)
                fw.dma("sp", gq[half * 64:(half + 1) * 64, cols[0]:cols[0] + 1], src, gqtb,
                       reads=[self.dtb[nm]], accw=[gqtb], allow_slow_non_contiguous=True)
        gq2, gq2tb = self.sb(es, "gq2", [128, 16], F32)

        def mk_gq2(e):
            last = None
            for (nm, cols) in plan:
                sc = 0.125 if nm.startswith("qn") else 1.0
                for c in cols:
                    last = e.tensor_scalar(gq2[:, c:c + 1], gq[:, cols[0]:cols[0] + 1], sc, None, ALU.mult)
            return last
        fw.op("dve", mk_gq2, reads=[gqtb], writes=[gq2tb])

        xt = []; xttb = []
        for i in range(2):
            t, tb = self.sb(es, "xt%d" % i, [128, 4, 1024], F32, dma=True)
            xt.append(t); xttb.append(tb)
        junk, junktb = self.sb(es, "junk", [128, 1024], BF16)
        xn = []; xntb = []
        for i in range(2):
            t, tb = self.sb(es, "xn%d" % i, [128, 1024], BF16)
            xn.append(t); xntb.append(tb)
        xnT = []; xnTtb = []
        for i in range(2):
            t, tb = self.sb(es, "xnT%d" % i, [128, 8, 512], BF16)
            xnT.append(t); xnTtb.append(tb)
        st = {}
        for nm, shp, dt, n in (("ss", [128, 1], F32, 4), ("lnv", [128, 1], F32, 4), ("rstd", [128, 1], F32, 4),
                               ("sq", [128, 512], BF16, 3), ("lnq", [128, 512], F32, 3), ("rsq", [128, 512], F32, 3),
                               ("zo", [128, 512], BF16, 4), ("vo", [128, 768], BF16, 3)):
            st[nm] = [self.sb(es, "%s%d" % (nm, i), shp, dt, dma=(nm in ("zo", "vo"))) for i in range(n)]
        cnt = {k: 0 for k in st}

        def nxt(nm):
            i = cnt[nm] % len(st[nm])
            cnt[nm] += 1
            return st[nm][i]

        ps, pstb = self.ps, self.pstb
        psrot = {"tr": [0, 1], "z": [2, 3, 4], "hs": [5], "v": [6, 7]}
        pcnt = {k: 0 for k in psrot}

        def pnext(role):
            i = psrot[role][pcnt[role] % len(psrot[role])]
            pcnt[role] += 1
            return i

        tiles = []
        for (g, S) in self.groups:
            tiles.append(dict(kind="mem", g=g, S=MEM, src=D["mem_" + g], srctb=self.dtb["mem_" + g], t0=0, TT=256,
                              W=wmem, Wtb=wmemtb, chunks=[(0, 14, ("kmT_" + g, 0)), (1, 15, ("kmT_" + g, 1))],
                              vparts=[(256, 256, "vm_" + g)]))
        for (g, S) in self.groups:
            chunks = [(c, c, ("qkA_" + g, c) if c < 8 else (("qkB_" + g, c - 8) if c < 12 else ("qC_" + g, c - 12)))
                      for c in range(NQK)]
            for t0 in range(0, S, 512):
                tiles.append(dict(kind="x", g=g, S=S, src=D[src_pref + g], srctb=self.dtb[src_pref + g], t0=t0, TT=512,
                                  W=win, Wtb=wintb, chunks=chunks,
                                  vparts=[(1792, 512, "vA_" + g), (2304, 256, "vB_" + g)]))
        NT = len(tiles)
        xn4 = [self.sb(es, "xnq%d" % i, [128, 1024], BF16) for i in range(2)]
        xn_stash = {}
        xcnt = {"n": 0}

        def load_x(ti):
            t = tiles[ti]
            nsub = t["TT"] // 128
            fw.dma("sp", xt[ti % 2][:, 0:nsub, :],
                   t["src"][t["t0"]:t["t0"] + t["TT"], :].rearrange("(j p) d -> p j d", p=128),
                   xttb[ti % 2], reads=[t["srctb"]], writes=[xttb[ti % 2]])

        def norm_a(ti, j):
            xb, xbtb = xt[ti % 2], xttb[ti % 2]
            (ss, sstb), (lnv, lnvtb), (rstd, rstdtb) = nxt("ss"), nxt("lnv"), nxt("rstd")
            self.token_rstd(xb[:, j, :], xbtb, junk, junktb, ss, sstb, lnv, lnvtb, rstd, rstdtb)
            (xnb, xnbtb) = xn4[xcnt["n"] % 2]
            xcnt["n"] += 1
            fw.op("pool", lambda e: e.tensor_scalar(xnb[:], xb[:, j, :], rstd[:, 0:1], 1.0, ALU.mult, ALU.mult),
                  reads=[xbtb, rstdtb], writes=[xnbtb])
            xn_stash[(ti, j)] = (xnb, xnbtb)

        def norm_b(ti, j):
            xT, xTtb = xnT[ti % 2], xnTtb[ti % 2]
            (xnb, xnbtb) = xn_stash.pop((ti, j))
            pi = pnext("tr")
            ptr = ps[pi][:].bitcast(BF16)

            def do_tr(e):
                last = None
                for k in range(8):
                    last = e.transpose(ptr[:, k * 128:(k + 1) * 128], xnb[:, k * 128:(k + 1) * 128], self.ident_b[:])
                return last
            fw.op("pe", do_tr, reads=[xnbtb, self.ident_b_tb], writes=[pstb[pi]])
            fw.op("dve", lambda e: e.tensor_copy(xT[:, :, j * 128:(j + 1) * 128], ptr.rearrange("p (k t) -> p k t", k=8)),
                  reads=[pstb[pi]], accw=[xTtb])

        def do_tile(ti):
            t = tiles[ti]
            TT, W, Wtb, chunks, vparts, t0 = t["TT"], t["W"], t["Wtb"], t["chunks"], t["vparts"], t["t0"]
            nsub = TT // 128
            xT, xTtb = xnT[ti % 2], xnTtb[ti % 2]
            sched = {}
            if ti + 1 < NT:
                nsn = tiles[ti + 1]["TT"] // 128
                nchk = len(chunks)
                if nchk >= 12:
                    for j in range(nsn):
                        sched.setdefault(1 + 3 * j, []).append(("a", j))
                        sched.setdefault(3 + 3 * j, []).append(("b", j))
                else:
                    for j in range(nsn):
                        sched.setdefault(nchk, []).append(("a", j))
                        sched.setdefault(nchk, []).append(("b", j))
            stash = {}

            def st1(ci):
                (wc, gc, (dname, didx)) = chunks[ci]
                zi = pnext("z")

                def do_z(e):
                    last = None
                    for k in range(8):
                        last = e.matmul(ps[zi][:, 0:TT], W[:, k, wc * 128:(wc + 1) * 128], xT[:, k, 0:TT],
                                        start=(k == 0), stop=(k == 7))
                    return last
                fw.op("pe", do_z, reads=[xTtb, Wtb], writes=[pstb[zi]])
                (sq, sqtb) = nxt("sq")
                fw.op("act", lambda e: e.activation(sq[:, 0:TT], ps[zi][:, 0:TT], AF.Square),
                      reads=[pstb[zi]], writes=[sqtb])
                stash[ci] = (zi, sq, sqtb)

            def st2(ci):
                (wc, gc, (dname, didx)) = chunks[ci]
                (zi, sq, sqtb) = stash.pop(ci)
                (lnq, lnqtb), (rsq, rsqtb), (zo, zotb) = nxt("lnq"), nxt("rsq"), nxt("zo")
                hi = pnext("hs")
                fw.op("pe", lambda e: e.matmul(ps[hi][:, 0:TT], self.bones_b[:], sq[:, 0:TT], start=True, stop=True),
                      reads=[sqtb, self.bones_b_tb], writes=[pstb[hi]])
                self.rstd_from_ss(ps[hi][:, 0:TT], rsq[:, 0:TT], lnq[:, 0:TT], 1.0 / 64, [pstb[hi]], [lnqtb], [rsqtb])
                fw.op("dve", lambda e: e.scalar_tensor_tensor(
                    zo[:, 0:TT], ps[zi][:, 0:TT], gq2[:, gc:gc + 1], rsq[:, 0:TT], ALU.mult, ALU.mult),
                    reads=[pstb[zi], rsqtb, gq2tb], writes=[zotb])
                fw.dma("pool", D[dname][didx, :, t0:t0 + TT], zo[:, 0:TT], zotb, reads=[zotb],
                       accw=[self.dtb[dname]])

            nchk = len(chunks)
            for ci in range(nchk + 1):
                if ci < nchk:
                    st1(ci)
                if ci >= 1:
                    st2(ci - 1)
                if ci == 1 and ti + 2 < NT:
                    load_x(ti + 2)
                for (what, j) in sched.get(ci, []):
                    if what == "a":
                        norm_a(ti + 1, j)
                    else:
                        norm_b(ti + 1, j)
            for j in range(nsub):
                (vo, votb) = nxt("vo")
                off = 0
                for (c0, cw, dname) in vparts:
                    vi = pnext("v")

                    def do_v(e, c0=c0, cw=cw, vi=vi, j=j):
                        last = None
                        for k in range(8):
                            last = e.matmul(ps[vi][:, 0:cw], xT[:, k, j * 128:(j + 1) * 128], W[:, k, c0:c0 + cw],
                                            start=(k == 0), stop=(k == 7))
                        return last
                    fw.op("pe", do_v, reads=[xTtb, Wtb], writes=[pstb[vi]])
                    if (j + (1 if off else 0)) % 2 == 0:
                        fw.op("act", lambda e, vi=vi, off=off, cw=cw, vo=vo: e.copy(vo[:, off:off + cw], ps[vi][:, 0:cw]),
                              reads=[pstb[vi]], accw=[votb])
                    else:
                        fw.op("dve", lambda e, vi=vi, off=off, cw=cw, vo=vo: e.tensor_copy(vo[:, off:off + cw], ps[vi][:, 0:cw]),
                              reads=[pstb[vi]], accw=[votb])
                    fw.dma("pool", D[dname][t0 + j * 128:t0 + (j + 1) * 128, :], vo[:, off:off + cw], votb,
                           reads=[votb], accw=[self.dtb[dname]])
                    off += cw

        load_x(0)
        if NT > 1:
            load_x(1)
        for j in range(tiles[0]["TT"] // 128):
            norm_a(0, j)
            norm_b(0, j)
        for ti in range(NT):
            do_tile(ti)
        self.end_phase(es)

    def setup_t5(self):
        nc, fw = self.nc, self.fw
        D = self.dram
        ges = self.es
        self.strip, self.strip_tb = self.sb(ges, "strip", [128, 4, 1152], F32)
        self.cb, self.cb_tb = self.sb(ges, "cb", [128, 8], F32, dma=True)
        for h in range(4):
            for side, b in ((0, 15), (1, 31)):
                fw.dma("sp", self.cb[:, 2 * h + side:2 * h + side + 1],
                       D["rel_bias"][b:b + 1, h:h + 1].partition_broadcast(128), self.cb_tb,
                       reads=[self.dtb["rel_bias"]], accw=[self.cb_tb], allow_slow_non_contiguous=True)
        es = self.begin_phase()
        g5, g5tb = self.sb(es, "g5", [32, 1280], F32, dma=True)
        g5b, g5btb = self.sb(es, "g5b", [32, 1280], BF16)
        rb, rbtb = self.sb(es, "rb", [32, 4], F32, dma=True)
        rb3, rb3tb = self.sb(es, "rb3", [32, 12], BF16)
        rd, rdtb = self.sb(es, "rd", [32, 8], F32)
        fw.dma("sp", g5[:], D["c_g5"][:, :], g5tb, reads=[self.dtb["c_g5"]], writes=[g5tb])
        fw.dma("sp", rb[:], D["rel_bias"][:, :], rbtb, reads=[self.dtb["rel_bias"]], writes=[rbtb])
        fw.op("dve", lambda e: e.tensor_copy(g5b[:], g5[:]), reads=[g5tb], writes=[g5btb])
        fw.op("dve", lambda e: e.tensor_copy(rb3[:, 0:4], rb[:]), reads=[rbtb], accw=[rb3tb])
        fw.op("dve", lambda e: e.tensor_tensor(rd[:, 0:4], rb[:], rb3[:, 0:4], ALU.subtract),
              reads=[rbtb, rb3tb], accw=[rdtb])
        fw.op("dve", lambda e: e.tensor_copy(rb3[:, 4:8], rd[:, 0:4]), reads=[rdtb], accw=[rb3tb])
        fw.op("dve", lambda e: e.tensor_tensor(rd[:, 4:8], rd[:, 0:4], rb3[:, 4:8], ALU.subtract),
              reads=[rdtb, rb3tb], accw=[rdtb])
        fw.op("dve", lambda e: e.tensor_copy(rb3[:, 8:12], rd[:, 4:8]), reads=[rdtb], accw=[rb3tb])
        ps, pstb = self.ps, self.pstb
        sa = [self.sb(es, "t5a%d" % i, [128, 32, 4], F32) for i in range(2)]
        sbb = [self.sb(es, "t5b%d" % i, [128, 32, 4], F32) for i in range(2)]
        for r in range(36):
            pi = r % 2
            (a_, atb), (b_, btb) = sa[r % 2], sbb[r % 2]

            def do_mm(e, r=r, pi=pi):
                last = None
                for ml in range(32):
                    m = r * 32 + ml
                    last = e.matmul(ps[pi][:, ml * 12:(ml + 1) * 12], g5b[:, 1152 - m:1280 - m], rb3[:, :],
                                    start=True, stop=True)
                return last
            fw.op("pe", do_mm, reads=[g5btb, rb3tb], writes=[pstb[pi]])
            pv = ps[pi][:, 0:384].rearrange("p (m t h) -> p m t h", t=3, h=4)
            fw.op("act", lambda e, a_=a_, pv=pv: e.copy(a_[:], pv[:, :, 0, :]), reads=[pstb[pi]], writes=[atb])
            fw.op("dve", lambda e, a_=a_, b_=b_, pv=pv: e.tensor_tensor(b_[:], pv[:, :, 1, :], a_[:], ALU.add),
                  reads=[pstb[pi], atb], writes=[btb])
            fw.op("dve", lambda e, b_=b_, pv=pv, r=r: e.tensor_tensor(
                self.strip[:, :, r * 32:(r + 1) * 32].rearrange("p h m -> p m h"), pv[:, :, 2, :], b_[:], ALU.add),
                reads=[pstb[pi], btb], accw=[self.strip_tb])
        self.end_phase(es)

    def phase_attn_a(self, l):
        nc, fw = self.nc, self.fw
        es = self.begin_phase()
        D = self.dram
        ps, pstb = self.ps, self.pstb
        lam_init = 0.8 - 0.6 * math.exp(-0.3 * l)
        lv = {}
        for nm in ("lam_q1", "lam_k1", "lam_q2", "lam_k2"):
            t, tb = self.sb(es, nm, [128, 64], F32, dma=True)
            fw.dma("sp", t[:], D[nm][l:l + 1, :].partition_broadcast(128), tb, reads=[self.dtb[nm]], writes=[tb],
                   allow_slow_non_contiguous=True)
            lv[nm] = (t, tb)
        ltmp, ltmptb = self.sb(es, "ltmp", [128, 64], F32)
        lsc, lsctb = self.sb(es, "lsc", [128, 8], F32)
        neg_lam, neg_lam_tb = self.sb(es, "neg_lam", [128, 1], F32)
        for i, (a, b) in enumerate((("lam_q1", "lam_k1"), ("lam_q2", "lam_k2"))):
            fw.op("dve", lambda e, a=a, b=b: e.tensor_tensor(ltmp[:], lv[a][0][:], lv[b][0][:], ALU.mult),
                  reads=[lv[a][1], lv[b][1]], writes=[ltmptb])
            fw.op("dve", lambda e, i=i: e.tensor_reduce(lsc[:, i:i + 1], ltmp[:], AX.X, ALU.add),
                  reads=[ltmptb], accw=[lsctb])
            fw.op("act", lambda e, i=i: e.activation(lsc[:, 2 + i:3 + i], lsc[:, i:i + 1], AF.Exp),
                  reads=[lsctb], accw=[lsctb])
        fw.op("dve", lambda e: e.tensor_tensor(lsc[:, 4:5], lsc[:, 3:4], lsc[:, 2:3], ALU.subtract),
              reads=[lsctb], accw=[lsctb])
        fw.op("dve", lambda e: e.tensor_scalar(neg_lam[:], lsc[:, 4:5], -lam_init, None, ALU.add),
              reads=[lsctb], writes=[neg_lam_tb])

        sel, seltb = self.sb(es, "sel", [128, 2, 128], F32)

        fw.op("pool", lambda e: e.memset(sel[:], 0.0), writes=[seltb])
        fw.op("pool", lambda e: e.memset(sel[0:1, 0, :], 1.0), writes=[seltb])
        fw.op("pool", lambda e: e.memset(sel[64:65, 1, :], 1.0), writes=[seltb])
        qkv = {}
        Smax = max(S for (_, S) in self.groups)
        for nm, shp in (("Q", [128, Smax]), ("K", [128, Smax]), ("V", [128, Smax // 128, 128])):
            qkv[nm] = [self.sb(es, "%s%d" % (nm, i), shp, BF16, dma=True) for i in range(2)]
        P = [self.sb(es, "P%d" % i, [128, 1024], BF16) for i in range(6)]
        T = [self.sb(es, "T%d" % i, [128, 1024], F32) for i in range(2)]
        ep = {nm: [self.sb(es, "%s%d" % (nm, i), [128, 512], F32) for i in range(2)] for nm in ("r1", "o1", "r2", "t2")}
        ost = [self.sb(es, "ost%d" % i, [128, 512], BF16, dma=True) for i in range(2)]
        cnt = {"P": 0, "T": 0, "ep": 0, "ost": 0}

        heads = [(g, S, h) for (g, S) in self.groups for h in range(4)]

        def load_head(idx):
            g, S, h = heads[idx]
            sl = idx % 2
            (q, qtb), (k, ktb), (v, vtb) = qkv["Q"][sl], qkv["K"][sl], qkv["V"][sl]
            nm = "qkA_" + g
            for c0 in range(0, S, 2048):
                fw.dma("sp", q[:, c0:c0 + 2048], D[nm][h, :, c0:c0 + 2048], qtb, reads=[self.dtb[nm]], writes=[qtb] if c0 == 0 else (), accw=() if c0 == 0 else [qtb])
                fw.dma("sp", k[:, c0:c0 + 2048], D[nm][4 + h, :, c0:c0 + 2048], ktb, reads=[self.dtb[nm]], writes=[ktb] if c0 == 0 else (), accw=() if c0 == 0 else [ktb])
            vn = "vA_" + g
            for c0 in range(0, S // 128, 8):
                fw.dma("sp", v[:, c0:c0 + 8, :],
                       D[vn][c0 * 128:(c0 + 8) * 128, h * 128:(h + 1) * 128].rearrange("(c p) e -> p c e", p=128),
                       vtb, reads=[self.dtb[vn]], writes=[vtb] if c0 == 0 else (), accw=() if c0 == 0 else [vtb])

        def do_head(idx, g, S, h):
            sl = idx % 2
            (q, qtb), (k, ktb), (v, vtb) = qkv["Q"][sl], qkv["K"][sl], qkv["V"][sl]
            nq, nk = S // 512, S // 128
            units = [(qc, kc) for qc in range(nq) for kc in range(nk)]
            pend = {}

            def stage_a(u):
                qc, kc = units[u]
                q0, k0 = qc * 512, kc * 128
                slot = u % 2
                ba, bb = 2 * slot, 2 * slot + 1
                pd = self.ps2[slot]

                def do_qk(e):
                    e.matmul(ps[ba][:], k[0:64, k0:k0 + 128], q[0:64, q0:q0 + 512], start=True, stop=True)
                    return e.matmul(ps[bb][:], k[64:128, k0:k0 + 128], q[64:128, q0:q0 + 512], start=True, stop=True)
                fw.op("pe", do_qk, reads=[qtb, ktb], writes=[pstb[ba], pstb[bb]])
                delta = k0 - q0
                (pp, pptb) = P[cnt["P"] % 6]
                cnt["P"] += 1
                if -256 < delta < 640:
                    (tt, tttb) = T[cnt["T"] % 2]
                    cnt["T"] += 1
                    st0 = 512 - delta

                    def do_add(e):
                        e.tensor_tensor(tt[:, 0:512], ps[ba][:], self.strip[:, h, st0:st0 + 512], ALU.add)
                        return e.tensor_tensor(tt[:, 512:1024], ps[bb][:], self.strip[:, h, st0:st0 + 512], ALU.add)
                    fw.op("dve", do_add, reads=[pstb[ba], pstb[bb], self.strip_tb], writes=[tttb])
                    fw.op("act", lambda e: e.activation(pp[:], tt[:], AF.Exp), reads=[tttb], writes=[pptb])
                else:
                    ci = 2 * h + (0 if delta < 0 else 1)
                    fw.op("act", lambda e: e.activation(pp[:], pd[:], AF.Exp, bias=self.cb[:, ci:ci + 1]),
                          reads=[pstb[ba], pstb[bb], self.cb_tb], writes=[pptb])
                pend[u] = (pp, pptb)

            def stage_b(u):
                qc, kc1 = units[u]
                kc0 = kc1 - 1
                (p1, p1tb) = pend.pop(u)
                (p0, p0tb) = pend.pop(u - 1)
                first, lastk = (kc0 == 0), (kc1 == nk - 1)
                kc = kc1

                def do_av(e):
                    e.matmul(ps[4][:], v[:, kc0, :], p0[:, 0:512], start=first, stop=False)
                    e.matmul(ps[5][:], v[:, kc0, :], p0[:, 512:1024], start=first, stop=False)
                    e.matmul(ps[4][:], v[:, kc1, :], p1[:, 0:512], start=False, stop=lastk)
                    e.matmul(ps[5][:], v[:, kc1, :], p1[:, 512:1024], start=False, stop=lastk)
                    e.matmul(ps[6][0:64, :], self.ones_b[:, 0:64], p0[:, 0:512], start=first, stop=False)
                    e.matmul(ps[6][64:128, :], self.ones_b[:, 0:64], p0[:, 512:1024], start=first, stop=False)
                    e.matmul(ps[6][0:64, :], self.ones_b[:, 0:64], p1[:, 0:512], start=False, stop=lastk)
                    return e.matmul(ps[6][64:128, :], self.ones_b[:, 0:64], p1[:, 512:1024], start=False, stop=lastk)
                fw.op("pe", do_av, reads=[vtb, p0tb, p1tb, self.ones_b_tb],
                      writes=[pstb[4], pstb[5], pstb[6]] if first else (),
                      accw=() if first else [pstb[4], pstb[5], pstb[6]])
                if lastk:
                    i = cnt["ep"] % 2
                    cnt["ep"] += 1
                    (cO1, cO1tb), (cL, cLtb), (cO2, cO2tb) = ep["r1"][i], ep["o1"][i], ep["r2"][i]
                    (os_, ostb) = ost[cnt["ost"] % 2]
                    cnt["ost"] += 1
                    fw.op("dve", lambda e: e.tensor_copy(cO1[:], ps[4][:]), reads=[pstb[4]], writes=[cO1tb])
                    fw.op("act", lambda e: e.copy(cL[:], ps[6][:]), reads=[pstb[6]], writes=[cLtb])
                    fw.op("dve", lambda e: e.tensor_copy(cO2[:], ps[5][:]), reads=[pstb[5]], writes=[cO2tb])
                    fw.op("dve", lambda e: e.reciprocal(cL[:], cL[:]), reads=[cLtb], writes=[cLtb])

                    def part2():
                        fw.op("pe", lambda e: e.matmul(ps[7][:], sel[:, 0, :], cL[:], start=True, stop=True),
                              reads=[seltb, cLtb], writes=[pstb[7]])
                        fw.op("dve", lambda e: e.tensor_tensor(cO1[:], cO1[:], ps[7][:], ALU.mult),
                              reads=[cO1tb, pstb[7]], writes=[cO1tb])
                        fw.op("pe", lambda e: e.matmul(ps[7][:], sel[:, 1, :], cL[:], start=True, stop=True),
                              reads=[seltb, cLtb], writes=[pstb[7]])
                        fw.op("dve", lambda e: e.tensor_tensor(cO2[:], cO2[:], ps[7][:], ALU.mult),
                              reads=[cO2tb, pstb[7]], writes=[cO2tb])
                        fw.op("dve", lambda e: e.scalar_tensor_tensor(os_[:], cO2[:], neg_lam[:, 0:1], cO1[:],
                                                                      ALU.mult, ALU.add),
                              reads=[cO2tb, cO1tb, neg_lam_tb], writes=[ostb])
                        fw.dma("pool", D["mixT_" + g][h, :, qc * 512:(qc + 1) * 512], os_[:], ostb, reads=[ostb],
                               accw=[self.dtb["mixT_" + g]])
                    deferred.append([u + 8, part2])

            deferred = []
            N = len(units)
            assert N % 2 == 0 and nk % 2 == 0
            for i in range(0, N + 2, 2):
                if i < N:
                    stage_a(i)
                    stage_a(i + 1)
                if i >= 2:
                    stage_b(i - 1)
                    while deferred and deferred[0][0] <= i - 1:
                        deferred.pop(0)[1]()
            while deferred:
                deferred.pop(0)[1]()

        load_head(0)
        for idx, (g, S, h) in enumerate(heads):
            if idx + 1 < len(heads):
                load_head(idx + 1)
            do_head(idx, g, S, h)
        self.end_phase(es)

    def phase_attn_bc(self, l):
        nc, fw = self.nc, self.fw
        es = self.begin_phase()
        D = self.dram
        ps, pstb, ps2 = self.ps, self.pstb, self.ps2
        sna, snatb = self.sb(es, "sna", [32, 64, 128], F32, dma=True)
        fw.dma("sp", sna[:], D["c_sna"][:, :, :], snatb, reads=[self.dtb["c_sna"]], writes=[snatb])
        rm, rmtb = self.sb(es, "rm", [128, 2, 8, 8], F32, dma=True)
        fw.dma("sp", rm[:], D["c_rm"][:, :].rearrange("p (e j r) -> p e j r", e=2, j=8), rmtb,
               reads=[self.dtb["c_rm"]], writes=[rmtb])
        text, texttb = self.sb(es, "text", [32, 4, 15], F32, dma=True)
        fw.op("dve", lambda e: e.memset(text[:], 1.0), writes=[texttb])
        for ei in range(15):
            fw.dma("sp", text[0:31, :, ei], D["na_bias"][l, :, 14 - ei, :].rearrange("h d -> d h"), texttb,
                   reads=[self.dtb["na_bias"], texttb], accw=[texttb], allow_slow_non_contiguous=True)
        zz, zztb = self.sb(es, "zz", [128, 60, 64], F32)
        for r in range(8):
            pi = r % 2

            def do_mm(e, r=r, pi=pi):
                last = None
                for cl in range(8):
                    c = r * 8 + cl
                    last = e.matmul(ps[pi][:, cl * 60:(cl + 1) * 60], sna[:, c, :],
                                    text[:].rearrange("d h e -> d (h e)"), start=True, stop=True)
                return last
            fw.op("pe", do_mm, reads=[snatb, texttb], writes=[pstb[pi]])
            fw.op("dve", lambda e, r=r, pi=pi: e.tensor_copy(
                zz[:, :, r * 8:(r + 1) * 8], ps[pi][:, 0:480].rearrange("p (c x) -> p x c", c=8)),
                reads=[pstb[pi]], accw=[zztb])
        wall, walltb = self.sb(es, "wall", [128, 4, 22, 64], F32)
        wint, winttb = self.sb(es, "wint", [128, 4, 22, 64], F32)
        fw.op("pool", lambda e: e.memset(wall[:], NEG), writes=[walltb])
        fw.op("pool", lambda e: e.memset(wint[:], NEG), writes=[winttb])
        zv = zz[:].rearrange("p (h e) c -> p h e c", h=4)
        fw.op("dve", lambda e: e.tensor_copy(wall[0:64, :, 3:18, :], zv[0:64, :, :, :]), reads=[zztb], writes=[walltb])
        fw.op("dve", lambda e: e.tensor_copy(wall[64:128, :, 4:19, :], zv[64:128, :, :, :]), reads=[zztb], accw=[walltb])
        fw.op("dve", lambda e: e.tensor_copy(wint[0:64, :, 7:15, :], zv[0:64, :, 4:12, :]), reads=[zztb], writes=[winttb])
        fw.op("dve", lambda e: e.tensor_copy(wint[64:128, :, 8:16, :], zv[64:128, :, 4:12, :]), reads=[zztb], accw=[winttb])

        qb = [self.sb(es, "qb%d" % i, [128, 512], BF16, dma=True) for i in range(3)]
        kb = [self.sb(es, "kb%d" % i, [128, 1024], BF16, dma=True) for i in range(3)]
        vb = [self.sb(es, "vb%d" % i, [128, 8, 128], BF16, dma=True) for i in range(3)]
        kmg = {g: self.sb(es, "km_" + g, [128, 2, MEM], BF16, dma=True) for (g, _) in self.groups}
        vmg = {g: self.sb(es, "vm_" + g, [128, 2, 256], BF16, dma=True) for (g, _) in self.groups}
        P = [self.sb(es, "P%d" % i, [128, 1024], BF16) for i in range(4)]
        T = [self.sb(es, "T%d" % i, [128, 1024], F32) for i in range(3)]
        rr = [self.sb(es, "rr%d" % i, [128, 512], F32) for i in range(2)]
        ost = [self.sb(es, "ost%d" % i, [128, 512], BF16, dma=True) for i in range(2)]
        cnt = {"P": 0, "T": 0, "job": 0, "u": 0}

        pend = {}

        def stage_a(u):
            q, qtb = u["q"], u["qtb"]
            slot = cnt["u"] % 2
            cnt["u"] += 1
            ba, bb = 2 * slot, 2 * slot + 1

            def do_qk(e):
                e.matmul(ps[ba][:], u["kA"], q[0:64, :], start=True, stop=True)
                return e.matmul(ps[bb][:], u["kB"], q[64:128, :], start=True, stop=True)
            fw.op("pe", do_qk, reads=[qtb] + u["ktbs"], writes=[pstb[ba], pstb[bb]])
            (pp, pptb) = P[cnt["P"] % 4]
            cnt["P"] += 1
            if u["bias"] is not None:
                (tt, tttb) = T[cnt["T"] % 3]
                cnt["T"] += 1
                wsel, wtb, j, edge, hp = u["bias"]
                i0 = 14 - 2 * j

                def do_add(e):
                    e.tensor_tensor(tt[:, 0:512], ps[ba][:],
                                    wsel[:, 2 * hp, i0:i0 + 8, :].rearrange("p a c -> p (a c)"), ALU.add)
                    return e.tensor_tensor(tt[:, 512:1024], ps[bb][:],
                                           wsel[:, 2 * hp + 1, i0:i0 + 8, :].rearrange("p a c -> p (a c)"), ALU.add)
                fw.op("dve", do_add, reads=[pstb[ba], pstb[bb], wtb], writes=[tttb])
                if edge is not None:
                    def do_rm(e):
                        last = None
                        for hh in range(2):
                            tv = tt[:, hh * 512:(hh + 1) * 512].rearrange("p (a c) -> p a c", c=64)
                            last = e.tensor_tensor(tv, tv, rm[:, edge, j, :].unsqueeze(2).to_broadcast([128, 8, 64]),
                                                   ALU.add)
                        return last
                    fw.op("dve", do_rm, reads=[tttb, rmtb], writes=[tttb])
                fw.op("act", lambda e: e.activation(pp[:], tt[:], AF.Exp), reads=[tttb], writes=[pptb])
            else:
                fw.op("act", lambda e: e.activation(pp[:], ps2[slot][:], AF.Exp),
                      reads=[pstb[ba], pstb[bb]], writes=[pptb])
            pend[id(u)] = (pp, pptb)

        def stage_b(u):
            (pp, pptb) = pend.pop(id(u))
            jobi, first, lastu = u["job"], u["first"], u["last"]
            ob, lb = 4 + jobi % 2, 6 + jobi % 2

            def do_av(e):
                e.matmul(ps[ob][0:64, :], u["vA"], pp[:, 0:512], start=first, stop=lastu)
                e.matmul(ps[ob][64:128, :], u["vB"], pp[:, 512:1024], start=first, stop=lastu)
                e.matmul(ps[lb][0:64, :], self.ones_b[:, 0:64], pp[:, 0:512], start=first, stop=lastu)
                return e.matmul(ps[lb][64:128, :], self.ones_b[:, 0:64], pp[:, 512:1024], start=first, stop=lastu)
            fw.op("pe", do_av, reads=u["vtbs"] + [pptb, self.ones_b_tb],
                  writes=[pstb[ob], pstb[lb]] if first else (), accw=() if first else [pstb[ob], pstb[lb]])
            if lastu:
                (r_, rtb) = rr[jobi % 2]
                (os_, ostb) = ost[jobi % 2]
                g, q0, dest_idx = u["g"], u["q0"], u["dest"]
                fw.op("act", lambda e: e.activation(r_[:], ps[lb][:], AF.Ln), reads=[pstb[lb]], writes=[rtb])
                fw.op("act", lambda e: e.activation(r_[:], r_[:], AF.Exp, scale=-1.0), reads=[rtb], writes=[rtb])
                fw.op("dve", lambda e: e.tensor_tensor(os_[:], ps[ob][:], r_[:], ALU.mult),
                      reads=[pstb[ob], rtb], writes=[ostb])
                fw.dma("pool", D["mixT_" + g][dest_idx, :, q0:q0 + 512], os_[:], ostb, reads=[ostb],
                       accw=[self.dtb["mixT_" + g]])

        jobs = []
        for (g, S) in self.groups:
            R = S // GRID_W
            for qt in range(S // 512):
                for hp in range(2):
                    jobs.append(("B", g, S, qt, hp))
            for qt in range(S // 512):
                for hp in range(2):
                    jobs.append(("C", g, S, qt, hp))

        def load_job(ji):
            kind, g, S, qt, hp = jobs[ji]
            sl = ji % 3
            q0 = qt * 512
            (q, qtb) = qb[sl]
            (km, kmtb), (vm, vmtb) = kmg[g], vmg[g]
            if kind == "B":
                fw.dma("sp", q[:], D["qkB_" + g][hp, :, q0:q0 + 512], qtb, reads=[self.dtb["qkB_" + g]], writes=[qtb])
                ks = q0 - 256
                lo, hi = max(ks, 0), min(ks + 1024, S)
                (k, ktb), (v, vtb) = kb[sl], vb[sl]
                fw.dma("sp", k[:, lo - ks:hi - ks], D["qkB_" + g][2 + hp, :, lo:hi], ktb,
                       reads=[self.dtb["qkB_" + g]], writes=[ktb])
                fw.dma("sp", v[:, (lo - ks) // 128:(hi - ks) // 128, :],
                       D["vB_" + g][lo:hi, hp * 128:(hp + 1) * 128].rearrange("(j p) e -> p j e", p=128), vtb,
                       reads=[self.dtb["vB_" + g]], writes=[vtb])
            else:
                fw.dma("sp", q[:], D["qC_" + g][hp, :, q0:q0 + 512], qtb, reads=[self.dtb["qC_" + g]], writes=[qtb])
                if qt == 0 and hp == 0:
                    fw.dma("sp", km[:], D["kmT_" + g][:, :, :].rearrange("c p m -> p c m"), kmtb,
                           reads=[self.dtb["kmT_" + g]], writes=[kmtb])
                    fw.dma("sp", vm[:], D["vm_" + g][:, :].rearrange("(c p) e -> p c e", p=128), vmtb,
                           reads=[self.dtb["vm_" + g]], writes=[vmtb])

        def job_units(ji):
            kind, g, S, qt, hp = jobs[ji]
            sl = ji % 3
            q0 = qt * 512
            (q, qtb) = qb[sl]
            (km, kmtb), (vm, vmtb) = kmg[g], vmg[g]
            units = []
            if kind == "B":
                (k, ktb), (v, vtb) = kb[sl], vb[sl]
                ks = q0 - 256
                nt = S // 512
                edge = 0 if qt == 0 else (1 if qt == nt - 1 else None)
                wsel, wtb = (wint, winttb) if edge is None else (wall, walltb)
                for j in range(8):
                    if ks + j * 128 < 0 or ks + j * 128 >= S:
                        continue
                    units.append(dict(kA=k[0:64, j * 128:(j + 1) * 128], kB=k[64:128, j * 128:(j + 1) * 128], ktbs=[ktb],
                                      vA=v[:, j, 0:64], vB=v[:, j, 64:128], vtbs=[vtb],
                                      bias=(wsel, wtb, j, edge, hp), dest=4 + hp))
            else:
                for mc in range(2):
                    units.append(dict(kA=km[0:64, hp, mc * 128:(mc + 1) * 128], kB=km[64:128, hp, mc * 128:(mc + 1) * 128],
                                      ktbs=[kmtb], vA=vm[:, mc, (2 * hp) * 64:(2 * hp + 1) * 64],
                                      vB=vm[:, mc, (2 * hp + 1) * 64:(2 * hp + 2) * 64], vtbs=[vmtb], bias=None,
                                      dest=6 + hp))
            for i, u in enumerate(units):
                u.update(job=ji, g=g, q0=q0, q=q, qtb=qtb, first=(i == 0), last=(i == len(units) - 1))
            return units

        load_job(0)
        flat = []
        for ji in range(len(jobs)):
            flat.extend(job_units(ji))
        nu = len(flat)
        for i in range(nu + 2):
            if i < nu:
                u = flat[i]
                if u["first"] and u["job"] + 1 < len(jobs):
                    load_job(u["job"] + 1)
                stage_a(u)
            if i >= 2:
                stage_b(flat[i - 2])
        self.end_phase(es)

    def phase_outproj(self, l):
        nc, fw = self.nc, self.fw
        es = self.begin_phase()
        D = self.dram
        ps, pstb = self.ps, self.pstb
        lam_init = 0.8 - 0.6 * math.exp(-0.3 * l)
        src_pref = "x_" if l == 0 else "x1_"
        stage = [self.sb(es, "wst%d" % i, [128, 1024], F32, dma=True) for i in range(3)]
        wout, wouttb = self.prep_weight(es, "wout", D["w_out"][l], self.dtb["w_out"], 8, D_MODEL, None, None,
                                        [t for t, _ in stage], [tb for _, tb in stage])
        gs0, gs0tb = self.sb(es, "gs0", [128, 1], F32, dma=True)
        gsub, gsubtb = self.sb(es, "gsub", [128, 1], F32)
        fw.dma("sp", gs0[:], D["subln_g"][l, :].rearrange("(p o) -> p o", o=1), gs0tb, reads=[self.dtb["subln_g"]],
               writes=[gs0tb], allow_slow_non_contiguous=True)
        fw.op("dve", lambda e: e.tensor_scalar(gsub[:], gs0[:], 1.0 - lam_init, None, ALU.mult),
              reads=[gs0tb], writes=[gsubtb])
        mx = [self.sb(es, "mx%d" % i, [128, 8, 512], BF16, dma=True) for i in range(3)]
        xt = [self.sb(es, "xt%d" % i, [128, 4, 1024], F32, dma=True) for i in range(3)]
        sq = [self.sb(es, "sq%d" % i, [128, 512], BF16) for i in range(2)]
        lnq = [self.sb(es, "lnq%d" % i, [128, 512], F32) for i in range(2)]
        rsq = [self.sb(es, "rsq%d" % i, [128, 512], F32) for i in range(2)]
        cnt = {"n": 0, "ps": 0}
        tiles = [(g, S, t0) for (g, S) in self.groups for t0 in range(0, S, 512)]

        def load_tile(ti):
            g, S, t0 = tiles[ti]
            (m, mtb), (x, xtb) = mx[ti % 3], xt[ti % 3]
            fw.dma("sp", m[:], D["mixT_" + g][:, :, t0:t0 + 512].rearrange("c p t -> p c t"), mtb,
                   reads=[self.dtb["mixT_" + g]], writes=[mtb])
            fw.dma("sp", x[:], D[src_pref + g][t0:t0 + 512, :].rearrange("(j p) d -> p j d", p=128), xtb,
                   reads=[self.dtb[src_pref + g]], writes=[xtb])

        def norm_tile(ti):
            g, S, t0 = tiles[ti]
            (m, mtb) = mx[ti % 3]
            for c in range(4):
                i = cnt["n"] % 2
                cnt["n"] += 1
                (sq_, sqtb), (ln_, lntb), (rs_, rstb) = sq[i], lnq[i], rsq[i]
                pi = 4 + (cnt["n"] % 2)
                fw.op("act", lambda e, c=c, sq_=sq_: e.activation(sq_[:], m[:, c, :], AF.Square), reads=[mtb], writes=[sqtb])
                fw.op("pe", lambda e, sq_=sq_, pi=pi: e.matmul(ps[pi][:], self.ones_b[:], sq_[:], start=True, stop=True),
                      reads=[sqtb, self.ones_b_tb], writes=[pstb[pi]])
                self.rstd_from_ss(ps[pi][:], rs_[:], ln_[:], 1.0 / 128, [pstb[pi]], [lntb], [rstb])
                fw.op("dve", lambda e, c=c, rs_=rs_: e.scalar_tensor_tensor(m[:, c, :], m[:, c, :], gsub[:, 0:1], rs_[:],
                                                                              ALU.mult, ALU.mult),
                      reads=[mtb, rstb, gsubtb], writes=[mtb])

        def do_tile(ti):
            g, S, t0 = tiles[ti]
            (m, mtb), (x, xtb) = mx[ti % 3], xt[ti % 3]
            for j in range(4):
                for n in range(2):
                    pi = cnt["ps"] % 4
                    cnt["ps"] += 1

                    def do_mm(e, j=j, n=n, pi=pi):
                        last = None
                        for c in range(8):
                            last = e.matmul(ps[pi][:], m[:, c, j * 128:(j + 1) * 128], wout[:, c, n * 512:(n + 1) * 512],
                                            start=(c == 0), stop=(c == 7))
                        return last
                    fw.op("pe", do_mm, reads=[mtb, wouttb], writes=[pstb[pi]])
                    fw.op("dve", lambda e, j=j, n=n, pi=pi: e.tensor_tensor(
                        x[:, j, n * 512:(n + 1) * 512], ps[pi][:], x[:, j, n * 512:(n + 1) * 512], ALU.add),
                        reads=[pstb[pi], xtb], accw=[xtb])
            fw.dma("pool", D["xmid_" + g][t0:t0 + 512, :].rearrange("(j p) d -> p j d", p=128), x[:], xtb,
                   reads=[xtb], accw=[self.dtb["xmid_" + g]])

        load_tile(0)
        if len(tiles) > 1:
            load_tile(1)
        norm_tile(0)
        for ti in range(len(tiles)):
            if ti + 2 < len(tiles):
                load_tile(ti + 2)
            if ti + 1 < len(tiles):
                norm_tile(ti + 1)
            do_tile(ti)
        self.end_phase(es)

    def phase_ffn(self, l, half, last):
        nc, fw = self.nc, self.fw
        es = self.begin_phase()
        D = self.dram
        ps, pstb = self.ps, self.pstb
        HC = NFF // 2
        HW_ = HC * 128
        stage = [self.sb(es, "wst%d" % i, [128, HW_], F32, dma=True) for i in range(2)]
        stl, sttb = [t for t, _ in stage], [tb for _, tb in stage]
        g2, g2tb = self.load_gain_cols(es, "g2", D["norm2_g"][l, :], self.dtb["norm2_g"], 8)
        wupv, wupvtb = self.prep_weight(es, "wupv", D["w_up"][l], self.dtb["w_up"], 8, HW_, g2, g2tb, stl, sttb,
                                        col0=half * HW_)
        wupg, wupgtb = self.prep_weight(es, "wupg", D["w_up"][l], self.dtb["w_up"], 8, HW_, g2, g2tb, stl, sttb,
                                        col0=D_FF + half * HW_)
        wdn, wdntb = self.sb(es, "wdn", [128, HC, D_MODEL], BF16)
        for i in range(HC):
            st, stb = stage[i % 2]
            r0 = half * HW_ + i * 128
            fw.dma("sp", st[:, 0:1024], D["w_down"][l, r0:r0 + 128, :], stb, reads=[self.dtb["w_down"]], writes=[stb])
            eng = ("pool", "dve")[i % 2]
            fw.op(eng, lambda e, st=st, i=i: e.tensor_copy(wdn[:, i, :], st[:, 0:1024]), reads=[stb], accw=[wdntb])
        for part, c0 in ((0, half * HW_), (1, D_FF + half * HW_)):
            (st, stb) = stage[part]
            fw.dma("sp", st[0:3, :], D["conv_w"][l, :, c0:c0 + HW_], stb, reads=[self.dtb["conv_w"]], writes=[stb])
            fw.dma("sp", st[3:4, :], D["conv_b"][l:l + 1, c0:c0 + HW_], stb, reads=[self.dtb["conv_b"]], accw=[stb])
        cwt, cwttb = self.sb(es, "cwt", [128, 2 * HC, 4], F32)

        def do_cwt(e):
            last = None
            for f in range(2 * HC):
                st = stage[f // HC][0]
                fl = f % HC
                last = e.transpose(ps[0][:, f * 4:(f + 1) * 4], st[0:4, fl * 128:(fl + 1) * 128], self.ident_f[0:4, 0:4])
            return last
        fw.op("pe", do_cwt, reads=[sttb[0], sttb[1], self.ident_f_tb], writes=[pstb[0]])
        fw.op("dve", lambda e: e.tensor_copy(cwt[:], ps[0][:, 0:8 * HC].rearrange("p (f w) -> p f w", w=4)),
              reads=[pstb[0]], writes=[cwttb])

        xs = [self.sb(es, "xs%d" % i, [128, 4, 1024], F32, dma=True) for i in range(2)]
        og = [self.sb(es, "og%d" % i, [128, 1024], F32, dma=True) for i in range(4)]
        junk, junktb = self.sb(es, "junk", [128, 1024], BF16)
        xn = [self.sb(es, "xn%d" % i, [128, 1024], BF16) for i in range(2)]
        xnT = [self.sb(es, "xnT%d" % i, [128, 8, 512], BF16) for i in range(2)]
        hT = [self.sb(es, "hT%d" % i, [128, HC, 512], BF16) for i in range(2)]
        for (t, tb) in hT:
            fw.op("pool", lambda e, t=t: e.memset(t[:], 0.0), writes=[tb])
        av = [self.sb(es, "av%d" % i, [128, 512], F32) for i in range(3)]
        ag = [self.sb(es, "ag%d" % i, [128, 512], F32) for i in range(3)]
        sg = [self.sb(es, "sg%d" % i, [128, 512], F32) for i in range(2)]
        sm = {nm: [self.sb(es, "%s%d" % (nm, i), [128, 1], F32) for i in range(4)] for nm in ("ss", "lnv", "rstd")}
        cnt = {"c": 0, "og": 0, "dn": 0, "xn": 0, "tr": 0}
        dest_pref = "y_" if last else "x1_"
        tiles = [(g, S, t0) for (g, S) in self.groups for t0 in range(0, S, FT)]

        def rows(ti):
            g, S, t0 = tiles[ti]
            T = min(FT, S - t0)
            return g, S, t0, T, t0 - 1

        def load_tile(ti):
            g, S, t0, T, r0 = rows(ti)
            (x, xtb) = xs[ti % 2]
            lo, hi = max(r0, 0), min(r0 + 512, S)
            full = (lo == r0 and hi == r0 + 512)
            if full:
                fw.dma("sp", x[:], D["xmid_" + g][r0:r0 + 512, :].rearrange("(j p) d -> p j d", p=128), xtb,
                       reads=[self.dtb["xmid_" + g]], writes=[xtb])
                return
            fw.op("pool", lambda e: e.memset(x[:], 0.0), writes=[xtb])
            for j in range(4):
                a, b = max(r0 + 128 * j, lo), min(r0 + 128 * (j + 1), hi)
                if a >= b:
                    continue
                p0 = a - (r0 + 128 * j)
                fw.dma_rows("sp", lambda p, c, j=j: x[p:p + c, j, :],
                            lambda o, c, a=a: D["xmid_" + g][a + o:a + o + c, :], p0, b - a, xtb, True,
                            reads=[self.dtb["xmid_" + g], xtb])

        xn_stash = {}

        def norm_sub_a(ti, j):
            (x, xtb) = xs[ti % 2]
            (ss, sstb), (lnv, lnvtb), (rstd, rstdtb) = sm["ss"][j], sm["lnv"][j], sm["rstd"][j]
            self.token_rstd(x[:, j, :], xtb, junk, junktb, ss, sstb, lnv, lnvtb, rstd, rstdtb)
            (xnb, xnbtb) = xn[cnt["xn"] % 2]
            cnt["xn"] += 1
            fw.op("pool", lambda e: e.tensor_scalar(xnb[:], x[:, j, :], rstd[:, 0:1], 1.0, ALU.mult, ALU.mult),
                  reads=[xtb, rstdtb], writes=[xnbtb])
            xn_stash[(ti, j)] = (xnb, xnbtb)

        def norm_sub_b(ti, j):
            (xT, xTtb) = xnT[ti % 2]
            (xnb, xnbtb) = xn_stash.pop((ti, j))
            pi = cnt["tr"] % 2
            cnt["tr"] += 1
            ptr = ps[pi][:].bitcast(BF16)

            def do_tr(e):
                last = None
                for k in range(8):
                    last = e.transpose(ptr[:, k * 128:(k + 1) * 128], xnb[:, k * 128:(k + 1) * 128], self.ident_b[:])
                return last
            fw.op("pe", do_tr, reads=[xnbtb, self.ident_b_tb], writes=[pstb[pi]])
            fw.op("act", lambda e: e.copy(xT[:, :, j * 128:(j + 1) * 128], ptr.rearrange("p (k t) -> p k t", k=8)),
                  reads=[pstb[pi]], accw=[xTtb])

        chunk_res = {}

        def up_chunk(ti, i):
            (xT, xTtb) = xnT[ti % 2]
            res = {}
            for kind, W, Wtb, f, bufs in (("v", wupv, wupvtb, i, av), ("g", wupg, wupgtb, HC + i, ag)):
                pi = 2 + (cnt["c"] % 4)
                cnt["c"] += 1
                (a_, atb) = bufs[i % 3]

                def do_up(e, W=W, pi=pi):
                    last = None
                    for k in range(8):
                        last = e.matmul(ps[pi][:], W[:, k, i * 128:(i + 1) * 128], xT[:, k, :],
                                        start=(k == 0), stop=(k == 7))
                    return last
                fw.op("pe", do_up, reads=[xTtb, Wtb], writes=[pstb[pi]])
                fw.op("act", lambda e, a_=a_, pi=pi, f=f: e.activation(
                    a_[:, 1:511], ps[pi][:, 1:511], AF.Identity, bias=cwt[:, f, 3:4], scale=cwt[:, f, 1:2]),
                    reads=[pstb[pi], cwttb], writes=[atb])
                fw.op("dve", lambda e, a_=a_, pi=pi, f=f: e.scalar_tensor_tensor(
                    a_[:, 1:511], ps[pi][:, 0:510], cwt[:, f, 0:1], a_[:, 1:511], ALU.mult, ALU.add),
                    reads=[pstb[pi], cwttb, atb], writes=[atb])
                fw.op("dve", lambda e, a_=a_, pi=pi, f=f: e.scalar_tensor_tensor(
                    a_[:, 1:511], ps[pi][:, 2:512], cwt[:, f, 2:3], a_[:, 1:511], ALU.mult, ALU.add),
                    reads=[pstb[pi], cwttb, atb], writes=[atb])
                res[kind] = (a_, atb)
            chunk_res[(ti, i)] = res

        def finish_chunk(ti, i):
            (h_, htb) = hT[ti % 2]
            res = chunk_res.pop((ti, i))
            (s_, stb_) = sg[i % 2]
            (a_g, agtb), (a_v, avtb) = res["g"], res["v"]
            fw.op("act", lambda e: e.activation(s_[:, 1:511], a_g[:, 1:511], AF.Silu), reads=[agtb], writes=[stb_])
            fw.op("pool", lambda e: e.tensor_tensor(h_[:, i, 1:511], a_v[:, 1:511], s_[:, 1:511], ALU.mult),
                  reads=[avtb, stb_], accw=[htb])

        base_pref = "xmid_" if half == 0 else dest_pref

        def og_prefetch(ti):
            g, S, t0, T, r0 = rows(ti)
            for j in range(4):
                a, b = max(r0 + 128 * j, t0), min(r0 + 128 * (j + 1), t0 + T)
                if a >= b:
                    continue
                p0 = a - (r0 + 128 * j)
                (o_, otb) = og[j]
                fw.dma_rows("sp", lambda p, c, o_=o_: o_[p:p + c, :],
                            lambda o, c, a=a: D[base_pref + g][a + o:a + o + c, :], p0, b - a, otb, True,
                            first_is_write=True, reads=[self.dtb[base_pref + g]])

        def do_tile(ti):
            nxt = ti + 1 < len(tiles)
            for i in range(HC):
                if i == 0 and nxt:
                    load_tile(ti + 1)
                up_chunk(ti, i)
                if i >= 1:
                    finish_chunk(ti, i - 1)
                if i == 1 and ti >= 1:
                    down_tile(ti - 1)
                if i == 5 and ti >= 1:
                    store_tile(ti - 1)
                if i == 6:
                    og_prefetch(ti)
                if nxt:
                    if i in (2, 4, 6, 8):
                        norm_sub_a(ti + 1, (i - 2) // 2)
                    if i in (4, 6, 8, 10):
                        norm_sub_b(ti + 1, (i - 4) // 2)
            finish_chunk(ti, HC - 1)

        def down_tile(ti):
            g, S, t0, T, r0 = rows(ti)
            (h_, htb) = hT[ti % 2]
            for j in range(4):
                a, b = max(r0 + 128 * j, t0), min(r0 + 128 * (j + 1), t0 + T)
                if a >= b:
                    continue
                (o_, otb) = og[j]
                for n in range(2):
                    pi = 6 + (cnt["dn"] % 2)
                    cnt["dn"] += 1

                    def do_dn(e, j=j, n=n, pi=pi):
                        last = None
                        for i in range(HC):
                            last = e.matmul(ps[pi][:], h_[:, i, j * 128:(j + 1) * 128], wdn[:, i, n * 512:(n + 1) * 512],
                                            start=(i == 0), stop=(i == HC - 1))
                        return last
                    fw.op("pe", do_dn, reads=[htb, wdntb], writes=[pstb[pi]])
                    fw.op("dve", lambda e, n=n, pi=pi, o_=o_: e.tensor_tensor(
                        o_[:, n * 512:(n + 1) * 512], ps[pi][:], o_[:, n * 512:(n + 1) * 512], ALU.add),
                        reads=[pstb[pi], otb], accw=[otb])

        def store_tile(ti):
            g, S, t0, T, r0 = rows(ti)
            for j in range(4):
                a, b = max(r0 + 128 * j, t0), min(r0 + 128 * (j + 1), t0 + T)
                if a >= b:
                    continue
                p0 = a - (r0 + 128 * j)
                (o_, otb) = og[j]
                fw.dma_rows("pool", lambda p, c, o_=o_: o_[p:p + c, :],
                            lambda o, c, a=a: D[dest_pref + g][a + o:a + o + c, :], p0, b - a, otb, False,
                            reads=[otb], accw=[self.dtb[dest_pref + g]])

        load_tile(0)
        for j in range(4):
            norm_sub_a(0, j)
            norm_sub_b(0, j)
        for ti in range(len(tiles)):
            do_tile(ti)
        down_tile(len(tiles) - 1)
        store_tile(len(tiles) - 1)
        self.end_phase(es)


_W_NAMES = ("norm1_g", "qn_a", "kn_a", "lam_q1", "lam_k1", "lam_q2", "lam_k2", "subln_g", "rel_bias", "qn_b",
            "kn_b", "na_bias", "mem_g", "w_mem_kv", "qn_c", "kn_c", "w_out", "norm2_g", "w_up", "conv_w",
            "conv_b", "w_down")


def make_in_maps(inputs, n_cores, groups):
    consts = _host_consts()
    perm = _win_perm()
    w_in_p = np.ascontiguousarray(np.asarray(inputs["w_in"], np.float32)[:, :, perm])
    shared = {n: np.ascontiguousarray(np.asarray(inputs[n], np.float32)) for n in _W_NAMES}
    shared["w_in"] = w_in_p
    shared.update(consts)
    srcmap = {"p": ("x_prompt", "mem_prompt"), "s": ("x_sample", "mem_sample")}
    in_maps = []
    for c in range(n_cores):
        m = dict(shared)
        for (g, S) in groups:
            xn, mn = srcmap[g]
            m["x_" + g] = np.ascontiguousarray(np.asarray(inputs[xn][c], np.float32))
            m["mem_" + g] = np.ascontiguousarray(np.asarray(inputs[mn][c], np.float32))
        in_maps.append(m)
    return in_maps


def kernel(**inputs):
    groups = [("p", 8192), ("s", 2048)]
    prog = Prog(groups, DEPTH)
    nc = prog.build()
    in_maps = make_in_maps(inputs, 8, groups)
    res = run_bass_kernel_spmd(nc, in_maps, core_ids=list(range(8)))
    y_p = np.stack([np.asarray(r["y_p"], np.float32) for r in res.results], axis=0)
    y_s = np.stack([np.asarray(r["y_s"], np.float32) for r in res.results], axis=0)
    return (y_p, y_s)
```

```python
import math
from contextlib import ExitStack

import numpy as np
import concourse.bass as bass
import concourse.mybir as mybir
from concourse.bass_utils import run_bass_kernel_spmd

F32 = mybir.dt.float32
BF16 = mybir.dt.bfloat16
AF = mybir.ActivationFunctionType
ALU = mybir.AluOpType
AX = mybir.AxisListType

D_MODEL = 1024
DEPTH = 2
NCH = 8
HEAD_DIM = 64
IN_WIDTH = 2560
NQK = 14
VW = 768
D_FF = 2816
NFF = 22
MEM = 256
EPS = 1e-6
NEG = -30000.0
GRID_W = 64
FT = 510

ENGS = ("pe", "act", "dve", "pool", "sp")


class TB:
    __slots__ = ("name", "w", "r", "sem", "qsem")

    def __init__(self, name):
        self.name = name
        self.w = {}
        self.r = {}
        self.sem = None
        self.qsem = {}


class FW:
    def __init__(self, nc, es, n_dma_sems=88):
        self.nc = nc
        self.ops = {e: [] for e in ENGS}
        self.seq = {e: 0 for e in ENGS}
        self.seen = {e: {} for e in ENGS}
        self.need = {e: set() for e in ENGS}
        self.esem = {e: es.enter_context(nc.semaphore("s_" + e)) for e in ENGS}
        self.dsems = [es.enter_context(nc.semaphore("d%d" % i)) for i in range(n_dma_sems)]
        self.dfree = list(range(n_dma_sems))
        self.dcount = [0] * n_dma_sems

    def tb(self, name, dma=False):
        t = TB(name)
        t.sem = True if dma else None
        return t

    def release(self, tbs):
        for t in tbs:
            for q, sidx in t.qsem.items():
                self.dfree.append(sidx)
            t.qsem = {}
            t.sem = None

    def sem_for(self, t, q):
        if q not in t.qsem:
            t.qsem[q] = self.dfree.pop()
        return t.qsem[q]

    def op(self, eng, fn, reads=(), writes=(), accw=(), dsem=None):
        waits = {}

        def merge(d):
            for k, v in d.items():
                if waits.get(k, 0) < v:
                    waits[k] = v

        for t in reads:
            merge(t.w)
        for t in writes:
            merge(t.w)
            merge(t.r)
        for t in accw:
            merge(t.r)
        if dsem is not None:
            key = ("d", dsem)
            self.dcount[dsem] += 16
            val = self.dcount[dsem]
        else:
            key = ("e", eng)
            self.seq[eng] += 1
            val = self.seq[eng]
        mywaits = []
        seen = self.seen[eng]
        for k, v in waits.items():
            if eng == "pe" and k == ("e", "pe"):
                continue
            if seen.get(k, 0) >= v:
                continue
            seen[k] = v
            mywaits.append((k, v))
            if k[0] == "e":
                self.need[k[1]].add(v)
        self.ops[eng].append((fn, mywaits, key, val))
        for t in reads:
            if t.r.get(key, 0) < val:
                t.r[key] = val
        for t in writes:
            t.w = {key: val}
            t.r = {}
        for t in accw:
            if t.w.get(key, 0) < val:
                t.w[key] = val

    def dma(self, q, out, in_, semtb, reads=(), writes=(), accw=(), semkey=0, **kw):
        self.op(q, lambda e: e.dma_start(out=out, in_=in_, **kw), reads=reads, writes=writes,
                accw=accw, dsem=self.sem_for(semtb, (q, semkey)))

    def dma_rows(self, q, sb_fn, dr_fn, p0, n, semtb, to_sbuf, first_is_write=False, **kw):
        pieces = []
        n16 = (n // 16) * 16
        if n16 > 0:
            pieces.append((0, n16))
        if n - n16 > 0:
            pieces.append((n16, n - n16))
        for idx, (o, c) in enumerate(pieces):
            sb_ap, dr_ap = sb_fn(p0 + o, c), dr_fn(o, c)
            k = dict(kw)
            k["semkey"] = idx
            if to_sbuf:
                if first_is_write and idx == 0:
                    k["writes"] = list(k.get("writes", [])) + [semtb]
                else:
                    k["accw"] = list(k.get("accw", [])) + [semtb]
                self.dma(q, sb_ap, dr_ap, semtb, **k)
            else:
                self.dma(q, dr_ap, sb_ap, semtb, **k)

    def barrier(self):
        allw = {}
        for e in ENGS:
            if self.seq[e] > 0:
                allw[("e", e)] = self.seq[e]
        for i, c in enumerate(self.dcount):
            if c > 0:
                allw[("d", i)] = c
        t = TB("barrier")
        t.w = allw
        for e in ENGS:
            self.op(e, lambda eng: eng.nop(), reads=[t])

    def emit(self):
        sigidx = {}
        for e in ENGS:
            m = {}
            cnt = 0
            for i in sorted(self.need[e]):
                cnt += 1
                m[i] = cnt
            sigidx[e] = m
        esem, dsems, need = self.esem, self.dsems, self.need
        dcount = self.dcount

        def body(ename, eng, final=False):
            for (fn, waits, key, val) in self.ops[ename]:
                for (k, v) in waits:
                    if k[0] == "e":
                        eng.wait_ge(esem[k[1]], sigidx[k[1]][v])
                    else:
                        eng.wait_ge(dsems[k[1]], v)
                ins = fn(eng)
                if key[0] == "d":
                    ins.then_inc(dsems[key[1]], 16)
                elif val in need[ename]:
                    ins.then_inc(esem[ename], 1)
            if final:
                for i, c in enumerate(dcount):
                    if c > 0:
                        eng.wait_ge(dsems[i], c)
                for e in ENGS:
                    if e != ename and sigidx[e]:
                        eng.wait_ge(esem[e], len(sigidx[e]))

        with self.nc.Block() as block:
            block.tensor(lambda eng: body("pe", eng))
            block.scalar(lambda eng: body("act", eng))
            block.vector(lambda eng: body("dve", eng))
            block.gpsimd(lambda eng: body("pool", eng))
            block.sync(lambda eng: body("sp", eng, final=True))


def _t5_bucket(rp):
    half, max_exact = 16, 8
    ret = np.where(rp > 0, half, 0)
    n = np.abs(rp)
    nf = np.maximum(n, 1).astype(np.float32)
    large = max_exact + (np.log(nf / np.float32(max_exact)) / np.float32(math.log(128 / max_exact))
                         * (half - max_exact)).astype(np.int32)
    large = np.minimum(large, half - 1)
    return ret + np.where(n < max_exact, n, large)


def _host_consts():
    c = {}
    i = np.arange(1280)
    bk = _t5_bucket(i - 640)
    G = np.zeros((32, 1280), np.float32)
    G[bk, i] = 1.0
    c["c_g5"] = G
    S = np.zeros((32, 64, 128), np.float32)
    for cq in range(64):
        cs = min(max(cq - 8, 0), 48)
        for p in range(128):
            ck = p % 64
            if cs <= ck < cs + 16:
                S[ck - cq + 15, cq, p] = 1.0
            else:
                S[31, cq, p] = NEG
    c["c_sna"] = S
    RM = np.zeros((128, 2, 8, 8), np.float32)
    for p in range(128):
        rl_k = p // 64
        for j in range(8):
            for rl in range(8):
                r = rl
                rs = max(r - 4, 0)
                rk = -4 + 2 * j + rl_k
                ok = (rs <= rk < rs + 8)
                RM[p, 0, j, rl] = 0.0 if ok else NEG
                rs = min(rl - 4, 0)
                rk = -4 + 2 * j + rl_k
                ok = (rs <= rk < rs + 8) and rk < 8
                RM[p, 1, j, rl] = 0.0 if ok else NEG
    c["c_rm"] = RM.reshape(128, 128)
    c["c_ident"] = np.eye(128, dtype=np.float32)
    bo = np.zeros((128, 128), np.float32)
    bo[:64, :64] = 1.0
    bo[64:, 64:] = 1.0
    c["c_bones"] = bo
    return c


def _win_perm():
    cols = []
    for h in range(4):
        cols += list(range(h * 64, h * 64 + 64)) + list(range(256 + h * 64, 256 + h * 64 + 64))
    for h in range(4):
        cols += list(range(512 + h * 64, 512 + h * 64 + 64)) + list(range(768 + h * 64, 768 + h * 64 + 64))
    cols += list(range(1536, 1792))
    cols += list(range(1792, 2048))
    cols += list(range(2304, 2560))
    cols += list(range(1024, 1536))
    cols += list(range(2048, 2304))
    return np.array(cols, np.int64)


class Prog:
    def __init__(self, groups, depth, debug=(), stop_after=None):
        self.groups = groups
        self.depth = depth
        self.debug = set(debug)
        self.stop_after = stop_after
        self.nc = bass.Bass("TRN2", target_bir_lowering=False)
        self.es = ExitStack()
        self.fw = FW(self.nc, self.es)
        self.dram = {}
        self.dtb = {}

    def din(self, name, shape, dt=F32):
        self.dram[name] = self.nc.dram_tensor(name, list(shape), dt, kind="ExternalInput").ap()
        self.dtb[name] = TB(name)
        return self.dram[name]

    def dout(self, name, shape, dt=F32):
        self.dram[name] = self.nc.dram_tensor(name, list(shape), dt, kind="ExternalOutput").ap()
        self.dtb[name] = TB(name)
        return self.dram[name]

    def dscr(self, name, shape, dt):
        kind = "ExternalOutput" if name in self.debug else "Internal"
        self.dram[name] = self.nc.dram_tensor(name, list(shape), dt, kind=kind).ap()
        self.dtb[name] = TB(name)
        return self.dram[name]

    def sb(self, es, name, shape, dt, dma=False):
        self._uid = getattr(self, "_uid", 0) + 1
        name = "sb%d_%s" % (self._uid, name)
        t = es.enter_context(self.nc.sbuf_tensor(name, list(shape), dt))
        tb = self.fw.tb(name, dma=dma)
        self._phase_tbs.append(tb)
        return t, tb

    def begin_phase(self):
        self._phase_tbs = []
        return ExitStack()

    def end_phase(self, es):
        self.fw.barrier()
        self.fw.release(self._phase_tbs)
        es.close()

    def build(self):
        nc, fw = self.nc, self.fw
        for (g, S) in self.groups:
            self.din("x_" + g, [S, D_MODEL])
            self.din("mem_" + g, [MEM, D_MODEL])
            self.dout("y_" + g, [S, D_MODEL])
        L = self.depth
        self.din("norm1_g", [DEPTH, D_MODEL]); self.din("w_in", [DEPTH, D_MODEL, IN_WIDTH])
        for n in ("qn_a", "kn_a", "lam_q1", "lam_k1", "lam_q2", "lam_k2", "qn_b", "kn_b", "qn_c", "kn_c"):
            self.din(n, [DEPTH, 64])
        self.din("subln_g", [DEPTH, 128]); self.din("rel_bias", [32, 4])
        self.din("na_bias", [DEPTH, 4, 15, 31]); self.din("mem_g", [DEPTH, D_MODEL])
        self.din("w_mem_kv", [DEPTH, D_MODEL, 512]); self.din("w_out", [DEPTH, D_MODEL, D_MODEL])
        self.din("norm2_g", [DEPTH, D_MODEL]); self.din("w_up", [DEPTH, D_MODEL, 2 * D_FF])
        self.din("conv_w", [DEPTH, 3, 2 * D_FF]); self.din("conv_b", [DEPTH, 2 * D_FF])
        self.din("w_down", [DEPTH, D_FF, D_MODEL])
        self.din("c_g5", [32, 1280]); self.din("c_sna", [32, 64, 128]); self.din("c_rm", [128, 128])
        self.din("c_ident", [128, 128]); self.din("c_bones", [128, 128])
        for (g, S) in self.groups:
            self.dscr("qkA_" + g, [8, 128, S], BF16)
            self.dscr("vA_" + g, [S, 512], BF16)
            self.dscr("qkB_" + g, [4, 128, S], BF16)
            self.dscr("vB_" + g, [S, 256], BF16)
            self.dscr("qC_" + g, [2, 128, S], BF16)
            self.dscr("kmT_" + g, [2, 128, MEM], BF16)
            self.dscr("vm_" + g, [MEM, 256], BF16)
            self.dscr("mixT_" + g, [8, 128, S], BF16)
            self.dscr("xmid_" + g, [S, D_MODEL], F32)
            self.dscr("x1_" + g, [S, D_MODEL], F32)

        ges = self.es
        self._phase_tbs = []
        self.ps = []
        self.pstb = []
        self.ps2 = []
        for i in range(4):
            self.ps2.append(ges.enter_context(nc.psum_tensor("psd%d" % i, [128, 1024], F32)))
        for i in range(8):
            self.ps.append(self.ps2[i // 2][:, (i % 2) * 512:(i % 2 + 1) * 512])
            self.pstb.append(fw.tb("ps%d" % i))
        self.ident_f, self.ident_f_tb = self.sb(ges, "ident_f", [128, 128], F32, dma=True)
        self.ident_b, self.ident_b_tb = self.sb(ges, "ident_b", [128, 128], BF16)
        self.ones_b, self.ones_b_tb = self.sb(ges, "ones_b", [128, 128], BF16)
        self.bones_f, self.bones_f_tb = self.sb(ges, "bones_f", [128, 128], F32, dma=True)
        self.bones_b, self.bones_b_tb = self.sb(ges, "bones_b", [128, 128], BF16)
        fw.dma("sp", self.ident_f[:], self.dram["c_ident"][:, :], self.ident_f_tb,
               reads=[self.dtb["c_ident"]], writes=[self.ident_f_tb])
        fw.dma("sp", self.bones_f[:], self.dram["c_bones"][:, :], self.bones_f_tb,
               reads=[self.dtb["c_bones"]], writes=[self.bones_f_tb])
        fw.op("dve", lambda e: e.tensor_copy(self.ident_b[:], self.ident_f[:]),
              reads=[self.ident_f_tb], writes=[self.ident_b_tb])
        fw.op("dve", lambda e: e.tensor_copy(self.bones_b[:], self.bones_f[:]),
              reads=[self.bones_f_tb], writes=[self.bones_b_tb])
        fw.op("dve", lambda e: e.memset(self.ones_b[:], 1.0), writes=[self.ones_b_tb])
        self.eps_t, self.eps_tb = self.sb(ges, "eps", [128, 1], F32)
        fw.op("dve", lambda e: e.memset(self.eps_t[:], EPS), writes=[self.eps_tb])
        self.mhalf, self.mhalf_tb = self.sb(ges, "mhalf", [128, 1], F32)
        fw.op("dve", lambda e: e.memset(self.mhalf[:], -0.5), writes=[self.mhalf_tb])

        self.setup_t5()
        for l in range(L):
            last = (l == L - 1)
            self.phase_proj(l)
            if self.stop_after == ("proj", l):
                break
            self.phase_attn_a(l)
            if self.stop_after == ("attn_a", l):
                break
            self.phase_attn_bc(l)
            if self.stop_after == ("attn_bc", l):
                break
            self.phase_outproj(l)
            if self.stop_after == ("outproj", l):
                break
            self.phase_ffn(l, 0, last)
            self.phase_ffn(l, 1, last)
        fw.emit()
        self.es.close()
        return nc

    def load_gain_cols(self, es, name, src_ap, src_tb, nchunk):
        fw = self.fw
        t, tb = self.sb(es, name, [128, nchunk], F32, dma=True)
        fw.dma("sp", t[:], src_ap.rearrange("(k p) -> p k", p=128), tb, reads=[src_tb], writes=[tb],
               allow_slow_non_contiguous=True)
        return t, tb

    def prep_weight(self, es, name, w_ap, w_tb, nk, ncols, gain, gain_tb, stage, stage_tbs, col0=0,
                    engines=("pool", "dve")):
        fw = self.fw
        wt, wtb = self.sb(es, name, [128, nk, ncols], BF16)
        CW = stage[0].shape[1]
        i = 0
        for k in range(nk):
            for c0 in range(0, ncols, CW):
                cw = min(CW, ncols - c0)
                st, stb = stage[i % len(stage)], stage_tbs[i % len(stage)]
                fw.dma("sp", st[:, 0:cw], w_ap[k * 128:(k + 1) * 128, col0 + c0:col0 + c0 + cw], stb,
                       reads=[w_tb], writes=[stb])
                eng = engines[i % len(engines)]
                if gain is None:
                    fw.op(eng, (lambda e, st=st, c0=c0, cw=cw, k=k: e.tensor_copy(wt[:, k, c0:c0 + cw], st[:, 0:cw])),
                          reads=[stb], accw=[wtb])
                else:
                    fw.op(eng, (lambda e, st=st, c0=c0, cw=cw, k=k: e.tensor_scalar(
                        wt[:, k, c0:c0 + cw], st[:, 0:cw], gain[:, k:k + 1], 1.0, ALU.mult, ALU.mult)),
                        reads=[stb, gain_tb], accw=[wtb])
                i += 1
        return wt, wtb

    def token_rstd(self, x_ap, xtb, junk, junktb, ss, sstb, tmp, tmptb, rstd, rstdtb):
        fw = self.fw
        fw.op("dve", lambda e: e.scalar_tensor_tensor(junk[:], x_ap, 1.0, x_ap, ALU.mult, ALU.mult,
                                                      accum_out=ss[:, 0:1]),
              reads=[xtb], writes=[junktb, sstb])
        fw.op("pool", lambda e: e.tensor_scalar(tmp[:, 0:1], ss[:, 0:1], 1.0 / D_MODEL, EPS, ALU.mult, ALU.add),
              reads=[sstb], writes=[tmptb])
        fw.op("pool", lambda e: e.tensor_tensor(rstd[:, 0:1], tmp[:, 0:1], self.mhalf[:, 0:1], ALU.pow),
              reads=[tmptb, self.mhalf_tb], writes=[rstdtb])

    def rstd_from_ss(self, ss_ap, out_ap, tmp_ap, inv_n, tbs_r, tbs_tmp, tbs_out):
        fw = self.fw
        fw.op("act", lambda e: e.activation(tmp_ap, ss_ap, AF.Ln, bias=self.eps_t[:, 0:1], scale=inv_n),
              reads=tbs_r + [self.eps_tb], writes=tbs_tmp)
        fw.op("act", lambda e: e.activation(out_ap, tmp_ap, AF.Exp, scale=-0.5),
              reads=tbs_tmp, writes=tbs_out)

    def phase_proj(self, l):
        nc, fw = self.nc, self.fw
        es = self.begin_phase()
        D = self.dram
        src_pref = "x_" if l == 0 else "x1_"
        stage = []
        stage_tbs = []
        for i in range(3):
            t, tb = self.sb(es, "wst%d" % i, [128, 2560], F32, dma=True)
            stage.append(t); stage_tbs.append(tb)
        g1, g1tb = self.load_gain_cols(es, "g1", D["norm1_g"][l, :], self.dtb["norm1_g"], 8)
        gm, gmtb = self.load_gain_cols(es, "gm", D["mem_g"][l, :], self.dtb["mem_g"], 8)
        win, wintb = self.prep_weight(es, "win", D["w_in"][l], self.dtb["w_in"], 8, IN_WIDTH, g1, g1tb,
                                      stage, stage_tbs)
        wmem, wmemtb = self.prep_weight(es, "wmem", D["w_mem_kv"][l], self.dtb["w_mem_kv"], 8, 512, gm, gmtb,
                                        stage, stage_tbs)
        gq, gqtb = self.sb(es, "gq", [128, 16], F32, dma=True)
        plan = [("qn_a", range(0, 4)), ("kn_a", range(4, 8)), ("qn_b", range(8, 10)), ("kn_b", range(10, 12)),
                ("qn_c", range(12, 14)), ("kn_c", range(14, 16))]
        for (nm, cols) in plan:
            for half in range(2):
                src = D[nm][l, :].rearrange("(p o) -> p o", o=1)
                fw.dma("sp", gq[half * 64:(half + 1) * 64, cols[0]:cols[0] + 1], src, gqtb,
                       reads=[self.dtb[nm]], accw=[gqtb], allow_slow_non_contiguous=True)
        gq2, gq2tb = self.sb(es, "gq2", [128, 16], F32)

        def mk_gq2(e):
            last = None
            for (nm, cols) in plan:
                sc = 0.125 if nm.startswith("qn") else 1.0
                for c in cols:
                    last = e.tensor_scalar(gq2[:, c:c + 1], gq[:, cols[0]:cols[0] + 1], sc, None, ALU.mult)
            return last
        fw.op("dve", mk_gq2, reads=[gqtb], writes=[gq2tb])

        xt = []; xttb = []
        for i in range(2):
            t, tb = self.sb(es, "xt%d" % i, [128, 4, 1024], F32, dma=True)
            xt.append(t); xttb.append(tb)
        junk, junktb = self.sb(es, "junk", [128, 1024], BF16)
        xn = []; xntb = []
        for i in range(2):
            t, tb = self.sb(es, "xn%d" % i, [128, 1024], BF16)
            xn.append(t); xntb.append(tb)
        xnT = []; xnTtb = []
        for i in range(2):
            t, tb = self.sb(es, "xnT%d" % i, [128, 8, 512], BF16)
            xnT.append(t); xnTtb.append(tb)
        st = {}
        for nm, shp, dt, n in (("ss", [128, 1], F32, 4), ("lnv", [128, 1], F32, 4), ("rstd", [128, 1], F32, 4),
                               ("sq", [128, 512], BF16, 3), ("lnq", [128, 512], F32, 3), ("rsq", [128, 512], F32, 3),
                               ("zo", [128, 512], BF16, 4), ("vo", [128, 768], BF16, 3)):
            st[nm] = [self.sb(es, "%s%d" % (nm, i), shp, dt, dma=(nm in ("zo", "vo"))) for i in range(n)]
        cnt = {k: 0 for k in st}

        def nxt(nm):
            i = cnt[nm] % len(st[nm])
            cnt[nm] += 1
            return st[nm][i]

        ps, pstb = self.ps, self.pstb
        psrot = {"tr": [0, 1], "z": [2, 3, 4], "hs": [5], "v": [6, 7]}
        pcnt = {k: 0 for k in psrot}

        def pnext(role):
            i = psrot[role][pcnt[role] % len(psrot[role])]
            pcnt[role] += 1
            return i

        tiles = []
        for (g, S) in self.groups:
            tiles.append(dict(kind="mem", g=g, S=MEM, src=D["mem_" + g], srctb=self.dtb["mem_" + g], t0=0, TT=256,
                              W=wmem, Wtb=wmemtb, chunks=[(0, 14, ("kmT_" + g, 0)), (1, 15, ("kmT_" + g, 1))],
                              vparts=[(256, 256, "vm_" + g)]))
        for (g, S) in self.groups:
            chunks = [(c, c, ("qkA_" + g, c) if c < 8 else (("qkB_" + g, c - 8) if c < 12 else ("qC_" + g, c - 12)))
                      for c in range(NQK)]
            for t0 in range(0, S, 512):
                tiles.append(dict(kind="x", g=g, S=S, src=D[src_pref + g], srctb=self.dtb[src_pref + g], t0=t0, TT=512,
                                  W=win, Wtb=wintb, chunks=chunks,
                                  vparts=[(1792, 512, "vA_" + g), (2304, 256, "vB_" + g)]))
        NT = len(tiles)
        xn4 = [self.sb(es, "xnq%d" % i, [128, 1024], BF16) for i in range(2)]
        xn_stash = {}
        xcnt = {"n": 0}

        def load_x(ti):
            t = tiles[ti]
            nsub = t["TT"] // 128
            fw.dma("sp", xt[ti % 2][:, 0:nsub, :],
                   t["src"][t["t0"]:t["t0"] + t["TT"], :].rearrange("(j p) d -> p j d", p=128),
                   xttb[ti % 2], reads=[t["srctb"]], writes=[xttb[ti % 2]])

        def norm_a(ti, j):
            xb, xbtb = xt[ti % 2], xttb[ti % 2]
            (ss, sstb), (lnv, lnvtb), (rstd, rstdtb) = nxt("ss"), nxt("lnv"), nxt("rstd")
            self.token_rstd(xb[:, j, :], xbtb, junk, junktb, ss, sstb, lnv, lnvtb, rstd, rstdtb)
            (xnb, xnbtb) = xn4[xcnt["n"] % 2]
            xcnt["n"] += 1
            fw.op("pool", lambda e: e.tensor_scalar(xnb[:], xb[:, j, :], rstd[:, 0:1], 1.0, ALU.mult, ALU.mult),
                  reads=[xbtb, rstdtb], writes=[xnbtb])
            xn_stash[(ti, j)] = (xnb, xnbtb)

        def norm_b(ti, j):
            xT, xTtb = xnT[ti % 2], xnTtb[ti % 2]
            (xnb, xnbtb) = xn_stash.pop((ti, j))
            pi = pnext("tr")
            ptr = ps[pi][:].bitcast(BF16)

            def do_tr(e):
                last = None
                for k in range(8):
                    last = e.transpose(ptr[:, k * 128:(k + 1) * 128], xnb[:, k * 128:(k + 1) * 128], self.ident_b[:])
                return last
            fw.op("pe", do_tr, reads=[xnbtb, self.ident_b_tb], writes=[pstb[pi]])
            fw.op("dve", lambda e: e.tensor_copy(xT[:, :, j * 128:(j + 1) * 128], ptr.rearrange("p (k t) -> p k t", k=8)),
                  reads=[pstb[pi]], accw=[xTtb])

        def do_tile(ti):
            t = tiles[ti]
            TT, W, Wtb, chunks, vparts, t0 = t["TT"], t["W"], t["Wtb"], t["chunks"], t["vparts"], t["t0"]
            nsub = TT // 128
            xT, xTtb = xnT[ti % 2], xnTtb[ti % 2]
            sched = {}
            if ti + 1 < NT:
                nsn = tiles[ti + 1]["TT"] // 128
                nchk = len(chunks)
                if nchk >= 12:
                    for j in range(nsn):
                        sched.setdefault(1 + 3 * j, []).append(("a", j))
                        sched.setdefault(3 + 3 * j, []).append(("b", j))
                else:
                    for j in range(nsn):
                        sched.setdefault(nchk, []).append(("a", j))
                        sched.setdefault(nchk, []).append(("b", j))
            stash = {}

            def st1(ci):
                (wc, gc, (dname, didx)) = chunks[ci]
                zi = pnext("z")

                def do_z(e):
                    last = None
                    for k in range(8):
                        last = e.matmul(ps[zi][:, 0:TT], W[:, k, wc * 128:(wc + 1) * 128], xT[:, k, 0:TT],
                                        start=(k == 0), stop=(k == 7))
                    return last
                fw.op("pe", do_z, reads=[xTtb, Wtb], writes=[pstb[zi]])
                (sq, sqtb) = nxt("sq")
                fw.op("act", lambda e: e.activation(sq[:, 0:TT], ps[zi][:, 0:TT], AF.Square),
                      reads=[pstb[zi]], writes=[sqtb])
                stash[ci] = (zi, sq, sqtb)

            def st2(ci):
                (wc, gc, (dname, didx)) = chunks[ci]
                (zi, sq, sqtb) = stash.pop(ci)
                (lnq, lnqtb), (rsq, rsqtb), (zo, zotb) = nxt("lnq"), nxt("rsq"), nxt("zo")
                hi = pnext("hs")
                fw.op("pe", lambda e: e.matmul(ps[hi][:, 0:TT], self.bones_b[:], sq[:, 0:TT], start=True, stop=True),
                      reads=[sqtb, self.bones_b_tb], writes=[pstb[hi]])
                self.rstd_from_ss(ps[hi][:, 0:TT], rsq[:, 0:TT], lnq[:, 0:TT], 1.0 / 64, [pstb[hi]], [lnqtb], [rsqtb])
                fw.op("dve", lambda e: e.scalar_tensor_tensor(
                    zo[:, 0:TT], ps[zi][:, 0:TT], gq2[:, gc:gc + 1], rsq[:, 0:TT], ALU.mult, ALU.mult),
                    reads=[pstb[zi], rsqtb, gq2tb], writes=[zotb])
                fw.dma("pool", D[dname][didx, :, t0:t0 + TT], zo[:, 0:TT], zotb, reads=[zotb],
                       accw=[self.dtb[dname]])

            nchk = len(chunks)
            for ci in range(nchk + 1):
                if ci < nchk:
                    st1(ci)
                if ci >= 1:
                    st2(ci - 1)
                if ci == 1 and ti + 2 < NT:
                    load_x(ti + 2)
                for (what, j) in sched.get(ci, []):
                    if what == "a":
                        norm_a(ti + 1, j)
                    else:
                        norm_b(ti + 1, j)
            for j in range(nsub):
                (vo, votb) = nxt("vo")
                off = 0
                for (c0, cw, dname) in vparts:
                    vi = pnext("v")

                    def do_v(e, c0=c0, cw=cw, vi=vi, j=j):
                        last = None
                        for k in range(8):
                            last = e.matmul(ps[vi][:, 0:cw], xT[:, k, j * 128:(j + 1) * 128], W[:, k, c0:c0 + cw],
                                            start=(k == 0), stop=(k == 7))
                        return last
                    fw.op("pe", do_v, reads=[xTtb, Wtb], writes=[pstb[vi]])
                    if (j + (1 if off else 0)) % 2 == 0:
                        fw.op("act", lambda e, vi=vi, off=off, cw=cw, vo=vo: e.copy(vo[:, off:off + cw], ps[vi][:, 0:cw]),
                              reads=[pstb[vi]], accw=[votb])
                    else:
                        fw.op("dve", lambda e, vi=vi, off=off, cw=cw, vo=vo: e.tensor_copy(vo[:, off:off + cw], ps[vi][:, 0:cw]),
                              reads=[pstb[vi]], accw=[votb])
                    fw.dma("pool", D[dname][t0 + j * 128:t0 + (j + 1) * 128, :], vo[:, off:off + cw], votb,
                           reads=[votb], accw=[self.dtb[dname]], semkey=(1 if off else 0))
                    off += cw

        load_x(0)
        if NT > 1:
            load_x(1)
        for j in range(tiles[0]["TT"] // 128):
            norm_a(0, j)
            norm_b(0, j)
        for ti in range(NT):
            do_tile(ti)
        self.end_phase(es)

    def setup_t5(self):
        nc, fw = self.nc, self.fw
        D = self.dram
        ges = self.es
        self.strip, self.strip_tb = self.sb(ges, "strip", [128, 4, 1152], F32)
        self.cb, self.cb_tb = self.sb(ges, "cb", [128, 8], F32, dma=True)
        for h in range(4):
            for side, b in ((0, 15), (1, 31)):
                fw.dma("sp", self.cb[:, 2 * h + side:2 * h + side + 1],
                       D["rel_bias"][b:b + 1, h:h + 1].partition_broadcast(128), self.cb_tb,
                       reads=[self.dtb["rel_bias"]], accw=[self.cb_tb], allow_slow_non_contiguous=True)
        es = self.begin_phase()
        g5, g5tb = self.sb(es, "g5", [32, 1280], F32, dma=True)
        g5b, g5btb = self.sb(es, "g5b", [32, 1280], BF16)
        rb, rbtb = self.sb(es, "rb", [32, 4], F32, dma=True)
        rb3, rb3tb = self.sb(es, "rb3", [32, 12], BF16)
        rd, rdtb = self.sb(es, "rd", [32, 8], F32)
        fw.dma("sp", g5[:], D["c_g5"][:, :], g5tb, reads=[self.dtb["c_g5"]], writes=[g5tb])
        fw.dma("sp", rb[:], D["rel_bias"][:, :], rbtb, reads=[self.dtb["rel_bias"]], writes=[rbtb])
        fw.op("dve", lambda e: e.tensor_copy(g5b[:], g5[:]), reads=[g5tb], writes=[g5btb])
        fw.op("dve", lambda e: e.tensor_copy(rb3[:, 0:4], rb[:]), reads=[rbtb], accw=[rb3tb])
        fw.op("dve", lambda e: e.tensor_tensor(rd[:, 0:4], rb[:], rb3[:, 0:4], ALU.subtract),
              reads=[rbtb, rb3tb], accw=[rdtb])
        fw.op("dve", lambda e: e.tensor_copy(rb3[:, 4:8], rd[:, 0:4]), reads=[rdtb], accw=[rb3tb])
        fw.op("dve", lambda e: e.tensor_tensor(rd[:, 4:8], rd[:, 0:4], rb3[:, 4:8], ALU.subtract),
              reads=[rdtb, rb3tb], accw=[rdtb])
        fw.op("dve", lambda e: e.tensor_copy(rb3[:, 8:12], rd[:, 4:8]), reads=[rdtb], accw=[rb3tb])
        ps, pstb = self.ps, self.pstb
        sa = [self.sb(es, "t5a%d" % i, [128, 32, 4], F32) for i in range(2)]
        sbb = [self.sb(es, "t5b%d" % i, [128, 32, 4], F32) for i in range(2)]
        for r in range(36):
            pi = r % 2
            (a_, atb), (b_, btb) = sa[r % 2], sbb[r % 2]

            def do_mm(e, r=r, pi=pi):
                last = None
                for ml in range(32):
                    m = r * 32 + ml
                    last = e.matmul(ps[pi][:, ml * 12:(ml + 1) * 12], g5b[:, 1152 - m:1280 - m], rb3[:, :],
                                    start=True, stop=True)
                return last
            fw.op("pe", do_mm, reads=[g5btb, rb3tb], writes=[pstb[pi]])
            pv = ps[pi][:, 0:384].rearrange("p (m t h) -> p m t h", t=3, h=4)
            fw.op("act", lambda e, a_=a_, pv=pv: e.copy(a_[:], pv[:, :, 0, :]), reads=[pstb[pi]], writes=[atb])
            fw.op("dve", lambda e, a_=a_, b_=b_, pv=pv: e.tensor_tensor(b_[:], pv[:, :, 1, :], a_[:], ALU.add),
                  reads=[pstb[pi], atb], writes=[btb])
            fw.op("dve", lambda e, b_=b_, pv=pv, r=r: e.tensor_tensor(
                self.strip[:, :, r * 32:(r + 1) * 32].rearrange("p h m -> p m h"), pv[:, :, 2, :], b_[:], ALU.add),
                reads=[pstb[pi], btb], accw=[self.strip_tb])
        self.end_phase(es)

    def phase_attn_a(self, l):
        nc, fw = self.nc, self.fw
        es = self.begin_phase()
        D = self.dram
        ps, pstb = self.ps, self.pstb
        lam_init = 0.8 - 0.6 * math.exp(-0.3 * l)
        lv = {}
        for nm in ("lam_q1", "lam_k1", "lam_q2", "lam_k2"):
            t, tb = self.sb(es, nm, [128, 64], F32, dma=True)
            fw.dma("sp", t[:], D[nm][l:l + 1, :].partition_broadcast(128), tb, reads=[self.dtb[nm]], writes=[tb],
                   allow_slow_non_contiguous=True)
            lv[nm] = (t, tb)
        ltmp, ltmptb = self.sb(es, "ltmp", [128, 64], F32)
        lsc, lsctb = self.sb(es, "lsc", [128, 8], F32)
        neg_lam, neg_lam_tb = self.sb(es, "neg_lam", [128, 1], F32)
        for i, (a, b) in enumerate((("lam_q1", "lam_k1"), ("lam_q2", "lam_k2"))):
            fw.op("dve", lambda e, a=a, b=b: e.tensor_tensor(ltmp[:], lv[a][0][:], lv[b][0][:], ALU.mult),
                  reads=[lv[a][1], lv[b][1]], writes=[ltmptb])
            fw.op("dve", lambda e, i=i: e.tensor_reduce(lsc[:, i:i + 1], ltmp[:], AX.X, ALU.add),
                  reads=[ltmptb], accw=[lsctb])
            fw.op("act", lambda e, i=i: e.activation(lsc[:, 2 + i:3 + i], lsc[:, i:i + 1], AF.Exp),
                  reads=[lsctb], accw=[lsctb])
        fw.op("dve", lambda e: e.tensor_tensor(lsc[:, 4:5], lsc[:, 3:4], lsc[:, 2:3], ALU.subtract),
              reads=[lsctb], accw=[lsctb])
        fw.op("dve", lambda e: e.tensor_scalar(neg_lam[:], lsc[:, 4:5], -lam_init, None, ALU.add),
              reads=[lsctb], writes=[neg_lam_tb])

        sel, seltb = self.sb(es, "sel", [128, 2, 128], F32)

        fw.op("pool", lambda e: e.memset(sel[:], 0.0), writes=[seltb])
        fw.op("pool", lambda e: e.memset(sel[0:1, 0, :], 1.0), writes=[seltb])
        fw.op("pool", lambda e: e.memset(sel[64:65, 1, :], 1.0), writes=[seltb])
        qkv = {}
        Smax = max(S for (_, S) in self.groups)
        for nm, shp in (("Q", [128, Smax]), ("K", [128, Smax]), ("V", [128, Smax // 128, 128])):
            qkv[nm] = [self.sb(es, "%s%d" % (nm, i), shp, BF16, dma=True) for i in range(2)]
        P = [self.sb(es, "P%d" % i, [128, 1024], BF16) for i in range(6)]
        T = [self.sb(es, "T%d" % i, [128, 1024], F32) for i in range(2)]
        ep = {nm: [self.sb(es, "%s%d" % (nm, i), [128, 512], F32) for i in range(2)] for nm in ("r1", "o1", "r2", "t2")}
        ost = [self.sb(es, "ost%d" % i, [128, 512], BF16, dma=True) for i in range(2)]
        cnt = {"P": 0, "T": 0, "ep": 0, "ost": 0}

        heads = [(g, S, h) for (g, S) in self.groups for h in range(4)]

        def load_head(idx):
            g, S, h = heads[idx]
            sl = idx % 2
            (q, qtb), (k, ktb), (v, vtb) = qkv["Q"][sl], qkv["K"][sl], qkv["V"][sl]
            nm = "qkA_" + g
            for c0 in range(0, S, 2048):
                fw.dma("sp", q[:, c0:c0 + 2048], D[nm][h, :, c0:c0 + 2048], qtb, reads=[self.dtb[nm]], writes=[qtb] if c0 == 0 else (), accw=() if c0 == 0 else [qtb])
                fw.dma("sp", k[:, c0:c0 + 2048], D[nm][4 + h, :, c0:c0 + 2048], ktb, reads=[self.dtb[nm]], writes=[ktb] if c0 == 0 else (), accw=() if c0 == 0 else [ktb])
            vn = "vA_" + g
            for c0 in range(0, S // 128, 8):
                fw.dma("sp", v[:, c0:c0 + 8, :],
                       D[vn][c0 * 128:(c0 + 8) * 128, h * 128:(h + 1) * 128].rearrange("(c p) e -> p c e", p=128),
                       vtb, reads=[self.dtb[vn]], writes=[vtb] if c0 == 0 else (), accw=() if c0 == 0 else [vtb])

        def do_head(idx, g, S, h):
            sl = idx % 2
            (q, qtb), (k, ktb), (v, vtb) = qkv["Q"][sl], qkv["K"][sl], qkv["V"][sl]
            nq, nk = S // 512, S // 128
            units = [(qc, kc) for qc in range(nq) for kc in range(nk)]
            pend = {}

            def stage_a(u):
                qc, kc = units[u]
                q0, k0 = qc * 512, kc * 128
                slot = u % 2
                ba, bb = 2 * slot, 2 * slot + 1
                pd = self.ps2[slot]

                def do_qk(e):
                    e.matmul(ps[ba][:], k[0:64, k0:k0 + 128], q[0:64, q0:q0 + 512], start=True, stop=True)
                    return e.matmul(ps[bb][:], k[64:128, k0:k0 + 128], q[64:128, q0:q0 + 512], start=True, stop=True)
                fw.op("pe", do_qk, reads=[qtb, ktb], writes=[pstb[ba], pstb[bb]])
                delta = k0 - q0
                (pp, pptb) = P[cnt["P"] % 6]
                cnt["P"] += 1
                if -256 < delta < 640:
                    (tt, tttb) = T[cnt["T"] % 2]
                    cnt["T"] += 1
                    st0 = 512 - delta

                    def do_add(e):
                        e.tensor_tensor(tt[:, 0:512], ps[ba][:], self.strip[:, h, st0:st0 + 512], ALU.add)
                        return e.tensor_tensor(tt[:, 512:1024], ps[bb][:], self.strip[:, h, st0:st0 + 512], ALU.add)
                    fw.op("dve", do_add, reads=[pstb[ba], pstb[bb], self.strip_tb], writes=[tttb])
                    fw.op("act", lambda e: e.activation(pp[:], tt[:], AF.Exp), reads=[tttb], writes=[pptb])
                else:
                    ci = 2 * h + (0 if delta < 0 else 1)
                    fw.op("act", lambda e: e.activation(pp[:], pd[:], AF.Exp, bias=self.cb[:, ci:ci + 1]),
                          reads=[pstb[ba], pstb[bb], self.cb_tb], writes=[pptb])
                pend[u] = (pp, pptb)

            def stage_b(u):
                qc, kc1 = units[u]
                kc0 = kc1 - 1
                (p1, p1tb) = pend.pop(u)
                (p0, p0tb) = pend.pop(u - 1)
                first, lastk = (kc0 == 0), (kc1 == nk - 1)
                kc = kc1

                def do_av(e):
                    e.matmul(ps[4][:], v[:, kc0, :], p0[:, 0:512], start=first, stop=False)
                    e.matmul(ps[5][:], v[:, kc0, :], p0[:, 512:1024], start=first, stop=False)
                    e.matmul(ps[4][:], v[:, kc1, :], p1[:, 0:512], start=False, stop=lastk)
                    e.matmul(ps[5][:], v[:, kc1, :], p1[:, 512:1024], start=False, stop=lastk)
                    e.matmul(ps[6][0:64, :], self.ones_b[:, 0:64], p0[:, 0:512], start=first, stop=False)
                    e.matmul(ps[6][64:128, :], self.ones_b[:, 0:64], p0[:, 512:1024], start=first, stop=False)
                    e.matmul(ps[6][0:64, :], self.ones_b[:, 0:64], p1[:, 0:512], start=False, stop=lastk)
                    return e.matmul(ps[6][64:128, :], self.ones_b[:, 0:64], p1[:, 512:1024], start=False, stop=lastk)
                fw.op("pe", do_av, reads=[vtb, p0tb, p1tb, self.ones_b_tb],
                      writes=[pstb[4], pstb[5], pstb[6]] if first else (),
                      accw=() if first else [pstb[4], pstb[5], pstb[6]])
                if lastk:
                    i = cnt["ep"] % 2
                    cnt["ep"] += 1
                    (cO1, cO1tb), (cL, cLtb), (cO2, cO2tb) = ep["r1"][i], ep["o1"][i], ep["r2"][i]
                    (os_, ostb) = ost[cnt["ost"] % 2]
                    cnt["ost"] += 1
                    fw.op("dve", lambda e: e.tensor_copy(cO1[:], ps[4][:]), reads=[pstb[4]], writes=[cO1tb])
                    fw.op("act", lambda e: e.copy(cL[:], ps[6][:]), reads=[pstb[6]], writes=[cLtb])
                    fw.op("dve", lambda e: e.tensor_copy(cO2[:], ps[5][:]), reads=[pstb[5]], writes=[cO2tb])
                    fw.op("dve", lambda e: e.reciprocal(cL[:], cL[:]), reads=[cLtb], writes=[cLtb])

                    def part2():
                        fw.op("pe", lambda e: e.matmul(ps[7][:], sel[:, 0, :], cL[:], start=True, stop=True),
                              reads=[seltb, cLtb], writes=[pstb[7]])
                        fw.op("dve", lambda e: e.tensor_tensor(cO1[:], cO1[:], ps[7][:], ALU.mult),
                              reads=[cO1tb, pstb[7]], writes=[cO1tb])
                        fw.op("pe", lambda e: e.matmul(ps[7][:], sel[:, 1, :], cL[:], start=True, stop=True),
                              reads=[seltb, cLtb], writes=[pstb[7]])
                        fw.op("dve", lambda e: e.tensor_tensor(cO2[:], cO2[:], ps[7][:], ALU.mult),
                              reads=[cO2tb, pstb[7]], writes=[cO2tb])
                        fw.op("dve", lambda e: e.scalar_tensor_tensor(os_[:], cO2[:], neg_lam[:, 0:1], cO1[:],
                                                                      ALU.mult, ALU.add),
                              reads=[cO2tb, cO1tb, neg_lam_tb], writes=[ostb])
                        fw.dma("pool", D["mixT_" + g][h, :, qc * 512:(qc + 1) * 512], os_[:], ostb, reads=[ostb],
                               accw=[self.dtb["mixT_" + g]])
                    deferred.append([u + 8, part2])

            deferred = []
            N = len(units)
            assert N % 2 == 0 and nk % 2 == 0
            for i in range(0, N + 2, 2):
                if i < N:
                    stage_a(i)
                    stage_a(i + 1)
                if i >= 2:
                    stage_b(i - 1)
                    while deferred and deferred[0][0] <= i - 1:
                        deferred.pop(0)[1]()
            while deferred:
                deferred.pop(0)[1]()

        load_head(0)
        for idx, (g, S, h) in enumerate(heads):
            if idx + 1 < len(heads):
                load_head(idx + 1)
            do_head(idx, g, S, h)
        self.end_phase(es)

    def phase_attn_bc(self, l):
        nc, fw = self.nc, self.fw
        es = self.begin_phase()
        D = self.dram
        ps, pstb, ps2 = self.ps, self.pstb, self.ps2
        sna, snatb = self.sb(es, "sna", [32, 64, 128], F32, dma=True)
        fw.dma("sp", sna[:], D["c_sna"][:, :, :], snatb, reads=[self.dtb["c_sna"]], writes=[snatb])
        rm, rmtb = self.sb(es, "rm", [128, 2, 8, 8], F32, dma=True)
        fw.dma("sp", rm[:], D["c_rm"][:, :].rearrange("p (e j r) -> p e j r", e=2, j=8), rmtb,
               reads=[self.dtb["c_rm"]], writes=[rmtb])
        text, texttb = self.sb(es, "text", [32, 4, 15], F32, dma=True)
        fw.op("dve", lambda e: e.memset(text[:], 1.0), writes=[texttb])
        for ei in range(15):
            fw.dma("sp", text[0:31, :, ei], D["na_bias"][l, :, 14 - ei, :].rearrange("h d -> d h"), texttb,
                   reads=[self.dtb["na_bias"], texttb], accw=[texttb], allow_slow_non_contiguous=True)
        zz, zztb = self.sb(es, "zz", [128, 60, 64], F32)
        for r in range(8):
            pi = r % 2

            def do_mm(e, r=r, pi=pi):
                last = None
                for cl in range(8):
                    c = r * 8 + cl
                    last = e.matmul(ps[pi][:, cl * 60:(cl + 1) * 60], sna[:, c, :],
                                    text[:].rearrange("d h e -> d (h e)"), start=True, stop=True)
                return last
            fw.op("pe", do_mm, reads=[snatb, texttb], writes=[pstb[pi]])
            fw.op("dve", lambda e, r=r, pi=pi: e.tensor_copy(
                zz[:, :, r * 8:(r + 1) * 8], ps[pi][:, 0:480].rearrange("p (c x) -> p x c", c=8)),
                reads=[pstb[pi]], accw=[zztb])
        wall, walltb = self.sb(es, "wall", [128, 4, 22, 64], F32)
        wint, winttb = self.sb(es, "wint", [128, 4, 22, 64], F32)
        fw.op("pool", lambda e: e.memset(wall[:], NEG), writes=[walltb])
        fw.op("pool", lambda e: e.memset(wint[:], NEG), writes=[winttb])
        zv = zz[:].rearrange("p (h e) c -> p h e c", h=4)
        fw.op("dve", lambda e: e.tensor_copy(wall[0:64, :, 3:18, :], zv[0:64, :, :, :]), reads=[zztb], writes=[walltb])
        fw.op("dve", lambda e: e.tensor_copy(wall[64:128, :, 4:19, :], zv[64:128, :, :, :]), reads=[zztb], accw=[walltb])
        fw.op("dve", lambda e: e.tensor_copy(wint[0:64, :, 7:15, :], zv[0:64, :, 4:12, :]), reads=[zztb], writes=[winttb])
        fw.op("dve", lambda e: e.tensor_copy(wint[64:128, :, 8:16, :], zv[64:128, :, 4:12, :]), reads=[zztb], accw=[winttb])

        qb = [self.sb(es, "qb%d" % i, [128, 512], BF16, dma=True) for i in range(3)]
        kb = [self.sb(es, "kb%d" % i, [128, 1024], BF16, dma=True) for i in range(3)]
        vb = [self.sb(es, "vb%d" % i, [128, 8, 128], BF16, dma=True) for i in range(3)]
        kmg = {g: self.sb(es, "km_" + g, [128, 2, MEM], BF16, dma=True) for (g, _) in self.groups}
        vmg = {g: self.sb(es, "vm_" + g, [128, 2, 256], BF16, dma=True) for (g, _) in self.groups}
        P = [self.sb(es, "P%d" % i, [128, 1024], BF16) for i in range(4)]
        T = [self.sb(es, "T%d" % i, [128, 1024], F32) for i in range(3)]
        rr = [self.sb(es, "rr%d" % i, [128, 512], F32) for i in range(2)]
        ost = [self.sb(es, "ost%d" % i, [128, 512], BF16, dma=True) for i in range(2)]
        cnt = {"P": 0, "T": 0, "job": 0, "u": 0}

        pend = {}

        def stage_a(u):
            q, qtb = u["q"], u["qtb"]
            slot = cnt["u"] % 2
            cnt["u"] += 1
            ba, bb = 2 * slot, 2 * slot + 1

            def do_qk(e):
                e.matmul(ps[ba][:], u["kA"], q[0:64, :], start=True, stop=True)
                return e.matmul(ps[bb][:], u["kB"], q[64:128, :], start=True, stop=True)
            fw.op("pe", do_qk, reads=[qtb] + u["ktbs"], writes=[pstb[ba], pstb[bb]])
            (pp, pptb) = P[cnt["P"] % 4]
            cnt["P"] += 1
            if u["bias"] is not None:
                (tt, tttb) = T[cnt["T"] % 3]
                cnt["T"] += 1
                wsel, wtb, j, edge, hp = u["bias"]
                i0 = 14 - 2 * j

                def do_add(e):
                    e.tensor_tensor(tt[:, 0:512], ps[ba][:],
                                    wsel[:, 2 * hp, i0:i0 + 8, :].rearrange("p a c -> p (a c)"), ALU.add)
                    return e.tensor_tensor(tt[:, 512:1024], ps[bb][:],
                                           wsel[:, 2 * hp + 1, i0:i0 + 8, :].rearrange("p a c -> p (a c)"), ALU.add)
                fw.op("dve", do_add, reads=[pstb[ba], pstb[bb], wtb], writes=[tttb])
                if edge is not None:
                    def do_rm(e):
                        last = None
                        for hh in range(2):
                            tv = tt[:, hh * 512:(hh + 1) * 512].rearrange("p (a c) -> p a c", c=64)
                            last = e.tensor_tensor(tv, tv, rm[:, edge, j, :].unsqueeze(2).to_broadcast([128, 8, 64]),
                                                   ALU.add)
                        return last
                    fw.op("dve", do_rm, reads=[tttb, rmtb], writes=[tttb])
                fw.op("act", lambda e: e.activation(pp[:], tt[:], AF.Exp), reads=[tttb], writes=[pptb])
            else:
                fw.op("act", lambda e: e.activation(pp[:], ps2[slot][:], AF.Exp),
                      reads=[pstb[ba], pstb[bb]], writes=[pptb])
            pend[id(u)] = (pp, pptb)

        def stage_b(u):
            (pp, pptb) = pend.pop(id(u))
            jobi, first, lastu = u["job"], u["first"], u["last"]
            ob, lb = 4 + jobi % 2, 6 + jobi % 2

            def do_av(e):
                e.matmul(ps[ob][0:64, :], u["vA"], pp[:, 0:512], start=first, stop=lastu)
                e.matmul(ps[ob][64:128, :], u["vB"], pp[:, 512:1024], start=first, stop=lastu)
                e.matmul(ps[lb][0:64, :], self.ones_b[:, 0:64], pp[:, 0:512], start=first, stop=lastu)
                return e.matmul(ps[lb][64:128, :], self.ones_b[:, 0:64], pp[:, 512:1024], start=first, stop=lastu)
            fw.op("pe", do_av, reads=u["vtbs"] + [pptb, self.ones_b_tb],
                  writes=[pstb[ob], pstb[lb]] if first else (), accw=() if first else [pstb[ob], pstb[lb]])
            if lastu:
                (r_, rtb) = rr[jobi % 2]
                (os_, ostb) = ost[jobi % 2]
                g, q0, dest_idx = u["g"], u["q0"], u["dest"]
                fw.op("act", lambda e: e.activation(r_[:], ps[lb][:], AF.Ln), reads=[pstb[lb]], writes=[rtb])
                fw.op("act", lambda e: e.activation(r_[:], r_[:], AF.Exp, scale=-1.0), reads=[rtb], writes=[rtb])
                fw.op("dve", lambda e: e.tensor_tensor(os_[:], ps[ob][:], r_[:], ALU.mult),
                      reads=[pstb[ob], rtb], writes=[ostb])
                fw.dma("pool", D["mixT_" + g][dest_idx, :, q0:q0 + 512], os_[:], ostb, reads=[ostb],
                       accw=[self.dtb["mixT_" + g]])

        jobs = []
        for (g, S) in self.groups:
            R = S // GRID_W
            for qt in range(S // 512):
                for hp in range(2):
                    jobs.append(("B", g, S, qt, hp))
            for qt in range(S // 512):
                for hp in range(2):
                    jobs.append(("C", g, S, qt, hp))

        def load_job(ji):
            kind, g, S, qt, hp = jobs[ji]
            sl = ji % 3
            q0 = qt * 512
            (q, qtb) = qb[sl]
            (km, kmtb), (vm, vmtb) = kmg[g], vmg[g]
            if kind == "B":
                fw.dma("sp", q[:], D["qkB_" + g][hp, :, q0:q0 + 512], qtb, reads=[self.dtb["qkB_" + g]], writes=[qtb])
                ks = q0 - 256
                lo, hi = max(ks, 0), min(ks + 1024, S)
                (k, ktb), (v, vtb) = kb[sl], vb[sl]
                fw.dma("sp", k[:, lo - ks:hi - ks], D["qkB_" + g][2 + hp, :, lo:hi], ktb,
                       reads=[self.dtb["qkB_" + g]], writes=[ktb])
                fw.dma("sp", v[:, (lo - ks) // 128:(hi - ks) // 128, :],
                       D["vB_" + g][lo:hi, hp * 128:(hp + 1) * 128].rearrange("(j p) e -> p j e", p=128), vtb,
                       reads=[self.dtb["vB_" + g]], writes=[vtb])
            else:
                fw.dma("sp", q[:], D["qC_" + g][hp, :, q0:q0 + 512], qtb, reads=[self.dtb["qC_" + g]], writes=[qtb])
                if qt == 0 and hp == 0:
                    fw.dma("sp", km[:], D["kmT_" + g][:, :, :].rearrange("c p m -> p c m"), kmtb,
                           reads=[self.dtb["kmT_" + g]], writes=[kmtb])
                    fw.dma("sp", vm[:], D["vm_" + g][:, :].rearrange("(c p) e -> p c e", p=128), vmtb,
                           reads=[self.dtb["vm_" + g]], writes=[vmtb])

        def job_units(ji):
            kind, g, S, qt, hp = jobs[ji]
            sl = ji % 3
            q0 = qt * 512
            (q, qtb) = qb[sl]
            (km, kmtb), (vm, vmtb) = kmg[g], vmg[g]
            units = []
            if kind == "B":
                (k, ktb), (v, vtb) = kb[sl], vb[sl]
                ks = q0 - 256
                nt = S // 512
                edge = 0 if qt == 0 else (1 if qt == nt - 1 else None)
                wsel, wtb = (wint, winttb) if edge is None else (wall, walltb)
                for j in range(8):
                    if ks + j * 128 < 0 or ks + j * 128 >= S:
                        continue
                    units.append(dict(kA=k[0:64, j * 128:(j + 1) * 128], kB=k[64:128, j * 128:(j + 1) * 128], ktbs=[ktb],
                                      vA=v[:, j, 0:64], vB=v[:, j, 64:128], vtbs=[vtb],
                                      bias=(wsel, wtb, j, edge, hp), dest=4 + hp))
            else:
                for mc in range(2):
                    units.append(dict(kA=km[0:64, hp, mc * 128:(mc + 1) * 128], kB=km[64:128, hp, mc * 128:(mc + 1) * 128],
                                      ktbs=[kmtb], vA=vm[:, mc, (2 * hp) * 64:(2 * hp + 1) * 64],
                                      vB=vm[:, mc, (2 * hp + 1) * 64:(2 * hp + 2) * 64], vtbs=[vmtb], bias=None,
                                      dest=6 + hp))
            for i, u in enumerate(units):
                u.update(job=ji, g=g, q0=q0, q=q, qtb=qtb, first=(i == 0), last=(i == len(units) - 1))
            return units

        load_job(0)
        flat = []
        for ji in range(len(jobs)):
            flat.extend(job_units(ji))
        nu = len(flat)
        for i in range(nu + 2):
            if i < nu:
                u = flat[i]
                if u["first"] and u["job"] + 1 < len(jobs):
                    load_job(u["job"] + 1)
                stage_a(u)
            if i >= 2:
                stage_b(flat[i - 2])
        self.end_phase(es)

    def phase_outproj(self, l):
        nc, fw = self.nc, self.fw
        es = self.begin_phase()
        D = self.dram
        ps, pstb = self.ps, self.pstb
        lam_init = 0.8 - 0.6 * math.exp(-0.3 * l)
        src_pref = "x_" if l == 0 else "x1_"
        stage = [self.sb(es, "wst%d" % i, [128, 1024], F32, dma=True) for i in range(3)]
        wout, wouttb = self.prep_weight(es, "wout", D["w_out"][l], self.dtb["w_out"], 8, D_MODEL, None, None,
                                        [t for t, _ in stage], [tb for _, tb in stage])
        gs0, gs0tb = self.sb(es, "gs0", [128, 1], F32, dma=True)
        gsub, gsubtb = self.sb(es, "gsub", [128, 1], F32)
        fw.dma("sp", gs0[:], D["subln_g"][l, :].rearrange("(p o) -> p o", o=1), gs0tb, reads=[self.dtb["subln_g"]],
               writes=[gs0tb], allow_slow_non_contiguous=True)
        fw.op("dve", lambda e: e.tensor_scalar(gsub[:], gs0[:], 1.0 - lam_init, None, ALU.mult),
              reads=[gs0tb], writes=[gsubtb])
        mx = [self.sb(es, "mx%d" % i, [128, 8, 512], BF16, dma=True) for i in range(3)]
        xt = [self.sb(es, "xt%d" % i, [128, 4, 1024], F32, dma=True) for i in range(3)]
        sq = [self.sb(es, "sq%d" % i, [128, 512], BF16) for i in range(2)]
        lnq = [self.sb(es, "lnq%d" % i, [128, 512], F32) for i in range(2)]
        rsq = [self.sb(es, "rsq%d" % i, [128, 512], F32) for i in range(2)]
        cnt = {"n": 0, "ps": 0}
        tiles = [(g, S, t0) for (g, S) in self.groups for t0 in range(0, S, 512)]

        def load_tile(ti):
            g, S, t0 = tiles[ti]
            (m, mtb), (x, xtb) = mx[ti % 3], xt[ti % 3]
            fw.dma("sp", m[:], D["mixT_" + g][:, :, t0:t0 + 512].rearrange("c p t -> p c t"), mtb,
                   reads=[self.dtb["mixT_" + g]], writes=[mtb])
            fw.dma("sp", x[:], D[src_pref + g][t0:t0 + 512, :].rearrange("(j p) d -> p j d", p=128), xtb,
                   reads=[self.dtb[src_pref + g]], writes=[xtb])

        def norm_tile(ti):
            g, S, t0 = tiles[ti]
            (m, mtb) = mx[ti % 3]
            for c in range(4):
                i = cnt["n"] % 2
                cnt["n"] += 1
                (sq_, sqtb), (ln_, lntb), (rs_, rstb) = sq[i], lnq[i], rsq[i]
                pi = 4 + (cnt["n"] % 2)
                fw.op("act", lambda e, c=c, sq_=sq_: e.activation(sq_[:], m[:, c, :], AF.Square), reads=[mtb], writes=[sqtb])
                fw.op("pe", lambda e, sq_=sq_, pi=pi: e.matmul(ps[pi][:], self.ones_b[:], sq_[:], start=True, stop=True),
                      reads=[sqtb, self.ones_b_tb], writes=[pstb[pi]])
                self.rstd_from_ss(ps[pi][:], rs_[:], ln_[:], 1.0 / 128, [pstb[pi]], [lntb], [rstb])
                fw.op("dve", lambda e, c=c, rs_=rs_: e.scalar_tensor_tensor(m[:, c, :], m[:, c, :], gsub[:, 0:1], rs_[:],
                                                                              ALU.mult, ALU.mult),
                      reads=[mtb, rstb, gsubtb], writes=[mtb])

        def do_tile(ti):
            g, S, t0 = tiles[ti]
            (m, mtb), (x, xtb) = mx[ti % 3], xt[ti % 3]
            for j in range(4):
                for n in range(2):
                    pi = cnt["ps"] % 4
                    cnt["ps"] += 1

                    def do_mm(e, j=j, n=n, pi=pi):
                        last = None
                        for c in range(8):
                            last = e.matmul(ps[pi][:], m[:, c, j * 128:(j + 1) * 128], wout[:, c, n * 512:(n + 1) * 512],
                                            start=(c == 0), stop=(c == 7))
                        return last
                    fw.op("pe", do_mm, reads=[mtb, wouttb], writes=[pstb[pi]])
                    fw.op("dve", lambda e, j=j, n=n, pi=pi: e.tensor_tensor(
                        x[:, j, n * 512:(n + 1) * 512], ps[pi][:], x[:, j, n * 512:(n + 1) * 512], ALU.add),
                        reads=[pstb[pi], xtb], accw=[xtb])
            fw.dma("pool", D["xmid_" + g][t0:t0 + 512, :].rearrange("(j p) d -> p j d", p=128), x[:], xtb,
                   reads=[xtb], accw=[self.dtb["xmid_" + g]])

        load_tile(0)
        if len(tiles) > 1:
            load_tile(1)
        norm_tile(0)
        for ti in range(len(tiles)):
            if ti + 2 < len(tiles):
                load_tile(ti + 2)
            if ti + 1 < len(tiles):
                norm_tile(ti + 1)
            do_tile(ti)
        self.end_phase(es)

    def phase_ffn(self, l, half, last):
        nc, fw = self.nc, self.fw
        es = self.begin_phase()
        D = self.dram
        ps, pstb = self.ps, self.pstb
        HC = NFF // 2
        HW_ = HC * 128
        stage = [self.sb(es, "wst%d" % i, [128, HW_], F32, dma=True) for i in range(2)]
        stl, sttb = [t for t, _ in stage], [tb for _, tb in stage]
        g2, g2tb = self.load_gain_cols(es, "g2", D["norm2_g"][l, :], self.dtb["norm2_g"], 8)
        wupv, wupvtb = self.prep_weight(es, "wupv", D["w_up"][l], self.dtb["w_up"], 8, HW_, g2, g2tb, stl, sttb,
                                        col0=half * HW_)
        wupg, wupgtb = self.prep_weight(es, "wupg", D["w_up"][l], self.dtb["w_up"], 8, HW_, g2, g2tb, stl, sttb,
                                        col0=D_FF + half * HW_)
        wdn, wdntb = self.sb(es, "wdn", [128, HC, D_MODEL], BF16)
        for i in range(HC):
            st, stb = stage[i % 2]
            r0 = half * HW_ + i * 128
            fw.dma("sp", st[:, 0:1024], D["w_down"][l, r0:r0 + 128, :], stb, reads=[self.dtb["w_down"]], writes=[stb])
            eng = ("pool", "dve")[i % 2]
            fw.op(eng, lambda e, st=st, i=i: e.tensor_copy(wdn[:, i, :], st[:, 0:1024]), reads=[stb], accw=[wdntb])
        for part, c0 in ((0, half * HW_), (1, D_FF + half * HW_)):
            (st, stb) = stage[part]
            fw.dma("sp", st[0:3, :], D["conv_w"][l, :, c0:c0 + HW_], stb, reads=[self.dtb["conv_w"]], writes=[stb])
            fw.dma("sp", st[3:4, :], D["conv_b"][l:l + 1, c0:c0 + HW_], stb, reads=[self.dtb["conv_b"]], accw=[stb])
        cwt, cwttb = self.sb(es, "cwt", [128, 2 * HC, 4], F32)

        def do_cwt(e):
            last = None
            for f in range(2 * HC):
                st = stage[f // HC][0]
                fl = f % HC
                last = e.transpose(ps[0][:, f * 4:(f + 1) * 4], st[0:4, fl * 128:(fl + 1) * 128], self.ident_f[0:4, 0:4])
            return last
        fw.op("pe", do_cwt, reads=[sttb[0], sttb[1], self.ident_f_tb], writes=[pstb[0]])
        fw.op("dve", lambda e: e.tensor_copy(cwt[:], ps[0][:, 0:8 * HC].rearrange("p (f w) -> p f w", w=4)),
              reads=[pstb[0]], writes=[cwttb])

        xs = [self.sb(es, "xs%d" % i, [128, 4, 1024], F32, dma=True) for i in range(2)]
        og = [self.sb(es, "og%d" % i, [128, 1024], F32, dma=True) for i in range(4)]
        junk, junktb = self.sb(es, "junk", [128, 1024], BF16)
        xn = [self.sb(es, "xn%d" % i, [128, 1024], BF16) for i in range(2)]
        xnT = [self.sb(es, "xnT%d" % i, [128, 8, 512], BF16) for i in range(2)]
        hT = [self.sb(es, "hT%d" % i, [128, HC, 512], BF16) for i in range(2)]
        for (t, tb) in hT:
            fw.op("pool", lambda e, t=t: e.memset(t[:], 0.0), writes=[tb])
        av = [self.sb(es, "av%d" % i, [128, 512], F32) for i in range(3)]
        ag = [self.sb(es, "ag%d" % i, [128, 512], F32) for i in range(3)]
        sg = [self.sb(es, "sg%d" % i, [128, 512], F32) for i in range(2)]
        sm = {nm: [self.sb(es, "%s%d" % (nm, i), [128, 1], F32) for i in range(4)] for nm in ("ss", "lnv", "rstd")}
        cnt = {"c": 0, "og": 0, "dn": 0, "xn": 0, "tr": 0}
        dest_pref = "y_" if last else "x1_"
        tiles = [(g, S, t0) for (g, S) in self.groups for t0 in range(0, S, FT)]

        def rows(ti):
            g, S, t0 = tiles[ti]
            T = min(FT, S - t0)
            return g, S, t0, T, t0 - 1

        def load_tile(ti):
            g, S, t0, T, r0 = rows(ti)
            (x, xtb) = xs[ti % 2]
            lo, hi = max(r0, 0), min(r0 + 512, S)
            full = (lo == r0 and hi == r0 + 512)
            if full:
                fw.dma("sp", x[:], D["xmid_" + g][r0:r0 + 512, :].rearrange("(j p) d -> p j d", p=128), xtb,
                       reads=[self.dtb["xmid_" + g]], writes=[xtb])
                return
            fw.op("pool", lambda e: e.memset(x[:], 0.0), writes=[xtb])
            for j in range(4):
                a, b = max(r0 + 128 * j, lo), min(r0 + 128 * (j + 1), hi)
                if a >= b:
                    continue
                p0 = a - (r0 + 128 * j)
                fw.dma_rows("sp", lambda p, c, j=j: x[p:p + c, j, :],
                            lambda o, c, a=a: D["xmid_" + g][a + o:a + o + c, :], p0, b - a, xtb, True,
                            reads=[self.dtb["xmid_" + g], xtb])

        xn_stash = {}

        def norm_sub_a(ti, j):
            (x, xtb) = xs[ti % 2]
            (ss, sstb), (lnv, lnvtb), (rstd, rstdtb) = sm["ss"][j], sm["lnv"][j], sm["rstd"][j]
            self.token_rstd(x[:, j, :], xtb, junk, junktb, ss, sstb, lnv, lnvtb, rstd, rstdtb)
            (xnb, xnbtb) = xn[cnt["xn"] % 2]
            cnt["xn"] += 1
            fw.op("pool", lambda e: e.tensor_scalar(xnb[:], x[:, j, :], rstd[:, 0:1], 1.0, ALU.mult, ALU.mult),
                  reads=[xtb, rstdtb], writes=[xnbtb])
            xn_stash[(ti, j)] = (xnb, xnbtb)

        def norm_sub_b(ti, j):
            (xT, xTtb) = xnT[ti % 2]
            (xnb, xnbtb) = xn_stash.pop((ti, j))
            pi = cnt["tr"] % 2
            cnt["tr"] += 1
            ptr = ps[pi][:].bitcast(BF16)

            def do_tr(e):
                last = None
                for k in range(8):
                    last = e.transpose(ptr[:, k * 128:(k + 1) * 128], xnb[:, k * 128:(k + 1) * 128], self.ident_b[:])
                return last
            fw.op("pe", do_tr, reads=[xnbtb, self.ident_b_tb], writes=[pstb[pi]])
            fw.op("act", lambda e: e.copy(xT[:, :, j * 128:(j + 1) * 128], ptr.rearrange("p (k t) -> p k t", k=8)),
                  reads=[pstb[pi]], accw=[xTtb])

        chunk_res = {}

        def up_chunk(ti, i):
            (xT, xTtb) = xnT[ti % 2]
            res = {}
            for kind, W, Wtb, f, bufs in (("v", wupv, wupvtb, i, av), ("g", wupg, wupgtb, HC + i, ag)):
                pi = 2 + (cnt["c"] % 4)
                cnt["c"] += 1
                (a_, atb) = bufs[i % 3]

                def do_up(e, W=W, pi=pi):
                    last = None
                    for k in range(8):
                        last = e.matmul(ps[pi][:], W[:, k, i * 128:(i + 1) * 128], xT[:, k, :],
                                        start=(k == 0), stop=(k == 7))
                    return last
                fw.op("pe", do_up, reads=[xTtb, Wtb], writes=[pstb[pi]])
                fw.op("act", lambda e, a_=a_, pi=pi, f=f: e.activation(
                    a_[:, 1:511], ps[pi][:, 1:511], AF.Identity, bias=cwt[:, f, 3:4], scale=cwt[:, f, 1:2]),
                    reads=[pstb[pi], cwttb], writes=[atb])
                fw.op("dve", lambda e, a_=a_, pi=pi, f=f: e.scalar_tensor_tensor(
                    a_[:, 1:511], ps[pi][:, 0:510], cwt[:, f, 0:1], a_[:, 1:511], ALU.mult, ALU.add),
                    reads=[pstb[pi], cwttb, atb], writes=[atb])
                fw.op("dve", lambda e, a_=a_, pi=pi, f=f: e.scalar_tensor_tensor(
                    a_[:, 1:511], ps[pi][:, 2:512], cwt[:, f, 2:3], a_[:, 1:511], ALU.mult, ALU.add),
                    reads=[pstb[pi], cwttb, atb], writes=[atb])
                res[kind] = (a_, atb)
            chunk_res[(ti, i)] = res

        def finish_chunk(ti, i):
            (h_, htb) = hT[ti % 2]
            res = chunk_res.pop((ti, i))
            (s_, stb_) = sg[i % 2]
            (a_g, agtb), (a_v, avtb) = res["g"], res["v"]
            fw.op("act", lambda e: e.activation(s_[:, 1:511], a_g[:, 1:511], AF.Silu), reads=[agtb], writes=[stb_])
            fw.op("pool", lambda e: e.tensor_tensor(h_[:, i, 1:511], a_v[:, 1:511], s_[:, 1:511], ALU.mult),
                  reads=[avtb, stb_], accw=[htb])

        base_pref = "xmid_" if half == 0 else dest_pref

        def og_prefetch(ti):
            g, S, t0, T, r0 = rows(ti)
            for j in range(4):
                a, b = max(r0 + 128 * j, t0), min(r0 + 128 * (j + 1), t0 + T)
                if a >= b:
                    continue
                p0 = a - (r0 + 128 * j)
                (o_, otb) = og[j]
                fw.dma_rows("sp", lambda p, c, o_=o_: o_[p:p + c, :],
                            lambda o, c, a=a: D[base_pref + g][a + o:a + o + c, :], p0, b - a, otb, True,
                            first_is_write=True, reads=[self.dtb[base_pref + g]])

        def do_tile(ti):
            nxt = ti + 1 < len(tiles)
            for i in range(HC):
                if i == 0 and nxt:
                    load_tile(ti + 1)
                up_chunk(ti, i)
                if i >= 1:
                    finish_chunk(ti, i - 1)
                if i == 1 and ti >= 1:
                    down_tile(ti - 1)
                if i == 5 and ti >= 1:
                    store_tile(ti - 1)
                if i == 6:
                    og_prefetch(ti)
                if nxt:
                    if i in (2, 4, 6, 8):
                        norm_sub_a(ti + 1, (i - 2) // 2)
                    if i in (4, 6, 8, 10):
                        norm_sub_b(ti + 1, (i - 4) // 2)
            finish_chunk(ti, HC - 1)

        def down_tile(ti):
            g, S, t0, T, r0 = rows(ti)
            (h_, htb) = hT[ti % 2]
            for j in range(4):
                a, b = max(r0 + 128 * j, t0), min(r0 + 128 * (j + 1), t0 + T)
                if a >= b:
                    continue
                (o_, otb) = og[j]
                for n in range(2):
                    pi = 6 + (cnt["dn"] % 2)
                    cnt["dn"] += 1

                    def do_dn(e, j=j, n=n, pi=pi):
                        last = None
                        for i in range(HC):
                            last = e.matmul(ps[pi][:], h_[:, i, j * 128:(j + 1) * 128], wdn[:, i, n * 512:(n + 1) * 512],
                                            start=(i == 0), stop=(i == HC - 1))
                        return last
                    fw.op("pe", do_dn, reads=[htb, wdntb], writes=[pstb[pi]])
                    fw.op("dve", lambda e, n=n, pi=pi, o_=o_: e.tensor_tensor(
                        o_[:, n * 512:(n + 1) * 512], ps[pi][:], o_[:, n * 512:(n + 1) * 512], ALU.add),
                        reads=[pstb[pi], otb], accw=[otb])

        def store_tile(ti):
            g, S, t0, T, r0 = rows(ti)
            for j in range(4):
                a, b = max(r0 + 128 * j, t0), min(r0 + 128 * (j + 1), t0 + T)
                if a >= b:
                    continue
                p0 = a - (r0 + 128 * j)
                (o_, otb) = og[j]
                fw.dma_rows("pool", lambda p, c, o_=o_: o_[p:p + c, :],
                            lambda o, c, a=a: D[dest_pref + g][a + o:a + o + c, :], p0, b - a, otb, False,
                            reads=[otb], accw=[self.dtb[dest_pref + g]])

        load_tile(0)
        for j in range(4):
            norm_sub_a(0, j)
            norm_sub_b(0, j)
        for ti in range(len(tiles)):
            do_tile(ti)
        down_tile(len(tiles) - 1)
        store_tile(len(tiles) - 1)
        self.end_phase(es)


_W_NAMES = ("norm1_g", "qn_a", "kn_a", "lam_q1", "lam_k1", "lam_q2", "lam_k2", "subln_g", "rel_bias", "qn_b",
            "kn_b", "na_bias", "mem_g", "w_mem_kv", "qn_c", "kn_c", "w_out", "norm2_g", "w_up", "conv_w",
            "conv_b", "w_down")


def make_in_maps(inputs, n_cores, groups):
    consts = _host_consts()
    perm = _win_perm()
    w_in_p = np.ascontiguousarray(np.asarray(inputs["w_in"], np.float32)[:, :, perm])
    shared = {n: np.ascontiguousarray(np.asarray(inputs[n], np.float32)) for n in _W_NAMES}
    shared["w_in"] = w_in_p
    shared.update(consts)
    srcmap = {"p": ("x_prompt", "mem_prompt"), "s": ("x_sample", "mem_sample")}
    in_maps = []
    for c in range(n_cores):
        m = dict(shared)
        for (g, S) in groups:
            xn, mn = srcmap[g]
            m["x_" + g] = np.ascontiguousarray(np.asarray(inputs[xn][c], np.float32))
            m["mem_" + g] = np.ascontiguousarray(np.asarray(inputs[mn][c], np.float32))
        in_maps.append(m)
    return in_maps


def kernel(**inputs):
    groups = [("p", 8192), ("s", 2048)]
    prog = Prog(groups, DEPTH)
    nc = prog.build()
    in_maps = make_in_maps(inputs, 8, groups)
    res = run_bass_kernel_spmd(nc, in_maps, core_ids=list(range(8)))
    y_p = np.stack([np.asarray(r["y_p"], np.float32) for r in res.results], axis=0)
    y_s = np.stack([np.asarray(r["y_s"], np.float32) for r in res.results], axis=0)
    return (y_p, y_s)
```

```python
import math
from contextlib import ExitStack

import numpy as np
import concourse.bass as bass
import concourse.mybir as mybir
from concourse.bass_utils import run_bass_kernel_spmd

F32 = mybir.dt.float32
BF16 = mybir.dt.bfloat16
AF = mybir.ActivationFunctionType
ALU = mybir.AluOpType
AX = mybir.AxisListType

D_MODEL = 1024
DEPTH = 2
NCH = 8
HEAD_DIM = 64
IN_WIDTH = 2560
NQK = 14
VW = 768
D_FF = 2816
NFF = 22
MEM = 256
EPS = 1e-6
NEG = -30000.0
GRID_W = 64
FT = 510

ENGS = ("pe", "act", "dve", "pool", "sp")


class TB:
    __slots__ = ("name", "w", "r", "sem", "qsem")

    def __init__(self, name):
        self.name = name
        self.w = {}
        self.r = {}
        self.sem = None
        self.qsem = {}


class FW:
    def __init__(self, nc, es, n_dma_sems=88):
        self.nc = nc
        self.ops = {e: [] for e in ENGS}
        self.seq = {e: 0 for e in ENGS}
        self.seen = {e: {} for e in ENGS}
        self.need = {e: set() for e in ENGS}
        self.esem = {e: es.enter_context(nc.semaphore("s_" + e)) for e in ENGS}
        self.dsems = [es.enter_context(nc.semaphore("d%d" % i)) for i in range(n_dma_sems)]
        self.dfree_q = {"sp": list(range(0, 52)), "pool": list(range(52, n_dma_sems))}
        self.dcount = [0] * n_dma_sems

    def tb(self, name, dma=False):
        t = TB(name)
        t.sem = True if dma else None
        return t

    def release(self, tbs):
        for t in tbs:
            for q, sidx in t.qsem.items():
                self.dfree_q[q[0]].append(sidx)
            t.qsem = {}
            t.sem = None

    def sem_for(self, t, q):
        if q not in t.qsem:
            t.qsem[q] = self.dfree_q[q[0]].pop()
        return t.qsem[q]

    def op(self, eng, fn, reads=(), writes=(), accw=(), dsem=None):
        waits = {}

        def merge(d):
            for k, v in d.items():
                if waits.get(k, 0) < v:
                    waits[k] = v

        for t in reads:
            merge(t.w)
        for t in writes:
            merge(t.w)
            merge(t.r)
        for t in accw:
            merge(t.r)
        if dsem is not None:
            key = ("d", dsem)
            self.dcount[dsem] += 16
            val = self.dcount[dsem]
        else:
            key = ("e", eng)
            self.seq[eng] += 1
            val = self.seq[eng]
        mywaits = []
        seen = self.seen[eng]
        for k, v in waits.items():
            if eng == "pe" and k == ("e", "pe"):
                continue
            if seen.get(k, 0) >= v:
                continue
            seen[k] = v
            mywaits.append((k, v))
            if k[0] == "e":
                self.need[k[1]].add(v)
        self.ops[eng].append((fn, mywaits, key, val))
        for t in reads:
            if t.r.get(key, 0) < val:
                t.r[key] = val
        for t in writes:
            t.w = {key: val}
            t.r = {}
        for t in accw:
            if t.w.get(key, 0) < val:
                t.w[key] = val

    def dma(self, q, out, in_, semtb, reads=(), writes=(), accw=(), semkey=0, **kw):
        self.op(q, lambda e: e.dma_start(out=out, in_=in_, **kw), reads=reads, writes=writes,
                accw=accw, dsem=self.sem_for(semtb, (q, semkey)))

    def dma_rows(self, q, sb_fn, dr_fn, p0, n, semtb, to_sbuf, first_is_write=False, **kw):
        pieces = []
        n16 = (n // 16) * 16
        if n16 > 0:
            pieces.append((0, n16))
        if n - n16 > 0:
            pieces.append((n16, n - n16))
        for idx, (o, c) in enumerate(pieces):
            sb_ap, dr_ap = sb_fn(p0 + o, c), dr_fn(o, c)
            k = dict(kw)
            k["semkey"] = idx
            if to_sbuf:
                if first_is_write and idx == 0:
                    k["writes"] = list(k.get("writes", [])) + [semtb]
                else:
                    k["accw"] = list(k.get("accw", [])) + [semtb]
                self.dma(q, sb_ap, dr_ap, semtb, **k)
            else:
                self.dma(q, dr_ap, sb_ap, semtb, **k)

    def barrier(self):
        allw = {}
        for e in ENGS:
            if self.seq[e] > 0:
                allw[("e", e)] = self.seq[e]
        for i, c in enumerate(self.dcount):
            if c > 0:
                allw[("d", i)] = c
        t = TB("barrier")
        t.w = allw
        for e in ENGS:
            self.op(e, lambda eng: eng.nop(), reads=[t])

    def emit(self):
        sigidx = {}
        for e in ENGS:
            m = {}
            cnt = 0
            for i in sorted(self.need[e]):
                cnt += 1
                m[i] = cnt
            sigidx[e] = m
        esem, dsems, need = self.esem, self.dsems, self.need
        dcount = self.dcount

        def body(ename, eng, final=False):
            for (fn, waits, key, val) in self.ops[ename]:
                for (k, v) in waits:
                    if k[0] == "e":
                        eng.wait_ge(esem[k[1]], sigidx[k[1]][v])
                    else:
                        eng.wait_ge(dsems[k[1]], v)
                ins = fn(eng)
                if key[0] == "d":
                    ins.then_inc(dsems[key[1]], 16)
                elif val in need[ename]:
                    ins.then_inc(esem[ename], 1)
            if final:
                for i, c in enumerate(dcount):
                    if c > 0:
                        eng.wait_ge(dsems[i], c)
                for e in ENGS:
                    if e != ename and sigidx[e]:
                        eng.wait_ge(esem[e], len(sigidx[e]))

        with self.nc.Block() as block:
            block.tensor(lambda eng: body("pe", eng))
            block.scalar(lambda eng: body("act", eng))
            block.vector(lambda eng: body("dve", eng))
            block.gpsimd(lambda eng: body("pool", eng))
            block.sync(lambda eng: body("sp", eng, final=True))


def _t5_bucket(rp):
    half, max_exact = 16, 8
    ret = np.where(rp > 0, half, 0)
    n = np.abs(rp)
    nf = np.maximum(n, 1).astype(np.float32)
    large = max_exact + (np.log(nf / np.float32(max_exact)) / np.float32(math.log(128 / max_exact))
                         * (half - max_exact)).astype(np.int32)
    large = np.minimum(large, half - 1)
    return ret + np.where(n < max_exact, n, large)


def _host_consts():
    c = {}
    i = np.arange(1280)
    bk = _t5_bucket(i - 640)
    G = np.zeros((32, 1280), np.float32)
    G[bk, i] = 1.0
    c["c_g5"] = G
    S = np.zeros((32, 64, 128), np.float32)
    for cq in range(64):
        cs = min(max(cq - 8, 0), 48)
        for p in range(128):
            ck = p % 64
            if cs <= ck < cs + 16:
                S[ck - cq + 15, cq, p] = 1.0
            else:
                S[31, cq, p] = NEG
    c["c_sna"] = S
    RM = np.zeros((128, 2, 8, 8), np.float32)
    for p in range(128):
        rl_k = p // 64
        for j in range(8):
            for rl in range(8):
                r = rl
                rs = max(r - 4, 0)
                rk = -4 + 2 * j + rl_k
                ok = (rs <= rk < rs + 8)
                RM[p, 0, j, rl] = 0.0 if ok else NEG
                rs = min(rl - 4, 0)
                rk = -4 + 2 * j + rl_k
                ok = (rs <= rk < rs + 8) and rk < 8
                RM[p, 1, j, rl] = 0.0 if ok else NEG
    c["c_rm"] = RM.reshape(128, 128)
    c["c_ident"] = np.eye(128, dtype=np.float32)
    bo = np.zeros((128, 128), np.float32)
    bo[:64, :64] = 1.0
    bo[64:, 64:] = 1.0
    c["c_bones"] = bo
    return c


def _win_perm():
    cols = []
    for h in range(4):
        cols += list(range(h * 64, h * 64 + 64)) + list(range(256 + h * 64, 256 + h * 64 + 64))
    for h in range(4):
        cols += list(range(512 + h * 64, 512 + h * 64 + 64)) + list(range(768 + h * 64, 768 + h * 64 + 64))
    cols += list(range(1536, 1792))
    cols += list(range(1792, 2048))
    cols += list(range(2304, 2560))
    cols += list(range(1024, 1536))
    cols += list(range(2048, 2304))
    return np.array(cols, np.int64)


class Prog:
    def __init__(self, groups, depth, debug=(), stop_after=None):
        self.groups = groups
        self.depth = depth
        self.debug = set(debug)
        self.stop_after = stop_after
        self.nc = bass.Bass("TRN2", target_bir_lowering=False)
        self.es = ExitStack()
        self.fw = FW(self.nc, self.es)
        self.dram = {}
        self.dtb = {}

    def din(self, name, shape, dt=F32):
        self.dram[name] = self.nc.dram_tensor(name, list(shape), dt, kind="ExternalInput").ap()
        self.dtb[name] = TB(name)
        return self.dram[name]

    def dout(self, name, shape, dt=F32):
        self.dram[name] = self.nc.dram_tensor(name, list(shape), dt, kind="ExternalOutput").ap()
        self.dtb[name] = TB(name)
        return self.dram[name]

    def dscr(self, name, shape, dt):
        kind = "ExternalOutput" if name in self.debug else "Internal"
        self.dram[name] = self.nc.dram_tensor(name, list(shape), dt, kind=kind).ap()
        self.dtb[name] = TB(name)
        return self.dram[name]

    def sb(self, es, name, shape, dt, dma=False):
        self._uid = getattr(self, "_uid", 0) + 1
        name = "sb%d_%s" % (self._uid, name)
        t = es.enter_context(self.nc.sbuf_tensor(name, list(shape), dt))
        tb = self.fw.tb(name, dma=dma)
        self._phase_tbs.append(tb)
        return t, tb

    def begin_phase(self):
        self._phase_tbs = []
        return ExitStack()

    def end_phase(self, es):
        self.fw.barrier()
        self.fw.release(self._phase_tbs)
        es.close()

    def build(self):
        nc, fw = self.nc, self.fw
        for (g, S) in self.groups:
            self.din("x_" + g, [S, D_MODEL])
            self.din("mem_" + g, [MEM, D_MODEL])
            self.dout("y_" + g, [S, D_MODEL])
        L = self.depth
        self.din("norm1_g", [DEPTH, D_MODEL]); self.din("w_in", [DEPTH, D_MODEL, IN_WIDTH])
        for n in ("qn_a", "kn_a", "lam_q1", "lam_k1", "lam_q2", "lam_k2", "qn_b", "kn_b", "qn_c", "kn_c"):
            self.din(n, [DEPTH, 64])
        self.din("subln_g", [DEPTH, 128]); self.din("rel_bias", [32, 4])
        self.din("na_bias", [DEPTH, 4, 15, 31]); self.din("mem_g", [DEPTH, D_MODEL])
        self.din("w_mem_kv", [DEPTH, D_MODEL, 512]); self.din("w_out", [DEPTH, D_MODEL, D_MODEL])
        self.din("norm2_g", [DEPTH, D_MODEL]); self.din("w_up", [DEPTH, D_MODEL, 2 * D_FF])
        self.din("conv_w", [DEPTH, 3, 2 * D_FF]); self.din("conv_b", [DEPTH, 2 * D_FF])
        self.din("w_down", [DEPTH, D_FF, D_MODEL])
        self.din("c_g5", [32, 1280]); self.din("c_sna", [32, 64, 128]); self.din("c_rm", [128, 128])
        self.din("c_ident", [128, 128]); self.din("c_bones", [128, 128])
        for (g, S) in self.groups:
            self.dscr("qkA_" + g, [8, 128, S], BF16)
            self.dscr("vA_" + g, [S, 512], BF16)
            self.dscr("qkB_" + g, [4, 128, S], BF16)
            self.dscr("vB_" + g, [S, 256], BF16)
            self.dscr("qC_" + g, [2, 128, S], BF16)
            self.dscr("kmT_" + g, [2, 128, MEM], BF16)
            self.dscr("vm_" + g, [MEM, 256], BF16)
            self.dscr("mixT_" + g, [8, 128, S], BF16)
            self.dscr("xmid_" + g, [S, D_MODEL], F32)
            self.dscr("x1_" + g, [S, D_MODEL], F32)

        ges = self.es
        self._phase_tbs = []
        self.ps = []
        self.pstb = []
        self.ps2 = []
        for i in range(4):
            self.ps2.append(ges.enter_context(nc.psum_tensor("psd%d" % i, [128, 1024], F32)))
        for i in range(8):
            self.ps.append(self.ps2[i // 2][:, (i % 2) * 512:(i % 2 + 1) * 512])
            self.pstb.append(fw.tb("ps%d" % i))
        self.ident_f, self.ident_f_tb = self.sb(ges, "ident_f", [128, 128], F32, dma=True)
        self.ident_b, self.ident_b_tb = self.sb(ges, "ident_b", [128, 128], BF16)
        self.ones_b, self.ones_b_tb = self.sb(ges, "ones_b", [128, 128], BF16)
        self.bones_f, self.bones_f_tb = self.sb(ges, "bones_f", [128, 128], F32, dma=True)
        self.bones_b, self.bones_b_tb = self.sb(ges, "bones_b", [128, 128], BF16)
        fw.dma("sp", self.ident_f[:], self.dram["c_ident"][:, :], self.ident_f_tb,
               reads=[self.dtb["c_ident"]], writes=[self.ident_f_tb])
        fw.dma("sp", self.bones_f[:], self.dram["c_bones"][:, :], self.bones_f_tb,
               reads=[self.dtb["c_bones"]], writes=[self.bones_f_tb])
        fw.op("dve", lambda e: e.tensor_copy(self.ident_b[:], self.ident_f[:]),
              reads=[self.ident_f_tb], writes=[self.ident_b_tb])
        fw.op("dve", lambda e: e.tensor_copy(self.bones_b[:], self.bones_f[:]),
              reads=[self.bones_f_tb], writes=[self.bones_b_tb])
        fw.op("dve", lambda e: e.memset(self.ones_b[:], 1.0), writes=[self.ones_b_tb])
        self.eps_t, self.eps_tb = self.sb(ges, "eps", [128, 1], F32)
        fw.op("dve", lambda e: e.memset(self.eps_t[:], EPS), writes=[self.eps_tb])
        self.mhalf, self.mhalf_tb = self.sb(ges, "mhalf", [128, 1], F32)
        fw.op("dve", lambda e: e.memset(self.mhalf[:], -0.5), writes=[self.mhalf_tb])

        self.setup_t5()
        for l in range(L):
            last = (l == L - 1)
            self.phase_proj(l)
            if self.stop_after == ("proj", l):
                break
            self.phase_attn_a(l)
            if self.stop_after == ("attn_a", l):
                break
            self.phase_attn_bc(l)
            if self.stop_after == ("attn_bc", l):
                break
            self.phase_outproj(l)
            if self.stop_after == ("outproj", l):
                break
            self.phase_ffn(l, 0, last)
            self.phase_ffn(l, 1, last)
        fw.emit()
        self.es.close()
        return nc

    def load_gain_cols(self, es, name, src_ap, src_tb, nchunk):
        fw = self.fw
        t, tb = self.sb(es, name, [128, nchunk], F32, dma=True)
        fw.dma("sp", t[:], src_ap.rearrange("(k p) -> p k", p=128), tb, reads=[src_tb], writes=[tb],
               allow_slow_non_contiguous=True)
        return t, tb

    def prep_weight(self, es, name, w_ap, w_tb, nk, ncols, gain, gain_tb, stage, stage_tbs, col0=0,
                    engines=("pool", "dve")):
        fw = self.fw
        wt, wtb = self.sb(es, name, [128, nk, ncols], BF16)
        CW = stage[0].shape[1]
        i = 0
        for k in range(nk):
            for c0 in range(0, ncols, CW):
                cw = min(CW, ncols - c0)
                st, stb = stage[i % len(stage)], stage_tbs[i % len(stage)]
                fw.dma("sp", st[:, 0:cw], w_ap[k * 128:(k + 1) * 128, col0 + c0:col0 + c0 + cw], stb,
                       reads=[w_tb], writes=[stb])
                eng = engines[i % len(engines)]
                if gain is None:
                    fw.op(eng, (lambda e, st=st, c0=c0, cw=cw, k=k: e.tensor_copy(wt[:, k, c0:c0 + cw], st[:, 0:cw])),
                          reads=[stb], accw=[wtb])
                else:
                    fw.op(eng, (lambda e, st=st, c0=c0, cw=cw, k=k: e.tensor_scalar(
                        wt[:, k, c0:c0 + cw], st[:, 0:cw], gain[:, k:k + 1], 1.0, ALU.mult, ALU.mult)),
                        reads=[stb, gain_tb], accw=[wtb])
                i += 1
        return wt, wtb

    def token_rstd(self, x_ap, xtb, junk, junktb, ss, sstb, tmp, tmptb, rstd, rstdtb):
        fw = self.fw
        fw.op("dve", lambda e: e.scalar_tensor_tensor(junk[:], x_ap, 1.0, x_ap, ALU.mult, ALU.mult,
                                                      accum_out=ss[:, 0:1]),
              reads=[xtb], writes=[junktb, sstb])
        fw.op("pool", lambda e: e.tensor_scalar(tmp[:, 0:1], ss[:, 0:1], 1.0 / D_MODEL, EPS, ALU.mult, ALU.add),
              reads=[sstb], writes=[tmptb])
        fw.op("pool", lambda e: e.tensor_tensor(rstd[:, 0:1], tmp[:, 0:1], self.mhalf[:, 0:1], ALU.pow),
              reads=[tmptb, self.mhalf_tb], writes=[rstdtb])

    def rstd_from_ss(self, ss_ap, out_ap, tmp_ap, inv_n, tbs_r, tbs_tmp, tbs_out):
        fw = self.fw
        fw.op("act", lambda e: e.activation(tmp_ap, ss_ap, AF.Ln, bias=self.eps_t[:, 0:1], scale=inv_n),
              reads=tbs_r + [self.eps_tb], writes=tbs_tmp)
        fw.op("act", lambda e: e.activation(out_ap, tmp_ap, AF.Exp, scale=-0.5),
              reads=tbs_tmp, writes=tbs_out)

    def phase_proj(self, l):
        nc, fw = self.nc, self.fw
        es = self.begin_phase()
        D = self.dram
        src_pref = "x_" if l == 0 else "x1_"
        stage = []
        stage_tbs = []
        for i in range(3):
            t, tb = self.sb(es, "wst%d" % i, [128, 2560], F32, dma=True)
            stage.append(t); stage_tbs.append(tb)
        g1, g1tb = self.load_gain_cols(es, "g1", D["norm1_g"][l, :], self.dtb["norm1_g"], 8)
        gm, gmtb = self.load_gain_cols(es, "gm", D["mem_g"][l, :], self.dtb["mem_g"], 8)
        win, wintb = self.prep_weight(es, "win", D["w_in"][l], self.dtb["w_in"], 8, IN_WIDTH, g1, g1tb,
                                      stage, stage_tbs)
        wmem, wmemtb = self.prep_weight(es, "wmem", D["w_mem_kv"][l], self.dtb["w_mem_kv"], 8, 512, gm, gmtb,
                                        stage, stage_tbs)
        gq, gqtb = self.sb(es, "gq", [128, 16], F32, dma=True)
        plan = [("qn_a", range(0, 4)), ("kn_a", range(4, 8)), ("qn_b", range(8, 10)), ("kn_b", range(10, 12)),
                ("qn_c", range(12, 14)), ("kn_c", range(14, 16))]
        for (nm, cols) in plan:
            for half in range(2):
                src = D[nm][l, :].rearrange("(p o) -> p o", o=1)
                fw.dma("sp", gq[half * 64:(half + 1) * 64, cols[0]:cols[0] + 1], src, gqtb,
                       reads=[self.dtb[nm]], accw=[gqtb], allow_slow_non_contiguous=True)
        gq2, gq2tb = self.sb(es, "gq2", [128, 16], F32)

        def mk_gq2(e):
            last = None
            for (nm, cols) in plan:
                sc = 0.125 if nm.startswith("qn") else 1.0
                for c in cols:
                    last = e.tensor_scalar(gq2[:, c:c + 1], gq[:, cols[0]:cols[0] + 1], sc, None, ALU.mult)
            return last
        fw.op("dve", mk_gq2, reads=[gqtb], writes=[gq2tb])

        xt = []; xttb = []
        for i in range(2):
            t, tb = self.sb(es, "xt%d" % i, [128, 4, 1024], F32, dma=True)
            xt.append(t); xttb.append(tb)
        junk, junktb = self.sb(es, "junk", [128, 1024], BF16)
        xn = []; xntb = []
        for i in range(2):
            t, tb = self.sb(es, "xn%d" % i, [128, 1024], BF16)
            xn.append(t); xntb.append(tb)
        xnT = []; xnTtb = []
        for i in range(2):
            t, tb = self.sb(es, "xnT%d" % i, [128, 8, 512], BF16)
            xnT.append(t); xnTtb.append(tb)
        st = {}
        for nm, shp, dt, n in (("ss", [128, 1], F32, 4), ("lnv", [128, 1], F32, 4), ("rstd", [128, 1], F32, 4),
                               ("sq", [128, 512], BF16, 3), ("lnq", [128, 512], F32, 3), ("rsq", [128, 512], F32, 3),
                               ("zo", [128, 512], BF16, 4), ("vo", [128, 768], BF16, 3)):
            st[nm] = [self.sb(es, "%s%d" % (nm, i), shp, dt, dma=(nm in ("zo", "vo"))) for i in range(n)]
        cnt = {k: 0 for k in st}

        def nxt(nm):
            i = cnt[nm] % len(st[nm])
            cnt[nm] += 1
            return st[nm][i]

        ps, pstb = self.ps, self.pstb
        psrot = {"tr": [0, 1], "z": [2, 3, 4], "hs": [5], "v": [6, 7]}
        pcnt = {k: 0 for k in psrot}

        def pnext(role):
            i = psrot[role][pcnt[role] % len(psrot[role])]
            pcnt[role] += 1
            return i

        tiles = []
        for (g, S) in self.groups:
            tiles.append(dict(kind="mem", g=g, S=MEM, src=D["mem_" + g], srctb=self.dtb["mem_" + g], t0=0, TT=256,
                              W=wmem, Wtb=wmemtb, chunks=[(0, 14, ("kmT_" + g, 0)), (1, 15, ("kmT_" + g, 1))],
                              vparts=[(256, 256, "vm_" + g)]))
        for (g, S) in self.groups:
            chunks = [(c, c, ("qkA_" + g, c) if c < 8 else (("qkB_" + g, c - 8) if c < 12 else ("qC_" + g, c - 12)))
                      for c in range(NQK)]
            for t0 in range(0, S, 512):
                tiles.append(dict(kind="x", g=g, S=S, src=D[src_pref + g], srctb=self.dtb[src_pref + g], t0=t0, TT=512,
                                  W=win, Wtb=wintb, chunks=chunks,
                                  vparts=[(1792, 512, "vA_" + g), (2304, 256, "vB_" + g)]))
        NT = len(tiles)
        xn4 = [self.sb(es, "xnq%d" % i, [128, 1024], BF16) for i in range(2)]
        xn_stash = {}
        xcnt = {"n": 0}

        def load_x(ti):
            t = tiles[ti]
            nsub = t["TT"] // 128
            fw.dma("sp", xt[ti % 2][:, 0:nsub, :],
                   t["src"][t["t0"]:t["t0"] + t["TT"], :].rearrange("(j p) d -> p j d", p=128),
                   xttb[ti % 2], reads=[t["srctb"]], writes=[xttb[ti % 2]])

        def norm_a(ti, j):
            xb, xbtb = xt[ti % 2], xttb[ti % 2]
            (ss, sstb), (lnv, lnvtb), (rstd, rstdtb) = nxt("ss"), nxt("lnv"), nxt("rstd")
            self.token_rstd(xb[:, j, :], xbtb, junk, junktb, ss, sstb, lnv, lnvtb, rstd, rstdtb)
            (xnb, xnbtb) = xn4[xcnt["n"] % 2]
            xcnt["n"] += 1
            fw.op("pool", lambda e: e.tensor_scalar(xnb[:], xb[:, j, :], rstd[:, 0:1], 1.0, ALU.mult, ALU.mult),
                  reads=[xbtb, rstdtb], writes=[xnbtb])
            xn_stash[(ti, j)] = (xnb, xnbtb)

        def norm_b(ti, j):
            xT, xTtb = xnT[ti % 2], xnTtb[ti % 2]
            (xnb, xnbtb) = xn_stash.pop((ti, j))
            pi = pnext("tr")
            ptr = ps[pi][:].bitcast(BF16)

            def do_tr(e):
                last = None
                for k in range(8):
                    last = e.transpose(ptr[:, k * 128:(k + 1) * 128], xnb[:, k * 128:(k + 1) * 128], self.ident_b[:])
                return last
            fw.op("pe", do_tr, reads=[xnbtb, self.ident_b_tb], writes=[pstb[pi]])
            fw.op("dve", lambda e: e.tensor_copy(xT[:, :, j * 128:(j + 1) * 128], ptr.rearrange("p (k t) -> p k t", k=8)),
                  reads=[pstb[pi]], accw=[xTtb])

        def do_tile(ti):
            t = tiles[ti]
            TT, W, Wtb, chunks, vparts, t0 = t["TT"], t["W"], t["Wtb"], t["chunks"], t["vparts"], t["t0"]
            nsub = TT // 128
            xT, xTtb = xnT[ti % 2], xnTtb[ti % 2]
            sched = {}
            if ti + 1 < NT:
                nsn = tiles[ti + 1]["TT"] // 128
                nchk = len(chunks)
                if nchk >= 12:
                    for j in range(nsn):
                        sched.setdefault(1 + 3 * j, []).append(("a", j))
                        sched.setdefault(3 + 3 * j, []).append(("b", j))
                else:
                    for j in range(nsn):
                        sched.setdefault(nchk, []).append(("a", j))
                        sched.setdefault(nchk, []).append(("b", j))
            stash = {}

            def st1(ci):
                (wc, gc, (dname, didx)) = chunks[ci]
                zi = pnext("z")

                def do_z(e):
                    last = None
                    for k in range(8):
                        last = e.matmul(ps[zi][:, 0:TT], W[:, k, wc * 128:(wc + 1) * 128], xT[:, k, 0:TT],
                                        start=(k == 0), stop=(k == 7))
                    return last
                fw.op("pe", do_z, reads=[xTtb, Wtb], writes=[pstb[zi]])
                (sq, sqtb) = nxt("sq")
                fw.op("act", lambda e: e.activation(sq[:, 0:TT], ps[zi][:, 0:TT], AF.Square),
                      reads=[pstb[zi]], writes=[sqtb])
                stash[ci] = (zi, sq, sqtb)

            def st2(ci):
                (wc, gc, (dname, didx)) = chunks[ci]
                (zi, sq, sqtb) = stash.pop(ci)
                (lnq, lnqtb), (rsq, rsqtb), (zo, zotb) = nxt("lnq"), nxt("rsq"), nxt("zo")
                hi = pnext("hs")
                fw.op("pe", lambda e: e.matmul(ps[hi][:, 0:TT], self.bones_b[:], sq[:, 0:TT], start=True, stop=True),
                      reads=[sqtb, self.bones_b_tb], writes=[pstb[hi]])
                self.rstd_from_ss(ps[hi][:, 0:TT], rsq[:, 0:TT], lnq[:, 0:TT], 1.0 / 64, [pstb[hi]], [lnqtb], [rsqtb])
                fw.op("dve", lambda e: e.scalar_tensor_tensor(
                    zo[:, 0:TT], ps[zi][:, 0:TT], gq2[:, gc:gc + 1], rsq[:, 0:TT], ALU.mult, ALU.mult),
                    reads=[pstb[zi], rsqtb, gq2tb], writes=[zotb])
                fw.dma("pool", D[dname][didx, :, t0:t0 + TT], zo[:, 0:TT], zotb, reads=[zotb],
                       accw=[self.dtb[dname]])

            nchk = len(chunks)
            for ci in range(nchk + 1):
                if ci < nchk:
                    st1(ci)
                if ci >= 1:
                    st2(ci - 1)
                if ci == 1 and ti + 2 < NT:
                    load_x(ti + 2)
                for (what, j) in sched.get(ci, []):
                    if what == "a":
                        norm_a(ti + 1, j)
                    else:
                        norm_b(ti + 1, j)
            for j in range(nsub):
                (vo, votb) = nxt("vo")
                off = 0
                for (c0, cw, dname) in vparts:
                    vi = pnext("v")

                    def do_v(e, c0=c0, cw=cw, vi=vi, j=j):
                        last = None
                        for k in range(8):
                            last = e.matmul(ps[vi][:, 0:cw], xT[:, k, j * 128:(j + 1) * 128], W[:, k, c0:c0 + cw],
                                            start=(k == 0), stop=(k == 7))
                        return last
                    fw.op("pe", do_v, reads=[xTtb, Wtb], writes=[pstb[vi]])
                    if (j + (1 if off else 0)) % 2 == 0:
                        fw.op("act", lambda e, vi=vi, off=off, cw=cw, vo=vo: e.copy(vo[:, off:off + cw], ps[vi][:, 0:cw]),
                              reads=[pstb[vi]], accw=[votb])
                    else:
                        fw.op("dve", lambda e, vi=vi, off=off, cw=cw, vo=vo: e.tensor_copy(vo[:, off:off + cw], ps[vi][:, 0:cw]),
                              reads=[pstb[vi]], accw=[votb])
                    fw.dma("pool", D[dname][t0 + j * 128:t0 + (j + 1) * 128, :], vo[:, off:off + cw], votb,
                           reads=[votb], accw=[self.dtb[dname]], semkey=(1 if off else 0))
                    off += cw

        load_x(0)
        if NT > 1:
            load_x(1)
        for j in range(tiles[0]["TT"] // 128):
            norm_a(0, j)
            norm_b(0, j)
        for ti in range(NT):
            do_tile(ti)
        self.end_phase(es)

    def setup_t5(self):
        nc, fw = self.nc, self.fw
        D = self.dram
        ges = self.es
        self.strip, self.strip_tb = self.sb(ges, "strip", [128, 4, 1152], F32)
        self.cb, self.cb_tb = self.sb(ges, "cb", [128, 8], F32, dma=True)
        for h in range(4):
            for side, b in ((0, 15), (1, 31)):
                fw.dma("sp", self.cb[:, 2 * h + side:2 * h + side + 1],
                       D["rel_bias"][b:b + 1, h:h + 1].partition_broadcast(128), self.cb_tb,
                       reads=[self.dtb["rel_bias"]], accw=[self.cb_tb], allow_slow_non_contiguous=True)
        es = self.begin_phase()
        g5, g5tb = self.sb(es, "g5", [32, 1280], F32, dma=True)
        g5b, g5btb = self.sb(es, "g5b", [32, 1280], BF16)
        rb, rbtb = self.sb(es, "rb", [32, 4], F32, dma=True)
        rb3, rb3tb = self.sb(es, "rb3", [32, 12], BF16)
        rd, rdtb = self.sb(es, "rd", [32, 8], F32)
        fw.dma("sp", g5[:], D["c_g5"][:, :], g5tb, reads=[self.dtb["c_g5"]], writes=[g5tb])
        fw.dma("sp", rb[:], D["rel_bias"][:, :], rbtb, reads=[self.dtb["rel_bias"]], writes=[rbtb])
        fw.op("dve", lambda e: e.tensor_copy(g5b[:], g5[:]), reads=[g5tb], writes=[g5btb])
        fw.op("dve", lambda e: e.tensor_copy(rb3[:, 0:4], rb[:]), reads=[rbtb], accw=[rb3tb])
        fw.op("dve", lambda e: e.tensor_tensor(rd[:, 0:4], rb[:], rb3[:, 0:4], ALU.subtract),
              reads=[rbtb, rb3tb], accw=[rdtb])
        fw.op("dve", lambda e: e.tensor_copy(rb3[:, 4:8], rd[:, 0:4]), reads=[rdtb], accw=[rb3tb])
        fw.op("dve", lambda e: e.tensor_tensor(rd[:, 4:8], rd[:, 0:4], rb3[:, 4:8], ALU.subtract),
              reads=[rdtb, rb3tb], accw=[rdtb])
        fw.op("dve", lambda e: e.tensor_copy(rb3[:, 8:12], rd[:, 4:8]), reads=[rdtb], accw=[rb3tb])
        ps, pstb = self.ps, self.pstb
        sa = [self.sb(es, "t5a%d" % i, [128, 32, 4], F32) for i in range(2)]
        sbb = [self.sb(es, "t5b%d" % i, [128, 32, 4], F32) for i in range(2)]
        for r in range(36):
            pi = r % 2
            (a_, atb), (b_, btb) = sa[r % 2], sbb[r % 2]

            def do_mm(e, r=r, pi=pi):
                last = None
                for ml in range(32):
                    m = r * 32 + ml
                    last = e.matmul(ps[pi][:, ml * 12:(ml + 1) * 12], g5b[:, 1152 - m:1280 - m], rb3[:, :],
                                    start=True, stop=True)
                return last
            fw.op("pe", do_mm, reads=[g5btb, rb3tb], writes=[pstb[pi]])
            pv = ps[pi][:, 0:384].rearrange("p (m t h) -> p m t h", t=3, h=4)
            fw.op("act", lambda e, a_=a_, pv=pv: e.copy(a_[:], pv[:, :, 0, :]), reads=[pstb[pi]], writes=[atb])
            fw.op("dve", lambda e, a_=a_, b_=b_, pv=pv: e.tensor_tensor(b_[:], pv[:, :, 1, :], a_[:], ALU.add),
                  reads=[pstb[pi], atb], writes=[btb])
            fw.op("dve", lambda e, b_=b_, pv=pv, r=r: e.tensor_tensor(
                self.strip[:, :, r * 32:(r + 1) * 32].rearrange("p h m -> p m h"), pv[:, :, 2, :], b_[:], ALU.add),
                reads=[pstb[pi], btb], accw=[self.strip_tb])
        self.end_phase(es)

    def phase_attn_a(self, l):
        nc, fw = self.nc, self.fw
        es = self.begin_phase()
        D = self.dram
        ps, pstb = self.ps, self.pstb
        lam_init = 0.8 - 0.6 * math.exp(-0.3 * l)
        lv = {}
        for nm in ("lam_q1", "lam_k1", "lam_q2", "lam_k2"):
            t, tb = self.sb(es, nm, [128, 64], F32, dma=True)
            fw.dma("sp", t[:], D[nm][l:l + 1, :].partition_broadcast(128), tb, reads=[self.dtb[nm]], writes=[tb],
                   allow_slow_non_contiguous=True)
            lv[nm] = (t, tb)
        ltmp, ltmptb = self.sb(es, "ltmp", [128, 64], F32)
        lsc, lsctb = self.sb(es, "lsc", [128, 8], F32)
        neg_lam, neg_lam_tb = self.sb(es, "neg_lam", [128, 1], F32)
        for i, (a, b) in enumerate((("lam_q1", "lam_k1"), ("lam_q2", "lam_k2"))):
            fw.op("dve", lambda e, a=a, b=b: e.tensor_tensor(ltmp[:], lv[a][0][:], lv[b][0][:], ALU.mult),
                  reads=[lv[a][1], lv[b][1]], writes=[ltmptb])
            fw.op("dve", lambda e, i=i: e.tensor_reduce(lsc[:, i:i + 1], ltmp[:], AX.X, ALU.add),
                  reads=[ltmptb], accw=[lsctb])
            fw.op("act", lambda e, i=i: e.activation(lsc[:, 2 + i:3 + i], lsc[:, i:i + 1], AF.Exp),
                  reads=[lsctb], accw=[lsctb])
        fw.op("dve", lambda e: e.tensor_tensor(lsc[:, 4:5], lsc[:, 3:4], lsc[:, 2:3], ALU.subtract),
              reads=[lsctb], accw=[lsctb])
        fw.op("dve", lambda e: e.tensor_scalar(neg_lam[:], lsc[:, 4:5], -lam_init, None, ALU.add),
              reads=[lsctb], writes=[neg_lam_tb])

        sel, seltb = self.sb(es, "sel", [128, 2, 128], F32)

        fw.op("pool", lambda e: e.memset(sel[:], 0.0), writes=[seltb])
        fw.op("pool", lambda e: e.memset(sel[0:1, 0, :], 1.0), writes=[seltb])
        fw.op("pool", lambda e: e.memset(sel[64:65, 1, :], 1.0), writes=[seltb])
        qkv = {}
        Smax = max(S for (_, S) in self.groups)
        for nm, shp in (("Q", [128, Smax]), ("K", [128, Smax]), ("V", [128, Smax // 128, 128])):
            qkv[nm] = [self.sb(es, "%s%d" % (nm, i), shp, BF16, dma=True) for i in range(2)]
        P = [self.sb(es, "P%d" % i, [128, 1024], BF16) for i in range(6)]
        T = [self.sb(es, "T%d" % i, [128, 1024], F32) for i in range(2)]
        ep = {nm: [self.sb(es, "%s%d" % (nm, i), [128, 512], F32) for i in range(2)] for nm in ("r1", "o1", "r2", "t2")}
        ost = [self.sb(es, "ost%d" % i, [128, 512], BF16, dma=True) for i in range(2)]
        cnt = {"P": 0, "T": 0, "ep": 0, "ost": 0}

        heads = [(g, S, h) for (g, S) in self.groups for h in range(4)]

        def load_head(idx):
            g, S, h = heads[idx]
            sl = idx % 2
            (q, qtb), (k, ktb), (v, vtb) = qkv["Q"][sl], qkv["K"][sl], qkv["V"][sl]
            nm = "qkA_" + g
            for c0 in range(0, S, 2048):
                fw.dma("sp", q[:, c0:c0 + 2048], D[nm][h, :, c0:c0 + 2048], qtb, reads=[self.dtb[nm]], writes=[qtb] if c0 == 0 else (), accw=() if c0 == 0 else [qtb])
                fw.dma("sp", k[:, c0:c0 + 2048], D[nm][4 + h, :, c0:c0 + 2048], ktb, reads=[self.dtb[nm]], writes=[ktb] if c0 == 0 else (), accw=() if c0 == 0 else [ktb])
            vn = "vA_" + g
            for c0 in range(0, S // 128, 8):
                fw.dma("sp", v[:, c0:c0 + 8, :],
                       D[vn][c0 * 128:(c0 + 8) * 128, h * 128:(h + 1) * 128].rearrange("(c p) e -> p c e", p=128),
                       vtb, reads=[self.dtb[vn]], writes=[vtb] if c0 == 0 else (), accw=() if c0 == 0 else [vtb])

        def do_head(idx, g, S, h):
            sl = idx % 2
            (q, qtb), (k, ktb), (v, vtb) = qkv["Q"][sl], qkv["K"][sl], qkv["V"][sl]
            nq, nk = S // 512, S // 128
            units = [(qc, kc) for qc in range(nq) for kc in range(nk)]
            pend = {}

            def stage_a(u):
                qc, kc = units[u]
                q0, k0 = qc * 512, kc * 128
                slot = u % 2
                ba, bb = 2 * slot, 2 * slot + 1
                pd = self.ps2[slot]

                def do_qk(e):
                    e.matmul(ps[ba][:], k[0:64, k0:k0 + 128], q[0:64, q0:q0 + 512], start=True, stop=True)
                    return e.matmul(ps[bb][:], k[64:128, k0:k0 + 128], q[64:128, q0:q0 + 512], start=True, stop=True)
                fw.op("pe", do_qk, reads=[qtb, ktb], writes=[pstb[ba], pstb[bb]])
                delta = k0 - q0
                (pp, pptb) = P[cnt["P"] % 6]
                cnt["P"] += 1
                if -256 < delta < 640:
                    (tt, tttb) = T[cnt["T"] % 2]
                    cnt["T"] += 1
                    st0 = 512 - delta

                    def do_add(e):
                        e.tensor_tensor(tt[:, 0:512], ps[ba][:], self.strip[:, h, st0:st0 + 512], ALU.add)
                        return e.tensor_tensor(tt[:, 512:1024], ps[bb][:], self.strip[:, h, st0:st0 + 512], ALU.add)
                    fw.op("dve", do_add, reads=[pstb[ba], pstb[bb], self.strip_tb], writes=[tttb])
                    fw.op("act", lambda e: e.activation(pp[:], tt[:], AF.Exp), reads=[tttb], writes=[pptb])
                else:
                    ci = 2 * h + (0 if delta < 0 else 1)
                    fw.op("act", lambda e: e.activation(pp[:], pd[:], AF.Exp, bias=self.cb[:, ci:ci + 1]),
                          reads=[pstb[ba], pstb[bb], self.cb_tb], writes=[pptb])
                pend[u] = (pp, pptb)

            def stage_b(u):
                qc, kc1 = units[u]
                kc0 = kc1 - 1
                (p1, p1tb) = pend.pop(u)
                (p0, p0tb) = pend.pop(u - 1)
                first, lastk = (kc0 == 0), (kc1 == nk - 1)
                kc = kc1

                def do_av(e):
                    e.matmul(ps[4][:], v[:, kc0, :], p0[:, 0:512], start=first, stop=False)
                    e.matmul(ps[5][:], v[:, kc0, :], p0[:, 512:1024], start=first, stop=False)
                    e.matmul(ps[4][:], v[:, kc1, :], p1[:, 0:512], start=False, stop=lastk)
                    e.matmul(ps[5][:], v[:, kc1, :], p1[:, 512:1024], start=False, stop=lastk)
                    e.matmul(ps[6][0:64, :], self.ones_b[:, 0:64], p0[:, 0:512], start=first, stop=False)
                    e.matmul(ps[6][64:128, :], self.ones_b[:, 0:64], p0[:, 512:1024], start=first, stop=False)
                    e.matmul(ps[6][0:64, :], self.ones_b[:, 0:64], p1[:, 0:512], start=False, stop=lastk)
                    return e.matmul(ps[6][64:128, :], self.ones_b[:, 0:64], p1[:, 512:1024], start=False, stop=lastk)
                fw.op("pe", do_av, reads=[vtb, p0tb, p1tb, self.ones_b_tb],
                      writes=[pstb[4], pstb[5], pstb[6]] if first else (),
                      accw=() if first else [pstb[4], pstb[5], pstb[6]])
                if lastk:
                    i = cnt["ep"] % 2
                    cnt["ep"] += 1
                    (cO1, cO1tb), (cL, cLtb), (cO2, cO2tb) = ep["r1"][i], ep["o1"][i], ep["r2"][i]
                    (os_, ostb) = ost[cnt["ost"] % 2]
                    cnt["ost"] += 1
                    fw.op("dve", lambda e: e.tensor_copy(cO1[:], ps[4][:]), reads=[pstb[4]], writes=[cO1tb])
                    fw.op("act", lambda e: e.copy(cL[:], ps[6][:]), reads=[pstb[6]], writes=[cLtb])
                    fw.op("dve", lambda e: e.tensor_copy(cO2[:], ps[5][:]), reads=[pstb[5]], writes=[cO2tb])
                    fw.op("dve", lambda e: e.reciprocal(cL[:], cL[:]), reads=[cLtb], writes=[cLtb])

                    def part2():
                        fw.op("pe", lambda e: e.matmul(ps[7][:], sel[:, 0, :], cL[:], start=True, stop=True),
                              reads=[seltb, cLtb], writes=[pstb[7]])
                        fw.op("dve", lambda e: e.tensor_tensor(cO1[:], cO1[:], ps[7][:], ALU.mult),
                              reads=[cO1tb, pstb[7]], writes=[cO1tb])
                        fw.op("pe", lambda e: e.matmul(ps[7][:], sel[:, 1, :], cL[:], start=True, stop=True),
                              reads=[seltb, cLtb], writes=[pstb[7]])
                        fw.op("dve", lambda e: e.tensor_tensor(cO2[:], cO2[:], ps[7][:], ALU.mult),
                              reads=[cO2tb, pstb[7]], writes=[cO2tb])
                        fw.op("dve", lambda e: e.scalar_tensor_tensor(os_[:], cO2[:], neg_lam[:, 0:1], cO1[:],
                                                                      ALU.mult, ALU.add),
                              reads=[cO2tb, cO1tb, neg_lam_tb], writes=[ostb])
                        fw.dma("pool", D["mixT_" + g][h, :, qc * 512:(qc + 1) * 512], os_[:], ostb, reads=[ostb],
                               accw=[self.dtb["mixT_" + g]])
                    deferred.append([u + 8, part2])

            deferred = []
            N = len(units)
            assert N % 2 == 0 and nk % 2 == 0
            for i in range(0, N + 2, 2):
                if i < N:
                    stage_a(i)
                    stage_a(i + 1)
                if i >= 2:
                    stage_b(i - 1)
                    while deferred and deferred[0][0] <= i - 1:
                        deferred.pop(0)[1]()
            while deferred:
                deferred.pop(0)[1]()

        load_head(0)
        for idx, (g, S, h) in enumerate(heads):
            if idx + 1 < len(heads):
                load_head(idx + 1)
            do_head(idx, g, S, h)
        self.end_phase(es)

    def phase_attn_bc(self, l):
        nc, fw = self.nc, self.fw
        es = self.begin_phase()
        D = self.dram
        ps, pstb, ps2 = self.ps, self.pstb, self.ps2
        sna, snatb = self.sb(es, "sna", [32, 64, 128], F32, dma=True)
        fw.dma("sp", sna[:], D["c_sna"][:, :, :], snatb, reads=[self.dtb["c_sna"]], writes=[snatb])
        rm, rmtb = self.sb(es, "rm", [128, 2, 8, 8], F32, dma=True)
        fw.dma("sp", rm[:], D["c_rm"][:, :].rearrange("p (e j r) -> p e j r", e=2, j=8), rmtb,
               reads=[self.dtb["c_rm"]], writes=[rmtb])
        text, texttb = self.sb(es, "text", [32, 4, 15], F32, dma=True)
        fw.op("dve", lambda e: e.memset(text[:], 1.0), writes=[texttb])
        for ei in range(15):
            fw.dma("sp", text[0:31, :, ei], D["na_bias"][l, :, 14 - ei, :].rearrange("h d -> d h"), texttb,
                   reads=[self.dtb["na_bias"], texttb], accw=[texttb], allow_slow_non_contiguous=True)
        zz, zztb = self.sb(es, "zz", [128, 60, 64], F32)
        for r in range(8):
            pi = r % 2

            def do_mm(e, r=r, pi=pi):
                last = None
                for cl in range(8):
                    c = r * 8 + cl
                    last = e.matmul(ps[pi][:, cl * 60:(cl + 1) * 60], sna[:, c, :],
                                    text[:].rearrange("d h e -> d (h e)"), start=True, stop=True)
                return last
            fw.op("pe", do_mm, reads=[snatb, texttb], writes=[pstb[pi]])
            fw.op("dve", lambda e, r=r, pi=pi: e.tensor_copy(
                zz[:, :, r * 8:(r + 1) * 8], ps[pi][:, 0:480].rearrange("p (c x) -> p x c", c=8)),
                reads=[pstb[pi]], accw=[zztb])
        wall, walltb = self.sb(es, "wall", [128, 4, 22, 64], F32)
        wint, winttb = self.sb(es, "wint", [128, 4, 22, 64], F32)
        fw.op("pool", lambda e: e.memset(wall[:], NEG), writes=[walltb])
        fw.op("pool", lambda e: e.memset(wint[:], NEG), writes=[winttb])
        zv = zz[:].rearrange("p (h e) c -> p h e c", h=4)
        fw.op("dve", lambda e: e.tensor_copy(wall[0:64, :, 3:18, :], zv[0:64, :, :, :]), reads=[zztb], writes=[walltb])
        fw.op("dve", lambda e: e.tensor_copy(wall[64:128, :, 4:19, :], zv[64:128, :, :, :]), reads=[zztb], accw=[walltb])
        fw.op("dve", lambda e: e.tensor_copy(wint[0:64, :, 7:15, :], zv[0:64, :, 4:12, :]), reads=[zztb], writes=[winttb])
        fw.op("dve", lambda e: e.tensor_copy(wint[64:128, :, 8:16, :], zv[64:128, :, 4:12, :]), reads=[zztb], accw=[winttb])

        qb = [self.sb(es, "qb%d" % i, [128, 512], BF16, dma=True) for i in range(3)]
        kb = [self.sb(es, "kb%d" % i, [128, 1024], BF16, dma=True) for i in range(3)]
        vb = [self.sb(es, "vb%d" % i, [128, 8, 128], BF16, dma=True) for i in range(3)]
        kmg = {g: self.sb(es, "km_" + g, [128, 2, MEM], BF16, dma=True) for (g, _) in self.groups}
        vmg = {g: self.sb(es, "vm_" + g, [128, 2, 256], BF16, dma=True) for (g, _) in self.groups}
        P = [self.sb(es, "P%d" % i, [128, 1024], BF16) for i in range(4)]
        T = [self.sb(es, "T%d" % i, [128, 1024], F32) for i in range(3)]
        rr = [self.sb(es, "rr%d" % i, [128, 512], F32) for i in range(2)]
        ost = [self.sb(es, "ost%d" % i, [128, 512], BF16, dma=True) for i in range(2)]
        cnt = {"P": 0, "T": 0, "job": 0, "u": 0}

        pend = {}

        def stage_a(u):
            q, qtb = u["q"], u["qtb"]
            slot = cnt["u"] % 2
            cnt["u"] += 1
            ba, bb = 2 * slot, 2 * slot + 1

            def do_qk(e):
                e.matmul(ps[ba][:], u["kA"], q[0:64, :], start=True, stop=True)
                return e.matmul(ps[bb][:], u["kB"], q[64:128, :], start=True, stop=True)
            fw.op("pe", do_qk, reads=[qtb] + u["ktbs"], writes=[pstb[ba], pstb[bb]])
            (pp, pptb) = P[cnt["P"] % 4]
            cnt["P"] += 1
            if u["bias"] is not None:
                (tt, tttb) = T[cnt["T"] % 3]
                cnt["T"] += 1
                wsel, wtb, j, edge, hp = u["bias"]
                i0 = 14 - 2 * j

                def do_add(e):
                    e.tensor_tensor(tt[:, 0:512], ps[ba][:],
                                    wsel[:, 2 * hp, i0:i0 + 8, :].rearrange("p a c -> p (a c)"), ALU.add)
                    return e.tensor_tensor(tt[:, 512:1024], ps[bb][:],
                                           wsel[:, 2 * hp + 1, i0:i0 + 8, :].rearrange("p a c -> p (a c)"), ALU.add)
                fw.op("dve", do_add, reads=[pstb[ba], pstb[bb], wtb], writes=[tttb])
                if edge is not None:
                    def do_rm(e):
                        last = None
                        for hh in range(2):
                            tv = tt[:, hh * 512:(hh + 1) * 512].rearrange("p (a c) -> p a c", c=64)
                            last = e.tensor_tensor(tv, tv, rm[:, edge, j, :].unsqueeze(2).to_broadcast([128, 8, 64]),
                                                   ALU.add)
                        return last
                    fw.op("dve", do_rm, reads=[tttb, rmtb], writes=[tttb])
                fw.op("act", lambda e: e.activation(pp[:], tt[:], AF.Exp), reads=[tttb], writes=[pptb])
            else:
                fw.op("act", lambda e: e.activation(pp[:], ps2[slot][:], AF.Exp),
                      reads=[pstb[ba], pstb[bb]], writes=[pptb])
            pend[id(u)] = (pp, pptb)

        def stage_b(u):
            (pp, pptb) = pend.pop(id(u))
            jobi, first, lastu = u["job"], u["first"], u["last"]
            ob, lb = 4 + jobi % 2, 6 + jobi % 2

            def do_av(e):
                e.matmul(ps[ob][0:64, :], u["vA"], pp[:, 0:512], start=first, stop=lastu)
                e.matmul(ps[ob][64:128, :], u["vB"], pp[:, 512:1024], start=first, stop=lastu)
                e.matmul(ps[lb][0:64, :], self.ones_b[:, 0:64], pp[:, 0:512], start=first, stop=lastu)
                return e.matmul(ps[lb][64:128, :], self.ones_b[:, 0:64], pp[:, 512:1024], start=first, stop=lastu)
            fw.op("pe", do_av, reads=u["vtbs"] + [pptb, self.ones_b_tb],
                  writes=[pstb[ob], pstb[lb]] if first else (), accw=() if first else [pstb[ob], pstb[lb]])
            if lastu:
                (r_, rtb) = rr[jobi % 2]
                (os_, ostb) = ost[jobi % 2]
                g, q0, dest_idx = u["g"], u["q0"], u["dest"]
                fw.op("act", lambda e: e.activation(r_[:], ps[lb][:], AF.Ln), reads=[pstb[lb]], writes=[rtb])
                fw.op("act", lambda e: e.activation(r_[:], r_[:], AF.Exp, scale=-1.0), reads=[rtb], writes=[rtb])
                fw.op("dve", lambda e: e.tensor_tensor(os_[:], ps[ob][:], r_[:], ALU.mult),
                      reads=[pstb[ob], rtb], writes=[ostb])
                fw.dma("pool", D["mixT_" + g][dest_idx, :, q0:q0 + 512], os_[:], ostb, reads=[ostb],
                       accw=[self.dtb["mixT_" + g]])

        jobs = []
        for (g, S) in self.groups:
            R = S // GRID_W
            for qt in range(S // 512):
                for hp in range(2):
                    jobs.append(("B", g, S, qt, hp))
            for qt in range(S // 512):
                for hp in range(2):
                    jobs.append(("C", g, S, qt, hp))

        def load_job(ji):
            kind, g, S, qt, hp = jobs[ji]
            sl = ji % 3
            q0 = qt * 512
            (q, qtb) = qb[sl]
            (km, kmtb), (vm, vmtb) = kmg[g], vmg[g]
            if kind == "B":
                fw.dma("sp", q[:], D["qkB_" + g][hp, :, q0:q0 + 512], qtb, reads=[self.dtb["qkB_" + g]], writes=[qtb])
                ks = q0 - 256
                lo, hi = max(ks, 0), min(ks + 1024, S)
                (k, ktb), (v, vtb) = kb[sl], vb[sl]
                fw.dma("sp", k[:, lo - ks:hi - ks], D["qkB_" + g][2 + hp, :, lo:hi], ktb,
                       reads=[self.dtb["qkB_" + g]], writes=[ktb])
                fw.dma("sp", v[:, (lo - ks) // 128:(hi - ks) // 128, :],
                       D["vB_" + g][lo:hi, hp * 128:(hp + 1) * 128].rearrange("(j p) e -> p j e", p=128), vtb,
                       reads=[self.dtb["vB_" + g]], writes=[vtb])
            else:
                fw.dma("sp", q[:], D["qC_" + g][hp, :, q0:q0 + 512], qtb, reads=[self.dtb["qC_" + g]], writes=[qtb])
                if qt == 0 and hp == 0:
                    fw.dma("sp", km[:], D["kmT_" + g][:, :, :].rearrange("c p m -> p c m"), kmtb,
                           reads=[self.dtb["kmT_" + g]], writes=[kmtb])
                    fw.dma("sp", vm[:], D["vm_" + g][:, :].rearrange("(c p) e -> p c e", p=128), vmtb,
                           reads=[self.dtb["vm_" + g]], writes=[vmtb])

        def job_units(ji):
            kind, g, S, qt, hp = jobs[ji]
            sl = ji % 3
            q0 = qt * 512
            (q, qtb) = qb[sl]
            (km, kmtb), (vm, vmtb) = kmg[g], vmg[g]
            units = []
            if kind == "B":
                (k, ktb), (v, vtb) = kb[sl], vb[sl]
                ks = q0 - 256
                nt = S // 512
                edge = 0 if qt == 0 else (1 if qt == nt - 1 else None)
                wsel, wtb = (wint, winttb) if edge is None else (wall, walltb)
                for j in range(8):
                    if ks + j * 128 < 0 or ks + j * 128 >= S:
                        continue
                    units.append(dict(kA=k[0:64, j * 128:(j + 1) * 128], kB=k[64:128, j * 128:(j + 1) * 128], ktbs=[ktb],
                                      vA=v[:, j, 0:64], vB=v[:, j, 64:128], vtbs=[vtb],
                                      bias=(wsel, wtb, j, edge, hp), dest=4 + hp))
            else:
                for mc in range(2):
                    units.append(dict(kA=km[0:64, hp, mc * 128:(mc + 1) * 128], kB=km[64:128, hp, mc * 128:(mc + 1) * 128],
                                      ktbs=[kmtb], vA=vm[:, mc, (2 * hp) * 64:(2 * hp + 1) * 64],
                                      vB=vm[:, mc, (2 * hp + 1) * 64:(2 * hp + 2) * 64], vtbs=[vmtb], bias=None,
                                      dest=6 + hp))
            for i, u in enumerate(units):
                u.update(job=ji, g=g, q0=q0, q=q, qtb=qtb, first=(i == 0), last=(i == len(units) - 1))
            return units

        load_job(0)
        flat = []
        for ji in range(len(jobs)):
            flat.extend(job_units(ji))
        nu = len(flat)
        for i in range(nu + 2):
            if i < nu:
                u = flat[i]
                if u["first"] and u["job"] + 1 < len(jobs):
                    load_job(u["job"] + 1)
                stage_a(u)
            if i >= 2:
                stage_b(flat[i - 2])
        self.end_phase(es)

    def phase_outproj(self, l):
        nc, fw = self.nc, self.fw
        es = self.begin_phase()
        D = self.dram
        ps, pstb = self.ps, self.pstb
        lam_init = 0.8 - 0.6 * math.exp(-0.3 * l)
        src_pref = "x_" if l == 0 else "x1_"
        stage = [self.sb(es, "wst%d" % i, [128, 1024], F32, dma=True) for i in range(3)]
        wout, wouttb = self.prep_weight(es, "wout", D["w_out"][l], self.dtb["w_out"], 8, D_MODEL, None, None,
                                        [t for t, _ in stage], [tb for _, tb in stage])
        gs0, gs0tb = self.sb(es, "gs0", [128, 1], F32, dma=True)
        gsub, gsubtb = self.sb(es, "gsub", [128, 1], F32)
        fw.dma("sp", gs0[:], D["subln_g"][l, :].rearrange("(p o) -> p o", o=1), gs0tb, reads=[self.dtb["subln_g"]],
               writes=[gs0tb], allow_slow_non_contiguous=True)
        fw.op("dve", lambda e: e.tensor_scalar(gsub[:], gs0[:], 1.0 - lam_init, None, ALU.mult),
              reads=[gs0tb], writes=[gsubtb])
        mx = [self.sb(es, "mx%d" % i, [128, 8, 512], BF16, dma=True) for i in range(3)]
        xt = [self.sb(es, "xt%d" % i, [128, 4, 1024], F32, dma=True) for i in range(3)]
        sq = [self.sb(es, "sq%d" % i, [128, 512], BF16) for i in range(2)]
        lnq = [self.sb(es, "lnq%d" % i, [128, 512], F32) for i in range(2)]
        rsq = [self.sb(es, "rsq%d" % i, [128, 512], F32) for i in range(2)]
        cnt = {"n": 0, "ps": 0}
        tiles = [(g, S, t0) for (g, S) in self.groups for t0 in range(0, S, 512)]

        def load_tile(ti):
            g, S, t0 = tiles[ti]
            (m, mtb), (x, xtb) = mx[ti % 3], xt[ti % 3]
            fw.dma("sp", m[:], D["mixT_" + g][:, :, t0:t0 + 512].rearrange("c p t -> p c t"), mtb,
                   reads=[self.dtb["mixT_" + g]], writes=[mtb])
            fw.dma("sp", x[:], D[src_pref + g][t0:t0 + 512, :].rearrange("(j p) d -> p j d", p=128), xtb,
                   reads=[self.dtb[src_pref + g]], writes=[xtb])

        def norm_tile(ti):
            g, S, t0 = tiles[ti]
            (m, mtb) = mx[ti % 3]
            for c in range(4):
                i = cnt["n"] % 2
                cnt["n"] += 1
                (sq_, sqtb), (ln_, lntb), (rs_, rstb) = sq[i], lnq[i], rsq[i]
                pi = 4 + (cnt["n"] % 2)
                fw.op("act", lambda e, c=c, sq_=sq_: e.activation(sq_[:], m[:, c, :], AF.Square), reads=[mtb], writes=[sqtb])
                fw.op("pe", lambda e, sq_=sq_, pi=pi: e.matmul(ps[pi][:], self.ones_b[:], sq_[:], start=True, stop=True),
                      reads=[sqtb, self.ones_b_tb], writes=[pstb[pi]])
                self.rstd_from_ss(ps[pi][:], rs_[:], ln_[:], 1.0 / 128, [pstb[pi]], [lntb], [rstb])
                fw.op("dve", lambda e, c=c, rs_=rs_: e.scalar_tensor_tensor(m[:, c, :], m[:, c, :], gsub[:, 0:1], rs_[:],
                                                                              ALU.mult, ALU.mult),
                      reads=[mtb, rstb, gsubtb], writes=[mtb])

        def do_tile(ti):
            g, S, t0 = tiles[ti]
            (m, mtb), (x, xtb) = mx[ti % 3], xt[ti % 3]
            for j in range(4):
                for n in range(2):
                    pi = cnt["ps"] % 4
                    cnt["ps"] += 1

                    def do_mm(e, j=j, n=n, pi=pi):
                        last = None
                        for c in range(8):
                            last = e.matmul(ps[pi][:], m[:, c, j * 128:(j + 1) * 128], wout[:, c, n * 512:(n + 1) * 512],
                                            start=(c == 0), stop=(c == 7))
                        return last
                    fw.op("pe", do_mm, reads=[mtb, wouttb], writes=[pstb[pi]])
                    fw.op("dve", lambda e, j=j, n=n, pi=pi: e.tensor_tensor(
                        x[:, j, n * 512:(n + 1) * 512], ps[pi][:], x[:, j, n * 512:(n + 1) * 512], ALU.add),
                        reads=[pstb[pi], xtb], accw=[xtb])
            fw.dma("pool", D["xmid_" + g][t0:t0 + 512, :].rearrange("(j p) d -> p j d", p=128), x[:], xtb,
                   reads=[xtb], accw=[self.dtb["xmid_" + g]])

        load_tile(0)
        if len(tiles) > 1:
            load_tile(1)
        norm_tile(0)
        for ti in range(len(tiles)):
            if ti + 2 < len(tiles):
                load_tile(ti + 2)
            if ti + 1 < len(tiles):
                norm_tile(ti + 1)
            do_tile(ti)
        self.end_phase(es)

    def phase_ffn(self, l, half, last):
        nc, fw = self.nc, self.fw
        es = self.begin_phase()
        D = self.dram
        ps, pstb = self.ps, self.pstb
        HC = NFF // 2
        HW_ = HC * 128
        stage = [self.sb(es, "wst%d" % i, [128, HW_], F32, dma=True) for i in range(2)]
        stl, sttb = [t for t, _ in stage], [tb for _, tb in stage]
        g2, g2tb = self.load_gain_cols(es, "g2", D["norm2_g"][l, :], self.dtb["norm2_g"], 8)
        wupv, wupvtb = self.prep_weight(es, "wupv", D["w_up"][l], self.dtb["w_up"], 8, HW_, g2, g2tb, stl, sttb,
                                        col0=half * HW_)
        wupg, wupgtb = self.prep_weight(es, "wupg", D["w_up"][l], self.dtb["w_up"], 8, HW_, g2, g2tb, stl, sttb,
                                        col0=D_FF + half * HW_)
        wdn, wdntb = self.sb(es, "wdn", [128, HC, D_MODEL], BF16)
        for i in range(HC):
            st, stb = stage[i % 2]
            r0 = half * HW_ + i * 128
            fw.dma("sp", st[:, 0:1024], D["w_down"][l, r0:r0 + 128, :], stb, reads=[self.dtb["w_down"]], writes=[stb])
            eng = ("pool", "dve")[i % 2]
            fw.op(eng, lambda e, st=st, i=i: e.tensor_copy(wdn[:, i, :], st[:, 0:1024]), reads=[stb], accw=[wdntb])
        for part, c0 in ((0, half * HW_), (1, D_FF + half * HW_)):
            (st, stb) = stage[part]
            fw.dma("sp", st[0:3, :], D["conv_w"][l, :, c0:c0 + HW_], stb, reads=[self.dtb["conv_w"]], writes=[stb])
            fw.dma("sp", st[3:4, :], D["conv_b"][l:l + 1, c0:c0 + HW_], stb, reads=[self.dtb["conv_b"]], accw=[stb])
        cwt, cwttb = self.sb(es, "cwt", [128, 2 * HC, 4], F32)

        def do_cwt(e):
            last = None
            for f in range(2 * HC):
                st = stage[f // HC][0]
                fl = f % HC
                last = e.transpose(ps[0][:, f * 4:(f + 1) * 4], st[0:4, fl * 128:(fl + 1) * 128], self.ident_f[0:4, 0:4])
            return last
        fw.op("pe", do_cwt, reads=[sttb[0], sttb[1], self.ident_f_tb], writes=[pstb[0]])
        fw.op("dve", lambda e: e.tensor_copy(cwt[:], ps[0][:, 0:8 * HC].rearrange("p (f w) -> p f w", w=4)),
              reads=[pstb[0]], writes=[cwttb])

        xs = [self.sb(es, "xs%d" % i, [128, 4, 1024], F32, dma=True) for i in range(2)]
        og = [self.sb(es, "og%d" % i, [128, 1024], F32, dma=True) for i in range(4)]
        junk, junktb = self.sb(es, "junk", [128, 1024], BF16)
        xn = [self.sb(es, "xn%d" % i, [128, 1024], BF16) for i in range(2)]
        xnT = [self.sb(es, "xnT%d" % i, [128, 8, 512], BF16) for i in range(2)]
        hT = [self.sb(es, "hT%d" % i, [128, HC, 512], BF16) for i in range(2)]
        for (t, tb) in hT:
            fw.op("pool", lambda e, t=t: e.memset(t[:], 0.0), writes=[tb])
        av = [self.sb(es, "av%d" % i, [128, 512], F32) for i in range(3)]
        ag = [self.sb(es, "ag%d" % i, [128, 512], F32) for i in range(3)]
        sg = [self.sb(es, "sg%d" % i, [128, 512], F32) for i in range(2)]
        sm = {nm: [self.sb(es, "%s%d" % (nm, i), [128, 1], F32) for i in range(4)] for nm in ("ss", "lnv", "rstd")}
        cnt = {"c": 0, "og": 0, "dn": 0, "xn": 0, "tr": 0}
        dest_pref = "y_" if last else "x1_"
        tiles = [(g, S, t0) for (g, S) in self.groups for t0 in range(0, S, FT)]

        def rows(ti):
            g, S, t0 = tiles[ti]
            T = min(FT, S - t0)
            return g, S, t0, T, t0 - 1

        def load_tile(ti):
            g, S, t0, T, r0 = rows(ti)
            (x, xtb) = xs[ti % 2]
            lo, hi = max(r0, 0), min(r0 + 512, S)
            full = (lo == r0 and hi == r0 + 512)
            if full:
                fw.dma("sp", x[:], D["xmid_" + g][r0:r0 + 512, :].rearrange("(j p) d -> p j d", p=128), xtb,
                       reads=[self.dtb["xmid_" + g]], writes=[xtb])
                return
            fw.op("pool", lambda e: e.memset(x[:], 0.0), writes=[xtb])
            for j in range(4):
                a, b = max(r0 + 128 * j, lo), min(r0 + 128 * (j + 1), hi)
                if a >= b:
                    continue
                p0 = a - (r0 + 128 * j)
                fw.dma_rows("sp", lambda p, c, j=j: x[p:p + c, j, :],
                            lambda o, c, a=a: D["xmid_" + g][a + o:a + o + c, :], p0, b - a, xtb, True,
                            reads=[self.dtb["xmid_" + g], xtb])

        xn_stash = {}

        def norm_sub_a(ti, j):
            (x, xtb) = xs[ti % 2]
            (ss, sstb), (lnv, lnvtb), (rstd, rstdtb) = sm["ss"][j], sm["lnv"][j], sm["rstd"][j]
            self.token_rstd(x[:, j, :], xtb, junk, junktb, ss, sstb, lnv, lnvtb, rstd, rstdtb)
            (xnb, xnbtb) = xn[cnt["xn"] % 2]
            cnt["xn"] += 1
            fw.op("pool", lambda e: e.tensor_scalar(xnb[:], x[:, j, :], rstd[:, 0:1], 1.0, ALU.mult, ALU.mult),
                  reads=[xtb, rstdtb], writes=[xnbtb])
            xn_stash[(ti, j)] = (xnb, xnbtb)

        def norm_sub_b(ti, j):
            (xT, xTtb) = xnT[ti % 2]
            (xnb, xnbtb) = xn_stash.pop((ti, j))
            pi = cnt["tr"] % 2
            cnt["tr"] += 1
            ptr = ps[pi][:].bitcast(BF16)

            def do_tr(e):
                last = None
                for k in range(8):
                    last = e.transpose(ptr[:, k * 128:(k + 1) * 128], xnb[:, k * 128:(k + 1) * 128], self.ident_b[:])
                return last
            fw.op("pe", do_tr, reads=[xnbtb, self.ident_b_tb], writes=[pstb[pi]])
            fw.op("act", lambda e: e.copy(xT[:, :, j * 128:(j + 1) * 128], ptr.rearrange("p (k t) -> p k t", k=8)),
                  reads=[pstb[pi]], accw=[xTtb])

        chunk_res = {}

        def up_chunk(ti, i):
            (xT, xTtb) = xnT[ti % 2]
            res = {}
            for kind, W, Wtb, f, bufs in (("v", wupv, wupvtb, i, av), ("g", wupg, wupgtb, HC + i, ag)):
                pi = 2 + (cnt["c"] % 4)
                cnt["c"] += 1
                (a_, atb) = bufs[i % 3]

                def do_up(e, W=W, pi=pi):
                    last = None
                    for k in range(8):
                        last = e.matmul(ps[pi][:], W[:, k, i * 128:(i + 1) * 128], xT[:, k, :],
                                        start=(k == 0), stop=(k == 7))
                    return last
                fw.op("pe", do_up, reads=[xTtb, Wtb], writes=[pstb[pi]])
                fw.op("act", lambda e, a_=a_, pi=pi, f=f: e.activation(
                    a_[:, 1:511], ps[pi][:, 1:511], AF.Identity, bias=cwt[:, f, 3:4], scale=cwt[:, f, 1:2]),
                    reads=[pstb[pi], cwttb], writes=[atb])
                fw.op("dve", lambda e, a_=a_, pi=pi, f=f: e.scalar_tensor_tensor(
                    a_[:, 1:511], ps[pi][:, 0:510], cwt[:, f, 0:1], a_[:, 1:511], ALU.mult, ALU.add),
                    reads=[pstb[pi], cwttb, atb], writes=[atb])
                fw.op("dve", lambda e, a_=a_, pi=pi, f=f: e.scalar_tensor_tensor(
                    a_[:, 1:511], ps[pi][:, 2:512], cwt[:, f, 2:3], a_[:, 1:511], ALU.mult, ALU.add),
                    reads=[pstb[pi], cwttb, atb], writes=[atb])
                res[kind] = (a_, atb)
            chunk_res[(ti, i)] = res

        def finish_chunk(ti, i):
            (h_, htb) = hT[ti % 2]
            res = chunk_res.pop((ti, i))
            (s_, stb_) = sg[i % 2]
            (a_g, agtb), (a_v, avtb) = res["g"], res["v"]
            fw.op("act", lambda e: e.activation(s_[:, 1:511], a_g[:, 1:511], AF.Silu), reads=[agtb], writes=[stb_])
            fw.op("pool", lambda e: e.tensor_tensor(h_[:, i, 1:511], a_v[:, 1:511], s_[:, 1:511], ALU.mult),
                  reads=[avtb, stb_], accw=[htb])

        base_pref = "xmid_" if half == 0 else dest_pref

        def og_prefetch(ti):
            g, S, t0, T, r0 = rows(ti)
            for j in range(4):
                a, b = max(r0 + 128 * j, t0), min(r0 + 128 * (j + 1), t0 + T)
                if a >= b:
                    continue
                p0 = a - (r0 + 128 * j)
                (o_, otb) = og[j]
                fw.dma_rows("sp", lambda p, c, o_=o_: o_[p:p + c, :],
                            lambda o, c, a=a: D[base_pref + g][a + o:a + o + c, :], p0, b - a, otb, True,
                            first_is_write=True, reads=[self.dtb[base_pref + g]])

        def do_tile(ti):
            nxt = ti + 1 < len(tiles)
            for i in range(HC):
                if i == 0 and nxt:
                    load_tile(ti + 1)
                up_chunk(ti, i)
                if i >= 1:
                    finish_chunk(ti, i - 1)
                if i == 1 and ti >= 1:
                    down_tile(ti - 1)
                if i == 5 and ti >= 1:
                    store_tile(ti - 1)
                if i == 6:
                    og_prefetch(ti)
                if nxt:
                    if i in (2, 4, 6, 8):
                        norm_sub_a(ti + 1, (i - 2) // 2)
                    if i in (4, 6, 8, 10):
                        norm_sub_b(ti + 1, (i - 4) // 2)
            finish_chunk(ti, HC - 1)

        def down_tile(ti):
            g, S, t0, T, r0 = rows(ti)
            (h_, htb) = hT[ti % 2]
            for j in range(4):
                a, b = max(r0 + 128 * j, t0), min(r0 + 128 * (j + 1), t0 + T)
                if a >= b:
                    continue
                (o_, otb) = og[j]
                for n in range(2):
                    pi = 6 + (cnt["dn"] % 2)
                    cnt["dn"] += 1

                    def do_dn(e, j=j, n=n, pi=pi):
                        last = None
                        for i in range(HC):
                            last = e.matmul(ps[pi][:], h_[:, i, j * 128:(j + 1) * 128], wdn[:, i, n * 512:(n + 1) * 512],
                                            start=(i == 0), stop=(i == HC - 1))
                        return last
                    fw.op("pe", do_dn, reads=[htb, wdntb], writes=[pstb[pi]])
                    fw.op("dve", lambda e, n=n, pi=pi, o_=o_: e.tensor_tensor(
                        o_[:, n * 512:(n + 1) * 512], ps[pi][:], o_[:, n * 512:(n + 1) * 512], ALU.add),
                        reads=[pstb[pi], otb], accw=[otb])

        def store_tile(ti):
            g, S, t0, T, r0 = rows(ti)
            for j in range(4):
                a, b = max(r0 + 128 * j, t0), min(r0 + 128 * (j + 1), t0 + T)
                if a >= b:
                    continue
                p0 = a - (r0 + 128 * j)
                (o_, otb) = og[j]
                fw.dma_rows("pool", lambda p, c, o_=o_: o_[p:p + c, :],
                            lambda o, c, a=a: D[dest_pref + g][a + o:a + o + c, :], p0, b - a, otb, False,
                            reads=[otb], accw=[self.dtb[dest_pref + g]])

        load_tile(0)
        for j in range(4):
            norm_sub_a(0, j)
            norm_sub_b(0, j)
        for ti in range(len(tiles)):
            do_tile(ti)
        down_tile(len(tiles) - 1)
        store_tile(len(tiles) - 1)
        self.end_phase(es)


_W_NAMES = ("norm1_g", "qn_a", "kn_a", "lam_q1", "lam_k1", "lam_q2", "lam_k2", "subln_g", "rel_bias", "qn_b",
            "kn_b", "na_bias", "mem_g", "w_mem_kv", "qn_c", "kn_c", "w_out", "norm2_g", "w_up", "conv_w",
            "conv_b", "w_down")


def make_in_maps(inputs, n_cores, groups):
    consts = _host_consts()
    perm = _win_perm()
    w_in_p = np.ascontiguousarray(np.asarray(inputs["w_in"], np.float32)[:, :, perm])
    shared = {n: np.ascontiguousarray(np.asarray(inputs[n], np.float32)) for n in _W_NAMES}
    shared["w_in"] = w_in_p
    shared.update(consts)
    srcmap = {"p": ("x_prompt", "mem_prompt"), "s": ("x_sample", "mem_sample")}
    in_maps = []
    for c in range(n_cores):
        m = dict(shared)
        for (g, S) in groups:
            xn, mn = srcmap[g]
            m["x_" + g] = np.ascontiguousarray(np.asarray(inputs[xn][c], np.float32))
            m["mem_" + g] = np.ascontiguousarray(np.asarray(inputs[mn][c], np.float32))
        in_maps.append(m)
    return in_maps


def kernel(**inputs):
    groups = [("p", 8192), ("s", 2048)]
    prog = Prog(groups, DEPTH)
    nc = prog.build()
    in_maps = make_in_maps(inputs, 8, groups)
    res = run_bass_kernel_spmd(nc, in_maps, core_ids=list(range(8)))
    y_p = np.stack([np.asarray(r["y_p"], np.float32) for r in res.results], axis=0)
    y_s = np.stack([np.asarray(r["y_s"], np.float32) for r in res.results], axis=0)
    return (y_p, y_s)
```
